# Optimizing a Trainium2 kernel written in Bass

```python
import math
import jax, jax.numpy as jnp
from jax import lax
import numpy as np

D_MODEL = 2048
BATCH = 4
SEQ = 2048
DEPTH = 4

N_EVEN = (DEPTH + 1) // 2
N_ODD = DEPTH // 2
NORM_EPS = 1e-6
NEG_INF = -1e30
FORCE_SCORE = 1e30
TINY = 1e-30

D_A = D_MODEL
A_DK = 128
A_HEADS = D_A // 128
A_DV = D_A // A_HEADS
A_QK = A_HEADS * A_DK
A_CHUNK = 64

D_B = D_MODEL
B_HEADS = 16
B_DH = D_B // B_HEADS
B_KV = 4
B_HPG = B_HEADS // B_KV
B_KVW = B_KV * B_DH
CMP_LEN = 32
CMP_STRIDE = 16
SEL_LEN = 64
SEL_TOPK = 16
WIN = 512
WIN_QB = 128
SEL_QB = 32

D_RNN = 2560
RG_BLOCKS = 10
RG_BS = D_RNN // RG_BLOCKS
CONV_W = 4
RG_C = 8.0

EVEN_SIZES = (A_QK, A_QK, D_A, D_A, D_B, B_KVW, B_KVW, B_KVW, B_KVW, B_KVW, B_KVW, 3 * B_HEADS, D_B)
EVEN_IN = sum(EVEN_SIZES)
EVEN_SPLIT_AT = tuple(int(v) for v in np.cumsum(EVEN_SIZES)[:-1])
EVEN_MIX = D_A + D_B

kernel_name = "hybrid_hgrn2_nsa_rglru_trunk"


def rms_norm(x, w):
    xf = x.astype(jnp.float32)
    y = xf * lax.rsqrt(jnp.mean(xf * xf, axis=-1, keepdims=True) + NORM_EPS)
    return (y * w.astype(jnp.float32)).astype(x.dtype)


def alibi_slopes(n):
    return jnp.asarray(2.0 ** (-8.0 * np.arange(1, n + 1) / n), dtype=jnp.float32)


def hgrn2(q, fz, v, lb, gain):
    bsz, T, _ = q.shape
    f32 = jnp.float32
    qf = jax.nn.silu(q.astype(f32))
    fz = fz.astype(f32)
    f = lb + (1.0 - lb) * jax.nn.sigmoid(fz)
    logf = jnp.log(jnp.maximum(f, TINY))
    k = (1.0 - lb) * jax.nn.sigmoid(-fz)
    vf = v.astype(f32)
    nc = T // A_CHUNK

    def to_chunks(a, d):
        return a.reshape(bsz, nc, A_CHUNK, A_HEADS, d).transpose(1, 0, 3, 2, 4)

    qc, kc, gc, vc = to_chunks(qf, A_DK), to_chunks(k, A_DK), to_chunks(logf, A_DK), to_chunks(vf, A_DV)
    causal = jnp.tril(jnp.ones((A_CHUNK, A_CHUNK), dtype=bool))[None, None, :, :, None]

    def step(S, inp):
        qt, kt, gt, vt = inp
        b = jnp.cumsum(gt, axis=2)
        o_inter = jnp.einsum('bhtk,bhkv->bhtv', qt * jnp.exp(b), S)
        rel = b[:, :, :, None, :] - b[:, :, None, :, :]
        decay = jnp.exp(jnp.where(causal, rel, NEG_INF))
        att = jnp.einsum('bhtk,bhsk,bhtsk->bhts', qt, kt, decay)
        o = o_inter + jnp.einsum('bhts,bhsv->bhtv', att, vt)
        b_last = b[:, :, -1:, :]
        S = jnp.exp(b_last[:, :, 0, :])[..., None] * S + jnp.einsum('bhsk,bhsv->bhkv', kt * jnp.exp(b_last - b), vt)
        return S, o

    S0 = jnp.zeros((bsz, A_HEADS, A_DK, A_DV), f32)
    _, o = lax.scan(step, S0, (qc, kc, gc, vc))
    o = o.transpose(1, 0, 3, 2, 4).reshape(bsz, T, A_HEADS, A_DV)
    o = o * lax.rsqrt(jnp.mean(o * o, axis=-1, keepdims=True) + NORM_EPS)
    return (o.reshape(bsz, T, D_A) * gain.astype(f32)).astype(q.dtype)


def nsa(q, kc, vc, ks, vs, kw, vw, gate_logits, pe_k, w1_k, w2_k, pe_v, w1_v, w2_v):
    bsz, T, _ = q.shape
    f32 = jnp.float32
    dt = q.dtype
    scale = B_DH ** -0.5
    slopes = alibi_slopes(B_HEADS).reshape(B_KV, B_HPG)

    def heads(a, n):
        return a.reshape(bsz, T, n, B_DH).transpose(0, 2, 1, 3)

    qh = heads(q, B_HEADS).reshape(bsz, B_KV, B_HPG, T, B_DH)
    pos = np.arange(T)

    n_cmp = (T - CMP_LEN) // CMP_STRIDE + 1
    cmp_idx = np.arange(n_cmp)[:, None] * CMP_STRIDE + np.arange(CMP_LEN)[None, :]

    def compress(a, pe, w1, w2):
        blk = heads(a, B_KV)[:, :, cmp_idx] + pe
        hid = jax.nn.silu(blk.reshape(bsz, B_KV, n_cmp, CMP_LEN * B_DH) @ w1)
        return hid @ w2

    Kc = compress(kc, pe_k, w1_k, w2_k)
    Vc = compress(vc, pe_v, w1_v, w2_v)
    dist_c = pos[:, None] - cmp_idx[None, :, -1]
    mask_c = dist_c >= 0
    s = jnp.einsum('bghtd,bgnd->bghtn', qh, Kc).astype(f32) * scale - slopes[..., None, None] * dist_c.astype(np.float32)
    p_cmp = jnp.where(mask_c, jax.nn.softmax(jnp.where(mask_c, s, NEG_INF), axis=-1), 0.0)
    o_cmp = jnp.einsum('bghtn,bgnd->bghtd', p_cmp.astype(dt), Vc)

    n_slc = T // SEL_LEN
    c_start = np.arange(n_cmp) * CMP_STRIDE
    s_start = np.arange(n_slc) * SEL_LEN
    overlap = ((c_start[:, None] <= s_start[None, :] + SEL_LEN - 1) &
               (c_start[:, None] + CMP_LEN - 1 >= s_start[None, :])).astype(np.float32)
    imp = jnp.einsum('bghtn,nj->bgtj', p_cmp, overlap)
    blk = np.arange(n_slc)[None, :]
    cur = (pos // SEL_LEN)[:, None]
    valid = blk <= cur
    forced = (blk == 0) | (blk == cur) | (blk == cur - 1)
    score = jnp.where(forced, FORCE_SCORE, jnp.where(valid, imp, NEG_INF))
    topk = min(SEL_TOPK, n_slc)
    _, idx = lax.top_k(score, topk)

    Ks = heads(ks, B_KV).reshape(bsz, B_KV, n_slc, SEL_LEN, B_DH)
    Vs = heads(vs, B_KV).reshape(bsz, B_KV, n_slc, SEL_LEN, B_DH)
    nqb = T // SEL_QB
    q_blocks = qh.reshape(bsz, B_KV, B_HPG, nqb, SEL_QB, B_DH).transpose(3, 0, 1, 2, 4, 5)
    idx_blocks = idx.reshape(bsz, B_KV, nqb, SEL_QB, topk).transpose(2, 0, 1, 3, 4)
    t_blocks = jnp.arange(T).reshape(nqb, SEL_QB)
    bi = jnp.arange(bsz)[:, None, None, None]
    gi = jnp.arange(B_KV)[None, :, None, None]
    offs = jnp.arange(SEL_LEN)

    def sel_block(args):
        qb, ib, tb = args
        kg = Ks[bi, gi, ib].reshape(bsz, B_KV, SEL_QB, topk * SEL_LEN, B_DH)
        vg = Vs[bi, gi, ib].reshape(bsz, B_KV, SEL_QB, topk * SEL_LEN, B_DH)
        kpos = (ib[..., None] * SEL_LEN + offs).reshape(bsz, B_KV, SEL_QB, topk * SEL_LEN)
        d = (tb[None, None, :, None] - kpos)[:, :, None]
        sb = jnp.einsum('bghqd,bgqsd->bghqs', qb, kg).astype(f32) * scale - slopes[:, :, None, None] * d.astype(f32)
        pb = jax.nn.softmax(jnp.where(d >= 0, sb, NEG_INF), axis=-1).astype(dt)
        return jnp.einsum('bghqs,bgqsd->bghqd', pb, vg)

    o_sel = lax.map(sel_block, (q_blocks, idx_blocks, t_blocks))
    o_sel = o_sel.transpose(1, 2, 3, 0, 4, 5).reshape(bsz, B_KV, B_HPG, T, B_DH)

    n_wb = T // WIN_QB
    span = WIN + WIN_QB
    kpos_w = np.arange(n_wb)[:, None] * WIN_QB - WIN + np.arange(span)[None, :]
    kidx = np.clip(kpos_w, 0, T - 1)
    Kw = heads(kw, B_KV)[:, :, kidx]
    Vw = heads(vw, B_KV)[:, :, kidx]
    qw = qh.reshape(bsz, B_KV, B_HPG, n_wb, WIN_QB, B_DH)
    d_w = pos.reshape(n_wb, WIN_QB)[:, :, None] - kpos_w[:, None, :]
    mask_w = (d_w >= 0) & (d_w < WIN) & (kpos_w[:, None, :] >= 0)
    sw = jnp.einsum('bghnqd,bgnkd->bghnqk', qw, Kw).astype(f32) * scale - slopes[:, :, None, None, None] * d_w.astype(np.float32)
    pw = jax.nn.softmax(jnp.where(mask_w, sw, NEG_INF), axis=-1).astype(dt)
    o_win = jnp.einsum('bghnqk,bgnkd->bghnqd', pw, Vw).reshape(bsz, B_KV, B_HPG, T, B_DH)

    g = jax.nn.sigmoid(gate_logits.astype(f32)).reshape(bsz, T, B_KV, B_HPG, 3).transpose(0, 2, 3, 1, 4)
    o = (g[..., 0:1] * o_cmp.astype(f32) + g[..., 1:2] * o_sel.astype(f32) + g[..., 2:3] * o_win.astype(f32))
    return o.transpose(0, 3, 1, 2, 4).reshape(bsz, T, D_B).astype(dt)


def rglru(xb, conv_w, conv_b, w_a, b_a, w_i, b_i, lam):
    bsz, T, _ = xb.shape
    f32 = jnp.float32
    xp = jnp.pad(xb, ((0, 0), (CONV_W - 1, 0), (0, 0)))
    xc = sum(xp[:, j:j + T] * conv_w[j] for j in range(CONV_W)) + conv_b
    xg = xc.reshape(bsz, T, RG_BLOCKS, RG_BS)
    r = jax.nn.sigmoid(jnp.einsum('btnd,nde->btne', xg, w_a).reshape(bsz, T, D_RNN).astype(f32) + b_a.astype(f32))
    i = jax.nn.sigmoid(jnp.einsum('btnd,nde->btne', xg, w_i).reshape(bsz, T, D_RNN).astype(f32) + b_i.astype(f32))
    log_a = -RG_C * jax.nn.softplus(-lam.astype(f32)) * r
    a = jnp.exp(log_a)
    u = jnp.sqrt(jnp.maximum(-jnp.expm1(2.0 * log_a), 0.0)) * (i * xc.astype(f32))

    def combine(c1, c2):
        a1, b1 = c1
        a2, b2 = c2
        return a1 * a2, a2 * b1 + b2

    _, h = lax.associative_scan(combine, (a, u), axis=1)
    return h.astype(xb.dtype)


def setup_inputs(seed: int = 0) -> dict:
    key = jax.random.key(seed)
    ks = jax.random.split(key, 24)
    nrm = jax.random.normal
    f32 = jnp.float32
    lo, hi = 0.9 ** (1.0 / RG_C), 0.999 ** (1.0 / RG_C)
    u = jax.random.uniform(ks[21], (N_ODD, D_RNN), f32, minval=lo, maxval=hi)
    return {
        'x': nrm(ks[0], (BATCH, SEQ, D_MODEL), f32),
        'norm_w': 1.0 + 0.02 * nrm(ks[1], (DEPTH, D_MODEL), f32),
        'final_norm_w': 1.0 + 0.02 * nrm(ks[2], (D_MODEL,), f32),
        'even_w_in': nrm(ks[3], (N_EVEN, D_MODEL, EVEN_IN), f32) * D_MODEL ** -0.5,
        'even_w_out': nrm(ks[4], (N_EVEN, EVEN_MIX, D_MODEL), f32) * EVEN_MIX ** -0.5,
        'hgrn_lb_logits': 0.5 * nrm(ks[5], (N_EVEN, A_QK), f32),
        'hgrn_norm_w': 1.0 + 0.02 * nrm(ks[6], (N_EVEN, D_A), f32),
        'cmp_pe_k': 0.02 * nrm(ks[7], (N_EVEN, CMP_LEN, B_DH), f32),
        'cmp_w1_k': nrm(ks[8], (N_EVEN, CMP_LEN * B_DH, B_DH), f32) * (CMP_LEN * B_DH) ** -0.5,
        'cmp_w2_k': nrm(ks[9], (N_EVEN, B_DH, B_DH), f32) * B_DH ** -0.5,
        'cmp_pe_v': 0.02 * nrm(ks[10], (N_EVEN, CMP_LEN, B_DH), f32),
        'cmp_w1_v': nrm(ks[11], (N_EVEN, CMP_LEN * B_DH, B_DH), f32) * (CMP_LEN * B_DH) ** -0.5,
        'cmp_w2_v': nrm(ks[12], (N_EVEN, B_DH, B_DH), f32) * B_DH ** -0.5,
        'odd_w_in': nrm(ks[13], (N_ODD, D_MODEL, 2 * D_RNN), f32) * D_MODEL ** -0.5,
        'odd_w_out': nrm(ks[14], (N_ODD, D_RNN, D_MODEL), f32) * D_RNN ** -0.5,
        'rg_conv_w': nrm(ks[15], (N_ODD, CONV_W, D_RNN), f32) * CONV_W ** -0.5,
        'rg_conv_b': 0.01 * nrm(ks[16], (N_ODD, D_RNN), f32),
        'rg_w_a': nrm(ks[17], (N_ODD, RG_BLOCKS, RG_BS, RG_BS), f32) * RG_BS ** -0.5,
        'rg_b_a': 0.01 * nrm(ks[18], (N_ODD, D_RNN), f32),
        'rg_w_i': nrm(ks[19], (N_ODD, RG_BLOCKS, RG_BS, RG_BS), f32) * RG_BS ** -0.5,
        'rg_b_i': 0.01 * nrm(ks[20], (N_ODD, D_RNN), f32),
        'rg_lambda': jnp.log(u) - jnp.log1p(-u),
    }


def reference(x, norm_w, final_norm_w, even_w_in, even_w_out, hgrn_lb_logits, hgrn_norm_w,
              cmp_pe_k, cmp_w1_k, cmp_w2_k, cmp_pe_v, cmp_w1_v, cmp_w2_v,
              odd_w_in, odd_w_out, rg_conv_w, rg_conv_b, rg_w_a, rg_b_a, rg_w_i, rg_b_i, rg_lambda):
    lb_sm = jax.nn.softmax(hgrn_lb_logits.astype(jnp.float32), axis=0)
    lb_all = jnp.cumsum(lb_sm, axis=0) - lb_sm[0]
    for layer in range(DEPTH):
        h = rms_norm(x, norm_w[layer])
        if layer % 2 == 0:
            e = layer // 2
            (a_q, a_f, a_i, a_g, b_q, b_kc, b_vc, b_ks, b_vs, b_kw, b_vw, b_gate, b_g) = jnp.split(
                h @ even_w_in[e], EVEN_SPLIT_AT, axis=-1)
            ya = hgrn2(a_q, a_f, a_i, lb_all[e], hgrn_norm_w[e]) * jax.nn.silu(a_g)
            yb = nsa(b_q, b_kc, b_vc, b_ks, b_vs, b_kw, b_vw, b_gate,
                     cmp_pe_k[e], cmp_w1_k[e], cmp_w2_k[e], cmp_pe_v[e], cmp_w1_v[e], cmp_w2_v[e]) * jax.nn.silu(b_g)
            y = jnp.concatenate([ya, yb], axis=-1) @ even_w_out[e]
        else:
            o = layer // 2
            xb, g = jnp.split(h @ odd_w_in[o], [D_RNN], axis=-1)
            hr = rglru(xb, rg_conv_w[o], rg_conv_b[o], rg_w_a[o], rg_b_a[o], rg_w_i[o], rg_b_i[o], rg_lambda[o])
            y = (hr * jax.nn.silu(g)) @ odd_w_out[o]
        x = x + y.astype(x.dtype)
    return rms_norm(x, final_norm_w)
```

```python
import numpy as np
from contextlib import ExitStack
import concourse.bass as bass
import concourse.mybir as mybir
from concourse.alu_op_type import AluOpType as ALU
from concourse.bass_utils import run_bass_kernel_spmd

AF = mybir.ActivationFunctionType
F32 = mybir.dt.float32
BF16 = mybir.dt.bfloat16

D = 2048
T = 2048
NT = T // 128
KC = D // 128
D_RNN = 2560
EPS = 1e-6
NCORES = 8


class Buf:
    def __init__(self, t, name=""):
        self.t = t
        self.name = name
        self.w = None
        self.r = []
        self.dsem = None
        self.dcnt = 0
        self.psum = False

    def __getitem__(self, idx):
        return self.t[idx]


ENGS = ('pe', 'act', 'dve', 'pool', 'sp')


class Prog:
    def __init__(self, nc, stack):
        self.nc = nc
        self.stack = stack
        self.ops = {k: [] for k in ENGS}
        self.esem = {k: stack.enter_context(nc.semaphore("es_" + k)) for k in ENGS}
        self.ecnt = {k: 0 for k in ENGS}
        self.waited = {k: {} for k in ENGS}
        self.nsem = 0
        self.nops = 0
        self.dtoks = {}
        self.keep = []
        self.uid = 0
        self.sem_pool = {'sw': [], 'hw': []}
        self.sec_bufs = []

    def sb(self, stack, name, shape, dtype):
        self.uid += 1
        return Buf(stack.enter_context(self.nc.sbuf_tensor("s%d_%s" % (self.uid, name), list(shape), dtype)), name)

    def ps(self, stack, name, shape, dtype=F32):
        self.uid += 1
        b = Buf(stack.enter_context(self.nc.psum_tensor("p%d_%s" % (self.uid, name), list(shape), dtype)), name)
        b.psum = True
        return b

    def dram(self, name, shape, dtype, kind):
        return Buf(self.nc.dram_tensor(name, list(shape), dtype, kind=kind), name)

    def view(self, b, name=""):
        return Buf(b.t, name or b.name)

    def _dsem(self, b, eng):
        kind = 'sw' if eng == 'pool' else 'hw'
        if b.dsem is None:
            b.dsem = {}
            b.dcnt = {}
        if kind not in b.dsem:
            if self.sem_pool[kind]:
                b.dsem[kind], b.dcnt[kind] = self.sem_pool[kind].pop()
            else:
                sem = self.stack.enter_context(self.nc.semaphore("ds%s_%d" % (kind, self.nsem)))
                self.keep.append(sem)
                self.nsem += 1
                b.dsem[kind], b.dcnt[kind] = sem, 0
            self.sec_bufs.append((b, kind))
        return kind

    def end_section(self):
        self.barrier()
        for b, kind in self.sec_bufs:
            self.sem_pool[kind].append((b.dsem.pop(kind), b.dcnt.pop(kind)))
        self.sec_bufs = []

    def _deps(self, eng, reads, writes, waw=True):
        need = {}
        own = self.esem[eng]

        def add(d, raw):
            if d is None:
                return
            s, v = d
            if s is own and (eng == 'pe' or raw == 'war'):
                return
            k = id(s)
            if k not in need or need[k][1] < v:
                need[k] = (s, v)

        for b in reads:
            add(b.w, 'raw')
            if b.psum:
                for d in b.r:
                    if d[0] is not own:
                        add(d, 'rar')
        for b in writes:
            if waw:
                add(b.w, 'waw')
            for d in b.r:
                add(d, 'war')
        out = []
        wd = self.waited[eng]
        for k, (s, v) in need.items():
            if wd.get(k, 0) >= v:
                continue
            wd[k] = v
            out.append((s, v))
        return out

    def _commit(self, reads, writes, tok):
        for b in reads:
            b.r.append(tok)
            if len(b.r) > 64:
                b.r = b.r[-64:] if False else b.r
        for b in writes:
            b.w = tok
            b.r = []

    def op(self, eng, fn, reads=(), writes=(), waw=True):
        waits = self._deps(eng, reads, writes, waw)
        self.ecnt[eng] += 1
        tok = (self.esem[eng], self.ecnt[eng])
        self.ops[eng].append((waits, fn, self.esem[eng], 1))
        self._commit(reads, writes, tok)
        self.nops += 1

    def dma(self, eng, out_ap, in_ap, reads=(), writes=(), owner=None):
        waits = self._deps(eng, reads, writes)
        kind = self._dsem(owner, eng)
        sem = owner.dsem[kind]
        owner.dcnt[kind] += 16
        tok = (sem, owner.dcnt[kind])
        self.dtoks[id(sem)] = tok

        def fn(e):
            return e.dma_start(out=out_ap, in_=in_ap)
        self.ops[eng].append((waits, fn, sem, 16))
        self._commit(reads, writes, tok)
        self.nops += 1

    def barrier(self):
        toks = [(self.esem[k], self.ecnt[k]) for k in ENGS if self.ecnt[k] > 0]
        toks += list(self.dtoks.values())
        for k in ENGS:
            waits = []
            wd = self.waited[k]
            for (s, v) in toks:
                if s is self.esem[k]:
                    continue
                if wd.get(id(s), 0) >= v:
                    continue
                wd[id(s)] = v
                waits.append((s, v))
            if waits:
                self.ops[k].append((waits, None, None, 0))

    def wait_all(self, eng, bufs):
        waits = self._deps(eng, bufs, bufs)
        self.ops[eng].append((waits, None, None, 0))

    def mm(self, out, lhsT, rhs, start, stop, reads, writes):
        self.op('pe', lambda e: e.matmul(out, lhsT=lhsT, rhs=rhs, start=start, stop=stop),
                reads, writes)

    def tr(self, out, in_, ident, reads, writes):
        self.op('pe', lambda e: e.transpose(out=out, in_=in_, identity=ident), reads, writes)

    def act(self, out, in_, func, reads, writes, waw=True, **kw):
        self.op('act', lambda e: e.activation(out=out, in_=in_, func=func, **kw), reads, writes, waw)

    def tt(self, eng, out, in0, in1, op, reads, writes, waw=True):
        self.op(eng, lambda e: e.tensor_tensor(out=out, in0=in0, in1=in1, op=op), reads, writes, waw)

    def ts(self, eng, out, in0, s1, s2, op0, op1, reads, writes, waw=True):
        if op1 is None:
            self.op(eng, lambda e: e.tensor_scalar(out=out, in0=in0, scalar1=s1, scalar2=None, op0=op0),
                    reads, writes, waw)
        else:
            self.op(eng, lambda e: e.tensor_scalar(out=out, in0=in0, scalar1=s1, scalar2=s2, op0=op0, op1=op1),
                    reads, writes, waw)

    def stt(self, out, in0, scalar, in1, op0, op1, reads, writes, waw=True):
        self.op('dve', lambda e: e.scalar_tensor_tensor(out=out, in0=in0, scalar=scalar, in1=in1,
                                                         op0=op0, op1=op1), reads, writes, waw)

    def cp(self, eng, out, in_, reads, writes, waw=True):
        if eng == 'act':
            self.op('act', lambda e: e.copy(out=out, in_=in_), reads, writes, waw)
        else:
            self.op(eng, lambda e: e.tensor_copy(out=out, in_=in_), reads, writes, waw)

    def memset(self, eng, ap, val, writes):
        self.op(eng, lambda e: e.memset(ap, val), (), writes)

    def scan(self, out, d0, d1, init, reads, writes):
        self.op('dve', lambda e: e.tensor_tensor_scan(out=out, data0=d0, data1=d1, initial=init,
                                                       op0=ALU.mult, op1=ALU.add), reads, writes)

    def emit(self):
        nc = self.nc
        ops = self.ops

        def replay(k, e):
            for waits, fn, sem, inc in ops[k]:
                for s, v in waits:
                    e.wait_ge(s, v)
                if fn is not None:
                    fn(e).then_inc(sem, inc)

        with nc.Block() as block:
            @block.tensor
            def _(e):
                replay('pe', e)

            @block.scalar
            def _(e):
                replay('act', e)

            @block.vector
            def _(e):
                replay('dve', e)

            @block.gpsimd
            def _(e):
                replay('pool', e)

            @block.sync
            def _(e):
                replay('sp', e)


def make_ident(P, st):
    idf = P.sb(st, "idf", [128, 128], F32)
    ident = P.sb(st, "ident", [128, 128], BF16)
    P.op('pool', lambda e: e.iota(idf[:], pattern=[[1, 128]], base=0, channel_multiplier=-1,
                                  allow_small_or_imprecise_dtypes=True), (), [idf])
    P.ts('dve', ident[:], idf[:], 0.0, None, ALU.is_equal, None, [idf], [ident])
    return ident


def emit_hT(P, st, x_d, nwb, ident, hT, hTv, xt, xn, ptr, small):
    eps, ss = small['eps'], small['ss']
    for tt in range(NT):
        xb = xt[tt % 2]
        xnb = xn[tt % 2]
        P.dma('sp', xb[:, 0:D], x_d.t[tt * 128:(tt + 1) * 128, :], (), [xb], owner=xb)
        s0 = ss[:, 2 * (tt % 2):2 * (tt % 2) + 1]
        s1 = ss[:, 2 * (tt % 2) + 1:2 * (tt % 2) + 2]
        P.act(xnb[:, 0:D], xb[:, 0:D], AF.Square, [xb], [xnb, ss], accum_out=s0)
        P.act(s1, s0, AF.Sqrt, [ss, eps], [ss], scale=1.0 / D, bias=eps[:])
        P.op('dve', lambda e, s1=s1: e.reciprocal(out=s1, in_=s1), [ss], [ss])
        P.stt(xnb[:, 0:D], xb[:, 0:D], s1, nwb[:, 0:D], ALU.mult, ALU.mult, [xb, ss, nwb], [xnb])
        for q in range(4):
            pt = ptr[q % 2]
            for j in range(4):
                kc = q * 4 + j
                P.tr(pt[:, j, :], xnb[:, kc * 128:(kc + 1) * 128], ident[:], [xnb, ident], [pt])
            eng = 'act' if q % 2 == 0 else 'dve'
            P.cp(eng, hT[:, q * 4:(q + 1) * 4, tt * 128:(tt + 1) * 128], pt[:, :, :], [pt], [hTv[tt]])


def build_odd_A():
    nc = bass.Bass("TRN2", target_bir_lowering=False)
    NCT = 10
    with ExitStack() as st:
        P = Prog(nc, st)
        x_d = P.dram("x", [T, D], F32, "ExternalInput")
        nw_d = P.dram("nw", [D], F32, "ExternalInput")
        wx_d = P.dram("wx", [D, 1280], F32, "ExternalInput")
        wg_d = P.dram("wg", [D, 1280], F32, "ExternalInput")
        wa_d = P.dram("wa", [5, 256, 256], F32, "ExternalInput")
        wi_d = P.dram("wi", [5, 256, 256], F32, "ExternalInput")
        vec_d = P.dram("vec", [128, NCT * 8], F32, "ExternalInput")
        out_d = P.dram("mixT", [1280, T], BF16, "ExternalOutput")

        ident = make_ident(P, st)
        hT = P.sb(st, "hT", [128, KC, T], BF16)
        hTv = [P.view(hT, "hT%d" % i) for i in range(NT)]
        Tb = [P.sb(st, "T%d" % i, [128, T], F32) for i in range(4)]
        Bb = [P.sb(st, "B%d" % i, [128, T], BF16) for i in range(2)]
        xraw = [P.sb(st, "xraw%d" % i, [128, T + 3], F32) for i in range(2)]
        xc = [P.sb(st, "xc%d" % i, [128, T], F32) for i in range(2)]
        mix = [P.sb(st, "mix%d" % i, [128, T], BF16) for i in range(2)]
        wxs = [P.sb(st, "wxs%d" % i, [128, KC, 256], BF16) for i in range(2)]
        wgs = [P.sb(st, "wgs%d" % i, [128, KC, 256], BF16) for i in range(2)]
        was = P.sb(st, "was", [128, 5, 2, 256], BF16)
        wis = P.sb(st, "wis", [128, 5, 2, 256], BF16)
        vec = P.sb(st, "vec", [128, NCT, 8], F32)
        cl = P.sb(st, "cl", [128, NCT], F32)
        eps = P.sb(st, "eps", [128, 1], F32)
        ss = P.sb(st, "ss", [128, 4], F32)
        ptr = [P.ps(st, "ptr%d" % i, [128, 4, 128], BF16) for i in range(2)]
        pj = [P.ps(st, "pj%d" % i, [128, 512], F32) for i in range(2)]
        pg = [P.ps(st, "pg%d" % i, [128, 512], F32) for i in range(2)]

        P.memset('dve', eps[:], EPS, [eps])
        nwb = xraw[0]
        P.dma('sp', nwb[:, 0:D], nw_d.t.ap().partition_broadcast(128), (), [nwb], owner=nwb)
        P.dma('sp', vec[:], vec_d.t.ap().rearrange("p (c k) -> p c k", k=8), (), [vec], owner=vec)
        emit_hT(P, st, x_d, nwb, ident, hT, hTv, Tb[0:2], Bb, ptr, {'eps': eps, 'ss': ss})
        for n in range(5):
            P.dma('pool', was[:, n, :, :], wa_d.t[n].rearrange("(dt p) e -> p dt e", p=128), (), [was], owner=was)
            P.dma('pool', wis[:, n, :, :], wi_d.t[n].rearrange("(dt p) e -> p dt e", p=128), (), [wis], owner=wis)
        P.act(cl[:], vec[:, :, 7], AF.Exp, [vec], [cl], scale=-1.0)
        P.act(cl[:], cl[:], AF.Ln, [cl], [cl], bias=1.0)
        P.ts('dve', cl[:], cl[:], -8.0, None, ALU.mult, None, [cl], [cl])
        for i in range(2):
            P.memset('dve', xraw[i][:, 0:3], 0.0, [xraw[i]])

        def load_w(n):
            w_x, w_g = wxs[n % 2], wgs[n % 2]
            P.dma('pool', w_x[:], wx_d.t[:, n * 256:(n + 1) * 256].rearrange("(kc p) c -> p kc c", p=128),
                  (), [w_x], owner=w_x)
            P.dma('pool', w_g[:], wg_d.t[:, n * 256:(n + 1) * 256].rearrange("(kc p) c -> p kc c", p=128),
                  (), [w_g], owner=w_g)

        npj = 0
        load_w(0)
        for n in range(5):
            w_x, w_g = wxs[n % 2], wgs[n % 2]
            if n + 1 < 5:
                load_w(n + 1)
            for hf in range(2):
                ct = 2 * n + hf
                xr = xraw[hf]
                for tb in range(4):
                    pp = pj[npj % 2]
                    npj += 1
                    for kc in range(KC):
                        P.mm(pp[:], w_x[:, kc, hf * 128:(hf + 1) * 128], hT[:, kc, tb * 512:(tb + 1) * 512],
                             kc == 0, kc == KC - 1, [w_x] + hTv[4 * tb:4 * tb + 4], [pp])
                    P.cp('act', xr[:, 3 + tb * 512:3 + (tb + 1) * 512], pp[:], [pp], [xr])
                c = xc[hf]
                P.ts('dve', c[:], xr[:, 3:3 + T], vec[:, ct, 3:4], vec[:, ct, 4:5], ALU.mult, ALU.add,
                     [xr, vec], [c])
                for j in range(3):
                    P.stt(c[:], xr[:, j:j + T], vec[:, ct, j:j + 1], c[:], ALU.mult, ALU.add, [xr, vec, c], [c])
                P.cp('pool', Bb[hf][:], c[:], [c], [Bb[hf]])
            for eh in range(2):
                ct = 2 * n + eh
                R, I, S, G = Tb
                for (dst, wsb, bcol) in ((R, was, 5), (I, wis, 6)):
                    for tb in range(4):
                        pp = pg[npj % 2]
                        npj += 1
                        for dt_ in range(2):
                            P.mm(pp[:], wsb[:, n, dt_, eh * 128:(eh + 1) * 128],
                                 Bb[dt_][:, tb * 512:(tb + 1) * 512], dt_ == 0, dt_ == 1, [wsb, Bb[dt_]], [pp])
                        P.act(dst[:, tb * 512:(tb + 1) * 512], pp[:], AF.Sigmoid, [pp, vec], [dst],
                              bias=vec[:, ct, bcol:bcol + 1])
                P.act(R[:], R[:], AF.Exp, [R, cl], [R], scale=cl[:, ct:ct + 1])
                P.tt('pool', S[:], R[:], R[:], ALU.mult, [R], [S])
                P.act(S[:], S[:], AF.Sqrt, [S], [S], scale=-1.0, bias=1.0)
                P.tt('dve', I[:], I[:], xc[eh][:], ALU.mult, [I, xc[eh]], [I])
                P.tt('dve', I[:], I[:], S[:], ALU.mult, [I, S], [I])
                P.scan(S[:], R[:], I[:], 0.0, [R, I], [S])
                for tb in range(4):
                    pp = pj[npj % 2]
                    npj += 1
                    for kc in range(KC):
                        P.mm(pp[:], w_g[:, kc, eh * 128:(eh + 1) * 128], hT[:, kc, tb * 512:(tb + 1) * 512],
                             kc == 0, kc == KC - 1, [w_g] + hTv[4 * tb:4 * tb + 4], [pp])
                    P.act(G[:, tb * 512:(tb + 1) * 512], pp[:], AF.Silu, [pp], [G])
                m = mix[eh]
                P.tt('dve', m[:], S[:], G[:], ALU.mult, [S, G], [m])
                P.dma('sp', out_d.t[ct * 128:(ct + 1) * 128, :], m[:], [m], [out_d], owner=m)
        P.wait_all('sp', mix)
        P.emit()
    return nc


def build_B(C, final):
    nc = bass.Bass("TRN2", target_bir_lowering=False)
    TB = 1024
    NTB = TB // 128
    KCC = C // 128
    NB = 256
    with ExitStack() as st:
        P = Prog(nc, st)
        mix_d = P.dram("mixT", [C, TB], BF16, "ExternalInput")
        x_d = P.dram("x", [TB, D], F32, "ExternalInput")
        w_d = P.dram("w", [C, D], F32, "ExternalInput")
        out_d = P.dram("xo", [TB, D], F32, "ExternalOutput")
        if final:
            fn_d = P.dram("fnw", [D], F32, "ExternalInput")
        mix = P.sb(st, "mix", [128, KCC, TB], BF16)
        xs = [P.sb(st, "xs%d" % i, [128, D], F32) for i in range(NTB)]
        ws = [P.sb(st, "ws%d" % i, [128, KCC, NB], BF16) for i in range(2)]
        pp = [P.ps(st, "pp%d" % i, [128, 512], F32) for i in range(4)]
        P.dma('sp', mix[:], mix_d.t.ap().rearrange("(kc p) t -> p kc t", p=128), (), [mix], owner=mix)
        for tt in range(NTB):
            P.dma('sp', xs[tt][:], x_d.t[tt * 128:(tt + 1) * 128, :], (), [xs[tt]], owner=xs[tt])
        if final:
            fnb = P.sb(st, "fnb", [128, D], F32)
            eps = P.sb(st, "eps", [128, 1], F32)
            ss = P.sb(st, "ss", [128, 2 * NTB], F32)
            junk = P.sb(st, "junk", [128, D], BF16)
            P.memset('dve', eps[:], EPS, [eps])
            P.dma('sp', fnb[:], fn_d.t.ap().partition_broadcast(128), (), [fnb], owner=fnb)
        k = 0
        for nb in range(D // NB):
            w = ws[nb % 2]
            P.dma('pool', w[:], w_d.t[:, nb * NB:(nb + 1) * NB].rearrange("(kc p) c -> p kc c", p=128),
                  (), [w], owner=w)
            for tt in range(NTB):
                p_ = pp[k % 4]
                k += 1
                for kc in range(KCC):
                    P.mm(p_[:, 0:NB], mix[:, kc, tt * 128:(tt + 1) * 128], w[:, kc, :], kc == 0, kc == KCC - 1,
                         [mix, w], [p_])
                xv = xs[tt]
                P.tt('dve', xv[:, nb * NB:(nb + 1) * NB], xv[:, nb * NB:(nb + 1) * NB], p_[:, 0:NB], ALU.add,
                     [xv, p_], [xv])
        for tt in range(NTB):
            xv = xs[tt]
            if final:
                s0 = ss[:, 2 * tt:2 * tt + 1]
                s1 = ss[:, 2 * tt + 1:2 * tt + 2]
                P.act(junk[:], xv[:], AF.Square, [xv], [junk, ss], accum_out=s0)
                P.act(s1, s0, AF.Sqrt, [ss, eps], [ss], scale=1.0 / D, bias=eps[:])
                P.op('dve', lambda e, s1=s1: e.reciprocal(out=s1, in_=s1), [ss], [ss])
                P.stt(xv[:], xv[:], s1, fnb[:], ALU.mult, ALU.mult, [xv, ss, fnb], [xv])
            P.dma('sp', out_d.t[tt * 128:(tt + 1) * 128, :], xv[:], [xv], [out_d], owner=xv)
        P.wait_all('sp', xs)
        P.emit()
    return nc


_CACHE = {}


def _get(name, fn):
    if name not in _CACHE:
        _CACHE[name] = fn()
    return _CACHE[name]


def _run(nc, in_maps):
    res = run_bass_kernel_spmd(nc, in_maps, core_ids=list(range(NCORES)))
    return res.results


def c_(a):
    return np.ascontiguousarray(a)


def run_odd_A(x, nw, w_in, conv_w, conv_b, w_a, b_a, w_i, b_i, lam):
    nc = _get("oddA", build_odd_A)
    in_maps = []
    for c in range(NCORES):
        b, hh = c // 2, c % 2
        lo = 1280 * hh
        vec = np.stack([conv_w[0, lo:lo + 1280], conv_w[1, lo:lo + 1280], conv_w[2, lo:lo + 1280],
                        conv_w[3, lo:lo + 1280], conv_b[lo:lo + 1280], b_a[lo:lo + 1280],
                        b_i[lo:lo + 1280], lam[lo:lo + 1280]], axis=-1)
        vec = vec.reshape(10, 128, 8).transpose(1, 0, 2).reshape(128, 80)
        in_maps.append({
            "x": c_(x[b]), "nw": c_(nw),
            "wx": c_(w_in[:, lo:lo + 1280]), "wg": c_(w_in[:, D_RNN + lo:D_RNN + lo + 1280]),
            "wa": c_(w_a[5 * hh:5 * hh + 5]), "wi": c_(w_i[5 * hh:5 * hh + 5]),
            "vec": c_(vec.astype(np.float32)),
        })
    res = _run(nc, in_maps)
    out = [np.concatenate([res[2 * b]["mixT"], res[2 * b + 1]["mixT"]], axis=0) for b in range(4)]
    return out


def run_B(mixT, x, w_out, final_w=None):
    C = w_out.shape[0]
    final = final_w is not None
    nc = _get("B%d%d" % (C, final), lambda: build_B(C, final))
    in_maps = []
    for c in range(NCORES):
        b, th = c // 2, c % 2
        m = {"mixT": c_(mixT[b][:, 1024 * th:1024 * (th + 1)]), "x": c_(x[b, 1024 * th:1024 * (th + 1)]),
             "w": c_(w_out)}
        if final:
            m["fnw"] = c_(final_w)
        in_maps.append(m)
    res = _run(nc, in_maps)
    xo = np.stack([np.concatenate([res[2 * b]["xo"], res[2 * b + 1]["xo"]], axis=0) for b in range(4)], axis=0)
    return xo


BIG = 30000.0
DBG_STAGE = 9
DBG_VAR = 0
SCALE = 128 ** -0.5


def _bf16_split(a):
    import ml_dtypes
    a = np.asarray(a, np.float32)
    hi = a.astype(ml_dtypes.bfloat16).astype(np.float32)
    lo = (a - hi).astype(ml_dtypes.bfloat16).astype(np.float32)
    return hi, lo


def even_const_tables():
    tb = {}
    s_l = np.arange(128, dtype=np.float32)
    La = np.zeros((128, 16, 128), np.float32)
    for m in range(16):
        dj = -m
        La[0, m] = s_l
        La[2, m] = s_l
        La[1, m] = 64.0 * (2 * dj - 1)
        La[3, m] = 64.0 * (2 * dj - 1)
    tb['La'] = La.reshape(128, 16 * 128)
    n = np.arange(127, dtype=np.float32)
    Lc = np.zeros((128, 16, 127), np.float32)
    for i in range(16):
        Lc[0, i] = 16.0 * (n - 8 * i)
        Lc[2, i] = 16.0 * (n - 8 * i)
        Lc[1, i] = -33.0
        Lc[3, i] = -33.0
    tb['Lc'] = Lc.reshape(128, 16 * 127)
    slopes = (2.0 ** (-8.0 * np.arange(1, 17) / 16)).astype(np.float32)
    Ra = np.zeros((128, 4, 4, 128), np.float32)
    for gg in range(4):
        g = gg
        for h in range(4):
            hi, lo = _bf16_split(slopes[4 * g + h])
            Ra[0, gg, h] = hi
            Ra[1, gg, h] = hi
            Ra[2, gg, h] = lo
            Ra[3, gg, h] = lo
    tb['Ra'] = Ra.reshape(128, 4 * 512)
    Sel = np.zeros((128, 16, 127), np.float32)
    for i in range(16):
        k = np.clip(np.arange(127) - 8 * i + 64, 0, 127)
        Sel[k, i, np.arange(127)] = 1.0
    tb['Sel'] = Sel.reshape(128, 16 * 127)
    kk = np.arange(128)[:, None]
    tl = np.arange(128)[None, :]
    G = np.where(16 * (kk - 64) + 31 > tl, -BIG, 0.0).astype(np.float32)
    tb['G'] = np.tile(G[:, None, :], (1, 4, 1)).reshape(128, 512)
    tric = np.where(kk > tl, -BIG, 0.0).astype(np.float32)
    trib = np.where(kk <= tl, -BIG, 0.0).astype(np.float32)
    tb['TRIc'] = np.tile(tric[:, None, :], (1, 4, 1)).reshape(128, 512)
    tb['TRIb'] = np.tile(trib[:, None, :], (1, 4, 1)).reshape(128, 512)
    E = np.zeros((128, 16, 128), np.float32)
    for jt in range(16):
        E[2 * jt, jt, 0:64] = 1.0
        E[2 * jt + 1, jt, 64:128] = 1.0
    tb['E'] = E.reshape(128, 16 * 128)
    VAL = np.zeros((128, 16, 32), np.float32)
    ADD = np.zeros((128, 16, 32), np.float32)
    for i in range(16):
        for t_l in range(128):
            cur = (128 * i + t_l) // 64
            for j in range(32):
                if j == 0:
                    ADD[t_l, i, j] = 3e30
                elif j == cur:
                    ADD[t_l, i, j] = 2e30
                elif j == cur - 1:
                    ADD[t_l, i, j] = 1e30
                elif j <= cur:
                    VAL[t_l, i, j] = 1.0
                else:
                    ADD[t_l, i, j] = -1e30
    tb['VAL'] = VAL.reshape(128, 512)
    tb['ADD'] = ADD.reshape(128, 512)
    c_start = np.arange(127) * 16
    s_start = np.arange(32) * 64
    ov = ((c_start[:, None] <= s_start[None, :] + 63) & (c_start[:, None] + 31 >= s_start[None, :]))
    OV = np.zeros((128, 32), np.float32)
    OV[:127] = ov
    tb['OV'] = OV
    rst = np.ones((128, 512), np.float32)
    rst[:, 0::128] = 0.0
    tb['RST'] = rst
    cat = np.concatenate([tb[k] for k in ('La', 'Lc', 'Ra', 'Sel', 'G', 'TRIc', 'TRIb', 'E', 'OV')], axis=1)
    f32 = np.concatenate([tb[k] for k in ('VAL', 'ADD', 'RST')], axis=1)
    return c_(cat.astype(np.float32)), c_(f32)


CB_OFF = {}
_o = 0
for _k, _w in (('La', 2048), ('Lc', 16 * 127), ('Ra', 2048), ('Sel', 16 * 127), ('G', 512), ('TRIc', 512),
               ('TRIb', 512), ('E', 2048), ('OV', 32)):
    CB_OFF[_k] = _o
    _o += _w
CB_W = _o


def build_even_A(e_idx, nh=8, ng=2, nqt=NT):
    nc = bass.Bass("TRN2", target_bir_lowering=False)
    with ExitStack() as st:
        P = Prog(nc, st)
        x_d = P.dram("x", [T, D], F32, "ExternalInput")
        nw_d = P.dram("nw", [D], F32, "ExternalInput")
        whg_d = P.dram("whg", [D, 8 * 512], F32, "ExternalInput")
        lbl_d = P.dram("lbl", [128, 16], F32, "ExternalInput")
        gain_d = P.dram("gain", [1024], F32, "ExternalInput")
        wnq_d = P.dram("wnq", [D, 2 * 512], F32, "ExternalInput")
        wnkv_d = P.dram("wnkv", [D, 2 * 768], F32, "ExternalInput")
        wngt_d = P.dram("wngt", [D, 2 * 12], F32, "ExternalInput")
        wnbg_d = P.dram("wnbg", [D, 2 * 512], F32, "ExternalInput")
        w1k_d = P.dram("w1k", [4096, 128], F32, "ExternalInput")
        w1v_d = P.dram("w1v", [4096, 128], F32, "ExternalInput")
        w2k_d = P.dram("w2k", [128, 128], F32, "ExternalInput")
        w2v_d = P.dram("w2v", [128, 128], F32, "ExternalInput")
        pek_d = P.dram("pekT", [128, 32], F32, "ExternalInput")
        pev_d = P.dram("pevT", [128, 32], F32, "ExternalInput")
        cb_d = P.dram("cb", [128, CB_W], F32, "ExternalInput")
        cf_d = P.dram("cf", [128, 1536], F32, "ExternalInput")
        out_d = P.dram("mixT", [2048, T], BF16, "ExternalOutput")

        ident = make_ident(P, st)
        idf_cm = P.sb(st, "idf2", [128, 128], F32)
        cm = P.sb(st, "cm", [128, 128], F32)
        P.op('pool', lambda e: e.iota(idf_cm[:], pattern=[[1, 128]], base=0, channel_multiplier=-1,
                                      allow_small_or_imprecise_dtypes=True), (), [idf_cm])
        P.ts('dve', cm[:], idf_cm[:], 0.0, None, ALU.is_ge, None, [idf_cm], [cm])
        hT = P.sb(st, "hT", [128, KC, T], BF16)
        hTv = [P.view(hT, "hT%d" % i) for i in range(NT)]
        eps = P.sb(st, "eps", [128, 1], F32)
        eps128 = eps
        ss = P.sb(st, "ss", [128, 4], F32)
        identf = P.sb(st, "identf", [128, 128], F32)
        P.ts('dve', identf[:], idf_cm[:], 0.0, None, ALU.is_equal, None, [idf_cm], [identf])
        cf = P.sb(st, "cf", [128, 1536], F32)
        P.memset('dve', eps[:], EPS, [eps])
        P.dma('sp', cf[:], cf_d.t.ap(), (), [cf], owner=cf)
        VAL = cf[:, 0:512]
        ADD = cf[:, 512:1024]
        RST = cf[:, 1024:1536]

        with ExitStack() as s0:
            xt = [P.sb(s0, "xt%d" % i, [128, D], F32) for i in range(2)]
            xn = [P.sb(s0, "xn%d" % i, [128, D], BF16) for i in range(2)]
            nwb = P.sb(s0, "nwb", [128, D], F32)
            ptr = [P.ps(s0, "ptr%d" % i, [128, 4, 128], BF16) for i in range(2)]
            P.dma('sp', nwb[:], nw_d.t.ap().partition_broadcast(128), (), [nwb], owner=nwb)
            emit_hT(P, s0, x_d, nwb, ident, hT, hTv, xt, xn, ptr, {'eps': eps, 'ss': ss})
            P.barrier()

        with ExitStack() as s1:
            ptr = [P.ps(s1, "ptrh%d" % i, [128, 4, 128], BF16) for i in range(2)]
            emit_hgrn2(P, s1, e_idx, hT, hTv, ident, cm, RST, eps, ptr, whg_d, lbl_d, gain_d, out_d, nh)
            P.barrier()

        with ExitStack() as s2:
            ptr = P.ps(s2, "ptrn", [128, 4, 128], BF16)
            emit_nsa(P, s2, hT, hTv, (ident, identf), ptr, VAL, ADD, wnq_d, wnkv_d, wngt_d, wnbg_d, w1k_d, w1v_d,
                     w2k_d, w2v_d, pek_d, pev_d, cb_d, out_d, ng, nqt)
            P.barrier()
        P.wait_all('sp', [out_d])
        P.emit()
    return nc


def emit_hgrn2(P, s1, e_idx, hT, hTv, ident, cm, RST, eps, ptr, whg_d, lbl_d, gain_d, out_d, nh):
    f32b = lambda n: P.sb(s1, n, [128, 512], F32)
    bfb = lambda n: P.sb(s1, n, [128, 512], BF16)
    whg = [P.sb(s1, "whg%d" % i, [128, KC, 512], BF16) for i in range(2)]
    lbl = P.sb(s1, "lbl", [128, 16, 2], F32)
    lbt = P.sb(s1, "lbt", [128, 16, 4], F32)
    lb = P.sb(s1, "lb", [128, 16], F32)
    oml = P.sb(s1, "oml", [128, 16], F32)
    gainb = P.sb(s1, "gainb", [128, 2048], F32)
    qs, t1, t2, gg_, kk, b128, dd, d3 = [f32b("hg_f%d" % i) for i in range(8)]
    EA1, EA2, EB1, EB2, EQB, EKA, E3, E5 = [f32b("hg_e%d" % i) for i in range(8)]
    QA, QB, KA, KB, QOB, KOA, QP, KH = [bfb("hg_o%d" % i) for i in range(8)]
    vtok = [P.sb(s1, "vtok%d" % i, [128, 4, 128], BF16) for i in range(2)]
    ag = [P.sb(s1, "ag%d" % i, [128, 4, 128], F32) for i in range(2)]
    at4 = P.sb(s1, "at4", [128, 4, 128], BF16)
    kht4 = P.sb(s1, "kht4", [128, 4, 128], BF16)
    S5 = P.sb(s1, "S5", [128, 5, 128], F32)
    Sb5 = P.sb(s1, "Sb5", [128, 5, 128], BF16)
    ob4 = P.sb(s1, "ob4", [128, 4, 128], F32)
    junk4 = P.sb(s1, "junk4", [128, 4, 128], F32)
    yb4 = P.sb(s1, "yb4", [128, 4, 128], BF16)
    sst4 = P.sb(s1, "sst4", [128, 8], F32)
    yaT = [P.sb(s1, "yaT%d" % i, [128, T], BF16) for i in range(2)]
    pq = [P.ps(s1, "pq%d" % i, [128, 512], F32) for i in range(2)]
    pv = [P.ps(s1, "pv%d" % i, [128, 512], F32) for i in range(2)]
    patt = P.ps(s1, "patt", [128, 4, 128], F32)
    po = P.ps(s1, "po", [128, 4, 128], F32)
    pS = P.ps(s1, "pS", [128, 4, 128], F32)

    P.dma('sp', lbl[:], lbl_d.t.ap().rearrange("p (h e) -> p h e", e=2), (), [lbl], owner=lbl)
    P.dma('sp', gainb[:], gain_d.t.ap().partition_broadcast(128), (), [gainb], owner=gainb)
    P.tt('dve', lbt[:, :, 0], lbl[:, :, 0], lbl[:, :, 1], ALU.max, [lbl], [lbt])
    P.tt('dve', lbt[:, :, 1], lbl[:, :, 0], lbt[:, :, 0], ALU.subtract, [lbl, lbt], [lbt])
    P.tt('dve', lbt[:, :, 2], lbl[:, :, 1], lbt[:, :, 0], ALU.subtract, [lbl, lbt], [lbt])
    P.act(lbt[:, :, 1:3], lbt[:, :, 1:3], AF.Exp, [lbt], [lbt])
    P.tt('dve', lbt[:, :, 3], lbt[:, :, 1], lbt[:, :, 2], ALU.add, [lbt], [lbt])
    P.op('dve', lambda e: e.reciprocal(out=lbt[:, :, 3], in_=lbt[:, :, 3]), [lbt], [lbt])
    P.tt('dve', lbt[:, :, 1], lbt[:, :, 1], lbt[:, :, 3], ALU.mult, [lbt], [lbt])
    P.tt('dve', lbt[:, :, 2], lbt[:, :, 2], lbt[:, :, 3], ALU.mult, [lbt], [lbt])
    if e_idx == 0:
        P.tt('dve', lb[:], lbt[:, :, 1], lbt[:, :, 1], ALU.subtract, [lbt], [lb])
    else:
        P.tt('dve', lbt[:, :, 3], lbt[:, :, 1], lbt[:, :, 2], ALU.add, [lbt], [lbt])
        P.tt('dve', lb[:], lbt[:, :, 3], lbt[:, :, 1], ALU.subtract, [lbt], [lb])
    P.ts('dve', oml[:], lb[:], -1.0, 1.0, ALU.mult, ALU.add, [lb], [oml])
    for b_ in (EA1, EA2, EB1, EB2, EQB, EKA):
        P.memset('pool', b_[:], 0.0, [b_])

    def v4(buf):
        return buf[:].rearrange("p (j c) -> p j c", c=128)

    def load_w(hd):
        w = whg[hd % 2]
        P.dma('pool', w[:], whg_d.t[:, hd * 512:(hd + 1) * 512].rearrange("(kc p) c -> p kc c", p=128),
              (), [w], owner=w)

    pt0, pt1 = ptr
    steps = [(hd, tb) for hd in range(nh) for tb in range(4)]

    def Pm(k):
        hd, tb = steps[k]
        w = whg[hd % 2]
        if tb == 0 and hd + 1 < nh:
            load_w(hd + 1)
        hts = hTv[4 * tb:4 * tb + 4]
        tsl = slice(tb * 512, (tb + 1) * 512)
        for kc in range(KC):
            P.mm(pq[0][:], w[:, kc, 0:128], hT[:, kc, tsl], kc == 0, kc == KC - 1, [w] + hts, [pq[0]])
        for kc in range(KC):
            P.mm(pq[1][:], w[:, kc, 128:256], hT[:, kc, tsl], kc == 0, kc == KC - 1, [w] + hts, [pq[1]])
        for j in range(4):
            pvb = pv[j // 2]
            for kc in range(KC):
                P.mm(pvb[:, (j % 2) * 256:(j % 2 + 1) * 256], hT[:, kc, tb * 512 + j * 128:tb * 512 + (j + 1) * 128],
                     w[:, kc, 256:512], kc == 0, kc == KC - 1, [w, hts[j]], [pvb])

    def Pe(k):
        vt, agk = vtok[k % 2], ag[k % 2]
        P.act(qs[:], pq[0][:], AF.Silu, [pq[0]], [qs])
        P.act(t1[:], pq[1][:], AF.Exp, [pq[1]], [t1], scale=-1.0)
        for j in range(4):
            pvb = pv[j // 2]
            o0 = (j % 2) * 256
            P.cp('dve', vt[:, j, :], pvb[:, o0:o0 + 128], [pvb], [vt])
            P.act(agk[:, j, :], pvb[:, o0 + 128:o0 + 256], AF.Silu, [pvb], [agk])

    def E(k):
        hd, tb = steps[k]
        P.act(t1[:], t1[:], AF.Ln, [t1], [t1], bias=1.0)
        P.act(t1[:], t1[:], AF.Exp, [t1], [t1], scale=-1.0)
        P.ts('dve', t2[:], t1[:], oml[:, hd:hd + 1], lb[:, hd:hd + 1], ALU.mult, ALU.add,
             [t1, oml, lb], [t2])
        P.ts('dve', t2[:], t2[:], 1e-30, None, ALU.max, None, [t2], [t2])
        P.act(gg_[:], t2[:], AF.Ln, [t2], [gg_])
        P.ts('dve', kk[:], t2[:], -1.0, 1.0, ALU.mult, ALU.add, [t2], [kk])
        P.scan(b128[:], RST, gg_[:], 0.0, [gg_], [b128])
        bv = v4(b128)
        dv_ = v4(dd)
        P.tt('dve', dv_[:, :, 0:64], bv[:, :, 0:64], bv[:, :, 31:32].to_broadcast([128, 4, 64]),
             ALU.subtract, [b128], [dd])
        P.tt('dve', dv_[:, :, 64:128], bv[:, :, 64:128], bv[:, :, 95:96].to_broadcast([128, 4, 64]),
             ALU.subtract, [b128], [dd])
        P.act(v4(EA1)[:, :, 0:64], dv_[:, :, 0:64], AF.Exp, [dd], [EA1])
        P.act(v4(EA2)[:, :, 0:64], dv_[:, :, 0:64], AF.Exp, [dd], [EA2], scale=-1.0)
        P.act(v4(EB1)[:, :, 64:128], dv_[:, :, 64:128], AF.Exp, [dd], [EB1])
        P.act(v4(EB2)[:, :, 64:128], dv_[:, :, 64:128], AF.Exp, [dd], [EB2], scale=-1.0)
        d3v = v4(d3)
        P.tt('dve', d3v[:, :, :], bv[:, :, :], bv[:, :, 63:64].to_broadcast([128, 4, 128]),
             ALU.subtract, [b128], [d3])
        P.act(v4(EQB)[:, :, 64:128], d3v[:, :, 64:128], AF.Exp, [d3], [EQB])
        P.act(v4(EKA)[:, :, 0:64], d3v[:, :, 0:64], AF.Exp, [d3], [EKA], scale=-1.0)
        P.act(E3[:], b128[:], AF.Exp, [b128], [E3])
        P.tt('dve', d3v[:, :, :], bv[:, :, :], bv[:, :, 127:128].to_broadcast([128, 4, 128]),
             ALU.subtract, [b128], [d3])
        P.act(E5[:], d3[:], AF.Exp, [d3], [E5], scale=-1.0)
        P.tt('dve', QA[:], qs[:], EA1[:], ALU.mult, [qs, EA1], [QA])
        P.tt('dve', QB[:], qs[:], EB1[:], ALU.mult, [qs, EB1], [QB])
        P.tt('dve', KA[:], kk[:], EA2[:], ALU.mult, [kk, EA2], [KA])
        P.tt('dve', KB[:], kk[:], EB2[:], ALU.mult, [kk, EB2], [KB])
        P.tt('pool', QOB[:], qs[:], EQB[:], ALU.mult, [qs, EQB], [QOB])
        P.tt('pool', KOA[:], kk[:], EKA[:], ALU.mult, [kk, EKA], [KOA])
        P.tt('pool', QP[:], qs[:], E3[:], ALU.mult, [qs, E3], [QP])
        P.tt('pool', KH[:], kk[:], E5[:], ALU.mult, [kk, E5], [KH])

    def L(k):
        hd, tb = steps[k]
        vt, agk = vtok[k % 2], ag[k % 2]
        yT = yaT[hd % 2]
        if tb == 0:
            P.memset('dve', S5[:, 0, :], 0.0, [S5])
            P.memset('dve', Sb5[:, 0, :], 0.0, [Sb5])
        cs_ = [slice(j * 128, (j + 1) * 128) for j in range(4)]
        for j in range(4):
            P.mm(patt[:, j, :], KA[:, cs_[j]], QA[:, cs_[j]], True, False, [KA, QA], [patt])
            P.mm(patt[:, j, :], KB[:, cs_[j]], QB[:, cs_[j]], False, False, [KB, QB], [patt])
            P.mm(patt[:, j, :], KOA[:, cs_[j]], QOB[:, cs_[j]], False, True, [KOA, QOB], [patt])
        P.tt('dve', at4[:], patt[:], cm[:].unsqueeze(1).to_broadcast([128, 4, 128]), ALU.mult, [patt, cm], [at4])
        for j in range(4):
            P.tr(pt0[:, j, :], KH[:, cs_[j]], ident[:], [KH, ident], [pt0])
        P.cp('act', kht4[:], pt0[:, 0:4, :], [pt0], [kht4])
        for j in range(4):
            P.mm(pS[:, j, :], kht4[:, j, :], vt[:, j, :], True, True, [kht4, vt], [pS])
        for j in range(4):
            P.stt(S5[:, j + 1, :], S5[:, j, :], E3[:, (j + 1) * 128 - 1:(j + 1) * 128], pS[:, j, :],
                  ALU.mult, ALU.add, [S5, E3, pS], [S5])
        P.cp('act', Sb5[:, 1:5, :], S5[:, 1:5, :], [S5], [Sb5])
        for j in range(4):
            P.mm(po[:, j, :], at4[:, j, :], vt[:, j, :], True, False, [at4, vt], [po])
            P.mm(po[:, j, :], QP[:, cs_[j]], Sb5[:, j, :], False, True, [QP, Sb5], [po])
        P.cp('pool', S5[:, 0, :], S5[:, 4, :], [S5], [S5])
        P.cp('pool', Sb5[:, 0, :], Sb5[:, 4, :], [Sb5], [Sb5])
        P.cp('dve', ob4[:], po[:], [po], [ob4])
        P.act(junk4[:], ob4[:], AF.Square, [ob4], [junk4])
        P.op('dve', lambda e: e.tensor_reduce(out=sst4[:, 0:4], in_=junk4[:], axis=mybir.AxisListType.X,
                                              op=ALU.add), [junk4], [sst4])
        P.act(sst4[:, 4:8], sst4[:, 0:4], AF.Sqrt, [sst4, eps], [sst4], scale=1.0 / 128, bias=eps[:])
        P.op('dve', lambda e: e.reciprocal(out=sst4[:, 4:8], in_=sst4[:, 4:8]), [sst4], [sst4])
        P.tt('dve', ob4[:], ob4[:], sst4[:, 4:8].unsqueeze(2).to_broadcast([128, 4, 128]), ALU.mult,
             [ob4, sst4], [ob4])
        P.tt('dve', ob4[:], ob4[:],
             gainb[:, hd * 128:(hd + 1) * 128].unsqueeze(1).to_broadcast([128, 4, 128]), ALU.mult,
             [ob4, gainb], [ob4])
        P.tt('dve', yb4[:], ob4[:], agk[:], ALU.mult, [ob4, agk], [yb4])
        for j in range(4):
            P.tr(pt1[:, 4 + j, :], yb4[:, j, :], ident[:], [yb4, ident], [pt1])
        P.cp('act', yT[:, tb * 512:(tb + 1) * 512].rearrange("p (j c) -> p j c", c=128), pt1[:, 4:8, :], [pt1], [yT])
        if tb == 3:
            P.dma('sp', out_d.t[hd * 128:(hd + 1) * 128, :], yT[:], [yT], (), owner=yT)

    load_w(0)
    Pm(0)
    Pe(0)
    for k in range(len(steps)):
        if k + 1 < len(steps):
            Pm(k + 1)
        E(k)
        if k + 1 < len(steps):
            Pe(k + 1)
        L(k)


def emit_nsa(P, s2, hT, hTv, ident, ptr, VAL, ADD, wnq_d, wnkv_d, wngt_d, wnbg_d, w1k_d, w1v_d,
             w2k_d, w2v_d, pek_d, pev_d, cb_d, out_d, ng=4, nqt=NT):
    cf_reads = []
    cb = P.sb(s2, "cb", [128, CB_W], BF16)
    P.dma('pool', cb[:], cb_d.t.ap(), (), [cb], owner=cb)

    def CB(name, lo, hi):
        return cb[:, CB_OFF[name] + lo:CB_OFF[name] + hi]

    wbuf = P.sb(s2, "wbuf", [128, KC, 768], BF16)
    wgt = P.sb(s2, "wgt", [128, KC, 12], BF16)
    w1k = P.sb(s2, "w1k", [128, 32, 128], BF16)
    w1v = P.sb(s2, "w1v", [128, 32, 128], BF16)
    w2k = P.sb(s2, "w2k", [128, 128], BF16)
    w2v = P.sb(s2, "w2v", [128, 128], BF16)
    pek = P.sb(s2, "pek", [128, 32], BF16)
    pev = P.sb(s2, "pev", [128, 32], BF16)
    P.dma('pool', w1k[:], w1k_d.t.ap().rearrange("(j d) o -> d j o", d=128), (), [w1k], owner=w1k)
    P.dma('pool', w1v[:], w1v_d.t.ap().rearrange("(j d) o -> d j o", d=128), (), [w1v], owner=w1v)
    P.dma('pool', w2k[:], w2k_d.t.ap(), (), [w2k], owner=w2k)
    P.dma('pool', w2v[:], w2v_d.t.ap(), (), [w2v], owner=w2v)
    P.dma('pool', pek[:], pek_d.t.ap(), (), [pek], owner=pek)
    P.dma('pool', pev[:], pev_d.t.ap(), (), [pev], owner=pev)

    qT = P.sb(s2, "qT", [128, 4, T], BF16)
    kcT = P.sb(s2, "kcT", [128, T], BF16)
    vcT = P.sb(s2, "vcT", [128, T], BF16)
    ksT = P.sb(s2, "ksT", [128, T], BF16)
    kwT = P.sb(s2, "kwT", [128, T], BF16)
    vsA = P.sb(s2, "vsA", [128, NT, 129], BF16)
    vwA = P.sb(s2, "vwA", [128, NT, 129], BF16)
    hidk = P.sb(s2, "hidk", [128, 127], BF16)
    hidv = P.sb(s2, "hidv", [128, 127], BF16)
    cbias = P.sb(s2, "cbias", [128, 2], F32)
    KcT = P.sb(s2, "KcT", [128, 127], BF16)
    VcA = P.sb(s2, "VcA", [128, 161], BF16)
    Rsel = P.sb(s2, "Rsel", [128, 4, 128], BF16)
    Pt = [P.sb(s2, "Pt%d" % i, [128, 4, 128], BF16) for i in range(3)]
    gts = P.sb(s2, "gts", [128, 12], F32)
    sgate = P.sb(s2, "sgate", [128, 512], F32)
    zz = P.sb(s2, "zz", [128, 3, 4], F32)
    cs = P.sb(s2, "cs", [128, 3, 4], F32)
    acc = P.sb(s2, "acc", [128, 4, 128], F32)
    imp = P.sb(s2, "imp", [128, 32], F32)
    score = P.sb(s2, "score", [128, 32], F32)
    work = P.sb(s2, "work", [128, 32], F32)
    m8 = P.sb(s2, "m8", [128, 16], F32)
    nm = P.sb(s2, "nm", [128, 32], F32)
    ybf = P.sb(s2, "ybf", [128, 512], BF16)
    yTt = [P.sb(s2, "yTt%d" % i, [128, 4, 128], BF16) for i in range(2)]
    psA = [P.ps(s2, "psA%d" % i, [128, 4, 128], F32) for i in range(2)]
    ident, identf = ident
    psO4 = P.ps(s2, "psO", [128, 4, 512], F32)
    psO = [psO4] * 4
    ocp = P.sb(s2, "ocp", [128, 4, 161], F32)
    impm = P.sb(s2, "impm", [128, 4, 32], F32)
    tmpo = P.sb(s2, "tmpo", [128, 4, 128], F32)
    psM = P.ps(s2, "psM", [128, 512], F32)
    psG = psM

    P.memset('dve', vsA[:, :, 128:129], 1.0, [vsA])
    P.memset('dve', vwA[:, :, 128:129], 1.0, [vwA])
    P.memset('dve', Rsel[:], 0.0, [Rsel])
    P.memset('dve', VcA[:], 0.0, [VcA])
    P.memset('dve', VcA[:, 128:129], 1.0, [VcA])
    P.cp('dve', VcA[:, 129:161], CB('OV', 0, 32), [cb], [VcA])

    def Oh(h):
        return psO4[:, h, 0:256]

    npa = [0]
    npt = [1]

    def nextA():
        p = psA[npa[0] % 2]
        npa[0] += 1
        return p

    for gg in range(ng):
        Ra = CB('Ra', gg * 512, (gg + 1) * 512)
        P.dma('pool', wbuf[:, :, 0:512], wnq_d.t[:, gg * 512:(gg + 1) * 512].rearrange("(kc p) c -> p kc c", p=128),
              (), [wbuf], owner=wbuf)
        for h in range(4):
            for tb in range(4):
                pp = nextA()
                ppf = pp[:].rearrange("p a b -> p (a b)")
                for kc in range(KC):
                    P.mm(ppf, wbuf[:, kc, h * 128:(h + 1) * 128], hT[:, kc, tb * 512:(tb + 1) * 512],
                         kc == 0, kc == KC - 1, [wbuf] + hTv[4 * tb:4 * tb + 4], [pp])
                P.act(qT[:, h, tb * 512:(tb + 1) * 512], ppf, AF.Copy, [pp], [qT], scale=SCALE)
        P.dma('pool', wbuf[:], wnkv_d.t[:, gg * 768:(gg + 1) * 768].rearrange("(kc p) c -> p kc c", p=128),
              (), [wbuf], owner=wbuf)
        P.dma('pool', wgt[:], wngt_d.t[:, gg * 12:(gg + 1) * 12].rearrange("(kc p) c -> p kc c", p=128),
              (), [wgt], owner=wgt)
        for (dst, col) in ((kcT, 0), (vcT, 128), (ksT, 256), (kwT, 512)):
            for tb in range(4):
                pp = nextA()
                ppf = pp[:].rearrange("p a b -> p (a b)")
                for kc in range(KC):
                    P.mm(ppf, wbuf[:, kc, col:col + 128], hT[:, kc, tb * 512:(tb + 1) * 512],
                         kc == 0, kc == KC - 1, [wbuf] + hTv[4 * tb:4 * tb + 4], [pp])
                P.cp('dve' if tb % 2 else 'act', dst[:, tb * 512:(tb + 1) * 512], ppf, [pp], [dst])
        for (dst, col) in ((vsA, 384), (vwA, 640)):
            for tt in range(NT):
                pp = nextA()
                ppf = pp[:].rearrange("p a b -> p (a b)")
                for kc in range(KC):
                    P.mm(ppf[:, 0:128], hT[:, kc, tt * 128:(tt + 1) * 128], wbuf[:, kc, col:col + 128],
                         kc == 0, kc == KC - 1, [wbuf, hTv[tt]], [pp])
                P.cp('dve' if tt % 2 else 'act', dst[:, tt, 0:128], ppf[:, 0:128], [pp], [dst])
        P.dma('pool', wbuf[:, :, 0:512], wnbg_d.t[:, gg * 512:(gg + 1) * 512].rearrange("(kc p) c -> p kc c", p=128),
              (), [wbuf], owner=wbuf)
        for (src, w1, w2, pe, hid, col) in ((kcT, w1k, w2k, pek, hidk, 0), (vcT, w1v, w2v, pev, hidv, 1)):
            srcv = src[:].rearrange("p (n r) -> p n r", r=16)
            for j in range(32):
                P.mm(psM[:, 0:127], w1[:, j, :], srcv[:, j // 16:j // 16 + 127, j % 16], j == 0, j == 31,
                     [w1, src], [psM])
            for j in range(32):
                P.mm(psM[:, 128:129], w1[:, j, :], pe[:, j:j + 1], j == 0, j == 31, [w1, pe], [psM])
            P.cp('dve', cbias[:, col:col + 1], psM[:, 128:129], [psM], [cbias])
            P.act(hid[:], psM[:, 0:127], AF.Silu, [psM, cbias], [hid], bias=cbias[:, col:col + 1])
        P.mm(psM[:, 256:383], w2k[:], hidk[:], True, True, [w2k, hidk], [psM])
        P.cp('dve', KcT[:], psM[:, 256:383], [psM], [KcT])
        P.mm(psM[0:127, 384:512], hidv[:], w2v[:], True, True, [hidv, w2v], [psM])
        P.cp('dve', VcA[0:127, 0:128], psM[0:127, 384:512], [psM], [VcA])

        for i in range(nqt):
            isl = slice(i * 128, (i + 1) * 128)
            qrhs = qT[:, :, isl]
            for kc in range(KC):
                P.mm(psG[:], hT[:, kc, isl], wbuf[:, kc, 0:512], kc == 0, kc == KC - 1, [wbuf, hTv[i]], [psG])
            P.act(sgate[:], psG[:], AF.Silu, [psG], [sgate])
            for kc in range(KC):
                P.mm(psM[:, 0:12], hT[:, kc, isl], wgt[:, kc, :], kc == 0, kc == KC - 1, [wgt, hTv[i]], [psM])
            P.act(gts[:], psM[:, 0:12], AF.Sigmoid, [psM], [gts])
            gv = gts[:].rearrange("p (h j) -> p j h", j=3)

            def finish(br, width):
                P.cp('dve', ocp[:, :, 0:width], psO4[:, :, 0:width], [psO4], [ocp])
                P.ts('dve', zz[:, br, :], ocp[:, :, 128], 1e-37, None, ALU.max, None, [ocp], [zz])
                P.op('dve', lambda e, br=br: e.reciprocal(out=zz[:, br, :], in_=zz[:, br, :]), [zz], [zz])
                if br == 0:
                    P.tt('dve', impm[:], ocp[:, :, 129:161], zz[:, 0, :].unsqueeze(2).to_broadcast([128, 4, 32]),
                         ALU.mult, [ocp, zz], [impm])
                    P.op('dve', lambda e: e.tensor_reduce(out=imp[:], in_=impm[:].rearrange("p h j -> p j h"),
                                                          axis=mybir.AxisListType.X, op=ALU.add), [impm], [imp])
                P.tt('dve', cs[:, br, :], zz[:, br, :], gv[:, br, :], ALU.mult, [zz, gts], [cs])
                if br == 0:
                    P.tt('dve', acc[:], ocp[:, :, 0:128], cs[:, br, :].unsqueeze(2).to_broadcast([128, 4, 128]),
                         ALU.mult, [ocp, cs], [acc])
                else:
                    P.tt('dve', tmpo[:], ocp[:, :, 0:128], cs[:, br, :].unsqueeze(2).to_broadcast([128, 4, 128]),
                         ALU.mult, [ocp, cs], [tmpo])
                    P.tt('dve', acc[:], acc[:], tmpo[:], ALU.add, [acc, tmpo], [acc])

            pp = nextA()
            P.mm(pp[0:127, :, :], KcT[:], qrhs, True, False, [KcT, qT], [pp])
            P.mm(pp[0:127, :, :], CB('Lc', i * 127, (i + 1) * 127), Ra.rearrange("p (a b) -> p a b", b=128),
                 False, False, [cb], [pp])
            P.mm(pp[0:127, :, :], CB('Sel', i * 127, (i + 1) * 127),
                 CB('G', 0, 512).rearrange("p (a b) -> p a b", b=128), False, True, [cb], [pp])
            pt_ = Pt[0]
            P.act(pt_[0:127, :, :], pp[0:127, :, :], AF.Exp, [pp], [pt_])
            for h in range(4):
                P.mm(Oh(h)[:, 0:161], pt_[0:127, h, :], VcA[0:127, :], True, True, [pt_, VcA], [psO[h]])
            finish(0, 161)
            P.tt('dve', score[:], imp[:], VAL[:, i * 32:(i + 1) * 32], ALU.mult, [imp], [score])
            P.tt('dve', score[:], score[:], ADD[:, i * 32:(i + 1) * 32], ALU.add, [score], [score])
            P.op('dve', lambda e: e.max(out=m8[:, 0:8], in_=score[:]), [score], [m8])
            P.op('dve', lambda e: e.match_replace(out=work[:], in_to_replace=m8[:, 0:8], in_values=score[:],
                                                  imm_value=-3e38), [score, m8], [work])
            P.op('dve', lambda e: e.max(out=m8[:, 8:16], in_=work[:]), [work], [m8])
            P.ts('dve', nm[:], score[:], m8[:, 15:16], -BIG, ALU.is_lt, ALU.mult, [score, m8], [nm])
            P.tr(psM[0:32, 128:256], nm[:], identf[:], [nm, identf], [psM])
            P.cp('dve', Rsel[0:32, :, :], psM[0:32, 128:256].rearrange("p (a b) -> p a b", a=1).to_broadcast([32, 4, 128]),
                 [psM], [Rsel])

            for br, (kT_, vA_) in ((2, (kwT, vwA)), (1, (ksT, vsA))):
                jlo = 0 if br == 1 else max(0, i - 4)

                def emit_qk(jt, br=br, kT_=kT_, jlo=jlo):
                    dj = jt - i
                    pp = nextA()
                    extra = []
                    if br == 1:
                        extra.append((CB('E', jt * 128, (jt + 1) * 128), Rsel[:], [cb, Rsel]))
                    if jt == i:
                        extra.append((ident[:], CB('TRIc', 0, 512).rearrange("p (a b) -> p a b", b=128), [ident, cb]))
                    if br == 2 and jt == i - 4:
                        extra.append((ident[:], CB('TRIb', 0, 512).rearrange("p (a b) -> p a b", b=128), [ident, cb]))
                    P.mm(pp[:], kT_[:, jt * 128:(jt + 1) * 128], qrhs, True, False, [kT_, qT], [pp])
                    P.mm(pp[:], CB('La', (-dj) * 128, (-dj + 1) * 128), Ra.rearrange("p (a b) -> p a b", b=128),
                         False, len(extra) == 0, [cb], [pp])
                    for xi, (l_, r_, rd) in enumerate(extra):
                        P.mm(pp[:], l_, r_, False, xi == len(extra) - 1, rd, [pp])
                    return pp

                def emit_pv(jt, pp, vA_=vA_, jlo=jlo):
                    pt_ = Pt[npt[0] % 3]
                    npt[0] += 1
                    P.act(pt_[:], pp[:], AF.Exp, [pp], [pt_])
                    for h in range(4):
                        P.mm(Oh(h)[:, 0:129], pt_[:, h, :], vA_[:, jt, :], jt == jlo, jt == i,
                             [pt_, vA_], [psO[h]])

                pend = None
                for jt in range(jlo, i + 1):
                    pp = emit_qk(jt)
                    if pend is not None:
                        emit_pv(*pend)
                    pend = (jt, pp)
                emit_pv(*pend)
                finish(br, 129)
            P.tt('dve', ybf[:], acc[:].rearrange("p a b -> p (a b)"), sgate[:], ALU.mult, [acc, sgate], [ybf])
            ptb = ptr
            for h in range(4):
                P.tr(ptb[:, h, :], ybf[:, h * 128:(h + 1) * 128], ident[:], [ybf, ident], [ptb])
            yt = yTt[i % 2]
            P.cp('act', yt[:], ptb[:], [ptb], [yt])
            P.dma('sp', out_d.t[2048 + gg * 512:2048 + (gg + 1) * 512, isl].rearrange("(h c) t -> c h t", c=128),
                  yt[:], [yt], (), owner=yt)


EV_OFF = dict(a_q=0, a_f=2048, a_i=4096, a_g=6144, b_q=8192, b_kc=10240, b_vc=10752, b_ks=11264, b_vs=11776,
              b_kw=12288, b_vw=12800, b_gate=13312, b_g=13360)


def run_even_A(e_idx, x, nw, w_in, lb_logits, gain, pe_k, w1_k, w2_k, pe_v, w1_v, w2_v):
    nc = _get("evenA%d" % e_idx, lambda: build_even_A(e_idx))
    in_maps = []
    tabs = [even_const_tables(hh) for hh in range(2)]
    packed = []
    for hh in range(2):
        cols = []
        for hd in range(8):
            gh = 8 * hh + hd
            for k in ('a_q', 'a_f', 'a_i', 'a_g'):
                cols.append(w_in[:, EV_OFF[k] + gh * 128:EV_OFF[k] + (gh + 1) * 128])
        whg = np.concatenate(cols, axis=1)
        wnq = np.concatenate([w_in[:, EV_OFF['b_q'] + (2 * hh + gg) * 512:EV_OFF['b_q'] + (2 * hh + gg + 1) * 512]
                              for gg in range(2)], axis=1)
        kv = []
        for gg in range(2):
            g = 2 * hh + gg
            for k in ('b_kc', 'b_vc', 'b_ks', 'b_vs', 'b_kw', 'b_vw'):
                kv.append(w_in[:, EV_OFF[k] + g * 128:EV_OFF[k] + (g + 1) * 128])
        wnkv = np.concatenate(kv, axis=1)
        wngt = w_in[:, EV_OFF['b_gate'] + 24 * hh:EV_OFF['b_gate'] + 24 * (hh + 1)]
        wnbg = w_in[:, EV_OFF['b_g'] + 1024 * hh:EV_OFF['b_g'] + 1024 * (hh + 1)]
        lbl = lb_logits[:, 1024 * hh:1024 * (hh + 1)]
        lbl = lbl.reshape(2, 8, 128).transpose(2, 1, 0).reshape(128, 16)
        packed.append(dict(whg=c_(whg), wnq=c_(wnq), wnkv=c_(wnkv), wngt=c_(wngt), wnbg=c_(wnbg), lbl=c_(lbl),
                           gain=c_(gain[1024 * hh:1024 * (hh + 1)])))
    for c in range(NCORES):
        b, hh = c // 2, c % 2
        m = dict(packed[hh])
        m.update({"x": c_(x[b]), "nw": c_(nw), "w1k": c_(w1_k), "w1v": c_(w1_v), "w2k": c_(w2_k), "w2v": c_(w2_v),
                  "pekT": c_(pe_k.T), "pevT": c_(pe_v.T), "cb": tabs[hh][0], "cf": tabs[hh][1]})
        in_maps.append(m)
    res = _run(nc, in_maps)
    out = []
    for b in range(4):
        r0, r1 = res[2 * b]["mixT"], res[2 * b + 1]["mixT"]
        out.append(np.concatenate([r0[0:1024], r1[0:1024], r0[1024:2048], r1[1024:2048]], axis=0))
    return out


def kernel(**inputs):
    inp = {k: np.asarray(v) for k, v in inputs.items()}
    x = np.ascontiguousarray(inp['x'], dtype=np.float32)
    for layer in range(4):
        if layer % 2 == 0:
            e = layer // 2
            mixT = run_even_A(e, x, inp['norm_w'][layer], inp['even_w_in'][e], inp['hgrn_lb_logits'],
                              inp['hgrn_norm_w'][e], inp['cmp_pe_k'][e], inp['cmp_w1_k'][e], inp['cmp_w2_k'][e],
                              inp['cmp_pe_v'][e], inp['cmp_w1_v'][e], inp['cmp_w2_v'][e])
            x = run_B(mixT, x, inp['even_w_out'][e])
        else:
            o = layer // 2
            mixT = run_odd_A(x, inp['norm_w'][layer], inp['odd_w_in'][o], inp['rg_conv_w'][o], inp['rg_conv_b'][o],
                             inp['rg_w_a'][o], inp['rg_b_a'][o], inp['rg_w_i'][o], inp['rg_b_i'][o],
                             inp['rg_lambda'][o])
            x = run_B(mixT, x, inp['odd_w_out'][o], inp['final_norm_w'] if layer == 3 else None)
    return x.astype(np.float32)


def emit_even_A(P, C, e_idx, x_src, nw_row, W, out_d):
    ident, identf, cm, eps, cf = C['ident'], C['identf'], C['cm'], C['eps'], C['cf']
    VAL = cf[:, 0:512]
    ADD = cf[:, 512:1024]
    RST = cf[:, 1024:1536]
    with ExitStack() as sa:
        hT = P.sb(sa, "hT", [128, KC, T], BF16)
        hTv = [P.view(hT, "hT%d" % i) for i in range(NT)]
        ss = P.sb(sa, "ss", [128, 4], F32)
        with ExitStack() as s0:
            xt = [P.sb(s0, "xt%d" % i, [128, D], F32) for i in range(2)]
            xn = [P.sb(s0, "xn%d" % i, [128, D], BF16) for i in range(2)]
            nwb = P.sb(s0, "nwb", [128, D], F32)
            ptr = [P.ps(s0, "ptr%d" % i, [128, 4, 128], BF16) for i in range(2)]
            P.dma('sp', nwb[:], nw_row.partition_broadcast(128), (), [nwb], owner=nwb)
            emit_hT(P, s0, x_src, nwb, ident, hT, hTv, xt, xn, ptr, {'eps': eps, 'ss': ss})
            P.end_section()
        with ExitStack() as s1:
            ptb = P.ps(s1, "ptrh", [128, 8, 128], BF16)
            emit_hgrn2(P, s1, e_idx, hT, hTv, ident, cm, RST, eps, (ptb, P.view(ptb)), W['whg'], W['lbl'], W['gain'],
                       out_d, 16)
            P.end_section()
        with ExitStack() as s2:
            ptr = P.ps(s2, "ptrn", [128, 4, 128], BF16)
            emit_nsa(P, s2, hT, hTv, (ident, identf), ptr, VAL, ADD, W['wnq'], W['wnkv'], W['wngt'], W['wnbg'],
                     W['w1k'], W['w1v'], W['w2k'], W['w2v'], W['pekT'], W['pevT'], C['cb_d'], out_d, 4, NT)
            P.end_section()


def emit_odd_A(P, C, x_src, nw_row, W, out_d):
    ident, eps = C['ident'], C['eps']
    NBLK = 10
    NCT = 20
    wx_d, wg_d, wa_d, wi_d, vec_d = W['wx'], W['wg'], W['wa'], W['wi'], W['vec']
    with ExitStack() as st:
        hT = P.sb(st, "hT", [128, KC, T], BF16)
        hTv = [P.view(hT, "hT%d" % i) for i in range(NT)]
        Tb = [P.sb(st, "T%d" % i, [128, T], F32) for i in range(4)]
        Bb = [P.sb(st, "B%d" % i, [128, T], BF16) for i in range(2)]
        xraw = [P.sb(st, "xraw%d" % i, [128, T + 3], F32) for i in range(2)]
        xc = [P.sb(st, "xc%d" % i, [128, T], F32) for i in range(2)]
        mix = [P.sb(st, "mix%d" % i, [128, T], BF16) for i in range(2)]
        wxs = [P.sb(st, "wxs%d" % i, [128, KC, 256], BF16) for i in range(2)]
        wgs = [P.sb(st, "wgs%d" % i, [128, KC, 256], BF16) for i in range(2)]
        was = [P.sb(st, "was%d" % i, [128, 2, 256], BF16) for i in range(2)]
        wis = [P.sb(st, "wis%d" % i, [128, 2, 256], BF16) for i in range(2)]
        vec = P.sb(st, "vec", [128, NCT, 8], F32)
        cl = P.sb(st, "cl", [128, NCT], F32)
        ss = P.sb(st, "ss", [128, 4], F32)
        ptr = [P.ps(st, "ptr%d" % i, [128, 4, 128], BF16) for i in range(2)]
        pj = [P.ps(st, "pj%d" % i, [128, 512], F32) for i in range(2)]
        pg = [P.ps(st, "pg%d" % i, [128, 512], F32) for i in range(2)]

        nwb = xraw[0]
        P.dma('sp', nwb[:, 0:D], nw_row.partition_broadcast(128), (), [nwb], owner=nwb)
        P.dma('sp', vec[:], vec_d.t.ap().rearrange("p (c k) -> p c k", k=8), (), [vec], owner=vec)
        emit_hT(P, st, x_src, nwb, ident, hT, hTv, Tb[0:2], Bb, ptr, {'eps': eps, 'ss': ss})
        P.act(cl[:], vec[:, :, 7], AF.Exp, [vec], [cl], scale=-1.0)
        P.act(cl[:], cl[:], AF.Ln, [cl], [cl], bias=1.0)
        P.ts('dve', cl[:], cl[:], -8.0, None, ALU.mult, None, [cl], [cl])
        for i in range(2):
            P.memset('dve', xraw[i][:, 0:3], 0.0, [xraw[i]])

        def load_w(n):
            P.dma('pool', wxs[n % 2][:], wx_d.t[:, n * 256:(n + 1) * 256].rearrange("(kc p) c -> p kc c", p=128),
                  (), [wxs[n % 2]], owner=wxs[n % 2])
            P.dma('pool', wgs[n % 2][:], wg_d.t[:, n * 256:(n + 1) * 256].rearrange("(kc p) c -> p kc c", p=128),
                  (), [wgs[n % 2]], owner=wgs[n % 2])
            P.dma('pool', was[n % 2][:], wa_d.t[n].rearrange("(dt p) e -> p dt e", p=128), (), [was[n % 2]],
                  owner=was[n % 2])
            P.dma('pool', wis[n % 2][:], wi_d.t[n].rearrange("(dt p) e -> p dt e", p=128), (), [wis[n % 2]],
                  owner=wis[n % 2])

        npj = 0
        load_w(0)
        for n in range(NBLK):
            w_x, w_g, w_a, w_i = wxs[n % 2], wgs[n % 2], was[n % 2], wis[n % 2]
            if n + 1 < NBLK:
                load_w(n + 1)
            for hf in range(2):
                ct = 2 * n + hf
                xr = xraw[hf]
                for tb in range(4):
                    pp = pj[npj % 2]
                    npj += 1
                    for kc in range(KC):
                        P.mm(pp[:], w_x[:, kc, hf * 128:(hf + 1) * 128], hT[:, kc, tb * 512:(tb + 1) * 512],
                             kc == 0, kc == KC - 1, [w_x] + hTv[4 * tb:4 * tb + 4], [pp])
                    P.cp('act', xr[:, 3 + tb * 512:3 + (tb + 1) * 512], pp[:], [pp], [xr])
                c = xc[hf]
                P.ts('dve', c[:], xr[:, 3:3 + T], vec[:, ct, 3:4], vec[:, ct, 4:5], ALU.mult, ALU.add,
                     [xr, vec], [c])
                for j in range(3):
                    P.stt(c[:], xr[:, j:j + T], vec[:, ct, j:j + 1], c[:], ALU.mult, ALU.add, [xr, vec, c], [c])
                P.cp('pool', Bb[hf][:], c[:], [c], [Bb[hf]])
            for eh in range(2):
                ct = 2 * n + eh
                R, I, S, G = Tb
                for (dst, wsb, bcol) in ((R, w_a, 5), (I, w_i, 6)):
                    for tb in range(4):
                        pp = pg[npj % 2]
                        npj += 1
                        for dt_ in range(2):
                            P.mm(pp[:], wsb[:, dt_, eh * 128:(eh + 1) * 128],
                                 Bb[dt_][:, tb * 512:(tb + 1) * 512], dt_ == 0, dt_ == 1, [wsb, Bb[dt_]], [pp])
                        P.act(dst[:, tb * 512:(tb + 1) * 512], pp[:], AF.Sigmoid, [pp, vec], [dst],
                              bias=vec[:, ct, bcol:bcol + 1])
                P.act(R[:], R[:], AF.Exp, [R, cl], [R], scale=cl[:, ct:ct + 1])
                P.tt('pool', S[:], R[:], R[:], ALU.mult, [R], [S])
                P.act(S[:], S[:], AF.Sqrt, [S], [S], scale=-1.0, bias=1.0)
                P.tt('dve', I[:], I[:], xc[eh][:], ALU.mult, [I, xc[eh]], [I])
                P.tt('dve', I[:], I[:], S[:], ALU.mult, [I, S], [I])
                P.scan(S[:], R[:], I[:], 0.0, [R, I], [S])
                for tb in range(4):
                    pp = pj[npj % 2]
                    npj += 1
                    for kc in range(KC):
                        P.mm(pp[:], w_g[:, kc, eh * 128:(eh + 1) * 128], hT[:, kc, tb * 512:(tb + 1) * 512],
                             kc == 0, kc == KC - 1, [w_g] + hTv[4 * tb:4 * tb + 4], [pp])
                    P.act(G[:, tb * 512:(tb + 1) * 512], pp[:], AF.Silu, [pp], [G])
                m = mix[eh]
                P.tt('dve', m[:], S[:], G[:], ALU.mult, [S, G], [m])
                P.dma('sp', out_d.t[ct * 128:(ct + 1) * 128, :], m[:], [m], (), owner=m)
        P.end_section()


def emit_B(P, C, Cdim, mix_d, w_d, x_src, x_dst, final, fn_row=None, out_d=None):
    KCC = Cdim // 128
    NB = 512
    TB = 1024
    eps = C['eps']
    with ExitStack() as st:
        mix = P.sb(st, "bmix", [128, KCC, TB], BF16)
        ws = [P.sb(st, "bws%d" % i, [128, KCC, NB], BF16) for i in range(2)]
        xb = [P.sb(st, "bx%d" % i, [128, NB], F32) for i in range(4)]
        pp = [P.ps(st, "bpp%d" % i, [128, 512], F32) for i in range(4)]
        k = 0
        nw = 0
        for th in range(2):
            P.dma('sp', mix[:], mix_d.t[:, th * TB:(th + 1) * TB].rearrange("(kc p) t -> p kc t", p=128),
                  (), [mix], owner=mix)
            for nb in range(D // NB):
                w = ws[nw % 2]
                nw += 1
                P.dma('pool', w[:], w_d.t[:, nb * NB:(nb + 1) * NB].rearrange("(kc p) c -> p kc c", p=128),
                      (), [w], owner=w)
                for tt in range(TB // 128):
                    rows = slice(th * TB + tt * 128, th * TB + (tt + 1) * 128)
                    cols = slice(nb * NB, (nb + 1) * NB)
                    p_ = pp[k % 4]
                    xv = xb[k % 4]
                    k += 1
                    P.dma('sp', xv[:], x_src.t[rows, cols], (), [xv], owner=xv)
                    for kc in range(KCC):
                        P.mm(p_[:, 0:NB], mix[:, kc, tt * 128:(tt + 1) * 128], w[:, kc, :], kc == 0, kc == KCC - 1,
                             [mix, w], [p_])
                    P.tt('dve', xv[:], xv[:], p_[:, 0:NB], ALU.add, [xv, p_], [xv])
                    P.dma('sp', x_dst.t[rows, cols], xv[:], [xv], (), owner=xv)
        P.end_section()
    if final:
        with ExitStack() as st:
            xt = [P.sb(st, "fx%d" % i, [128, D], F32) for i in range(2)]
            junk = P.sb(st, "fjunk", [128, D], BF16)
            fnb = P.sb(st, "fnb", [128, D], F32)
            ss = P.sb(st, "fss", [128, 4], F32)
            P.dma('sp', fnb[:], fn_row.partition_broadcast(128), (), [fnb], owner=fnb)
            for tt in range(NT):
                xv = xt[tt % 2]
                s0 = ss[:, 2 * (tt % 2):2 * (tt % 2) + 1]
                s1 = ss[:, 2 * (tt % 2) + 1:2 * (tt % 2) + 2]
                P.dma('sp', xv[:], x_dst.t[tt * 128:(tt + 1) * 128, :], (), [xv], owner=xv)
                P.act(junk[:], xv[:], AF.Square, [xv], [junk, ss], accum_out=s0)
                P.act(s1, s0, AF.Sqrt, [ss, eps], [ss], scale=1.0 / D, bias=eps[:])
                P.op('dve', lambda e, s1=s1: e.reciprocal(out=s1, in_=s1), [ss], [ss])
                P.stt(xv[:], xv[:], s1, fnb[:], ALU.mult, ALU.mult, [xv, ss, fnb], [xv])
                P.dma('sp', out_d.t[tt * 128:(tt + 1) * 128, :], xv[:], [xv], (), owner=xv)
            P.end_section()


def build_fused(nlayers=4):
    nc = bass.Bass("TRN2", target_bir_lowering=False)
    with ExitStack() as st:
        P = Prog(nc, st)
        x_d = P.dram("x", [T, D], F32, "ExternalInput")
        nw_d = P.dram("nw", [4, D], F32, "ExternalInput")
        fnw_d = P.dram("fnw", [D], F32, "ExternalInput")
        cb_d = P.dram("cb", [128, CB_W], F32, "ExternalInput")
        cf_d = P.dram("cf", [128, 1536], F32, "ExternalInput")
        lbl_d = P.dram("lbl", [128, 32], F32, "ExternalInput")
        WE, WO = [], []
        for e in range(2):
            W = {'lbl': lbl_d}
            for nm, shp in (('whg', [D, 16 * 512]), ('gain', [2048]), ('wnq', [D, 2048]), ('wnkv', [D, 3072]),
                            ('wngt', [D, 48]), ('wnbg', [D, 2048]), ('w1k', [4096, 128]), ('w1v', [4096, 128]),
                            ('w2k', [128, 128]), ('w2v', [128, 128]), ('pekT', [128, 32]), ('pevT', [128, 32]),
                            ('wout', [4096, D])):
                W[nm] = P.dram("e%d_%s" % (e, nm), shp, F32, "ExternalInput")
            WE.append(W)
        for o in range(2):
            W = {}
            for nm, shp in (('wx', [D, 2560]), ('wg', [D, 2560]), ('wa', [10, 256, 256]), ('wi', [10, 256, 256]),
                            ('vec', [128, 160]), ('wout', [2560, D])):
                W[nm] = P.dram("o%d_%s" % (o, nm), shp, F32, "ExternalInput")
            WO.append(W)
        xs_d = P.dram("xs_scratch", [T, D], F32, "Internal")
        mixE_d = P.dram("mixE_scratch", [4096, T], BF16, "Internal")
        mixO_d = P.dram("mixO_scratch", [2560, T], BF16, "Internal")
        out_d = P.dram("out", [T, D], F32, "ExternalOutput")

        ident = make_ident(P, st)
        idf2 = P.sb(st, "idf2", [128, 128], F32)
        cm = P.sb(st, "cm", [128, 128], F32)
        identf = P.sb(st, "identf", [128, 128], F32)
        P.op('pool', lambda e: e.iota(idf2[:], pattern=[[1, 128]], base=0, channel_multiplier=-1,
                                      allow_small_or_imprecise_dtypes=True), (), [idf2])
        P.ts('dve', cm[:], idf2[:], 0.0, None, ALU.is_ge, None, [idf2], [cm])
        P.ts('dve', identf[:], idf2[:], 0.0, None, ALU.is_equal, None, [idf2], [identf])
        eps = P.sb(st, "eps", [128, 1], F32)
        P.memset('dve', eps[:], EPS, [eps])
        cf = P.sb(st, "cf", [128, 1536], F32)
        P.dma('sp', cf[:], cf_d.t.ap(), (), [cf], owner=cf)
        C = dict(ident=ident, identf=identf, cm=cm, eps=eps, cf=cf, cb_d=cb_d)
        P.end_section()

        for layer in range(nlayers):
            x_src = x_d if layer == 0 else xs_d
            nw_row = nw_d.t[layer]
            last = layer == nlayers - 1
            if layer % 2 == 0:
                W = WE[layer // 2]
                emit_even_A(P, C, layer // 2, x_src, nw_row, W, mixE_d)
                emit_B(P, C, 4096, mixE_d, W['wout'], x_src, xs_d, last, fnw_d.t.ap(), out_d)
            else:
                W = WO[layer // 2]
                emit_odd_A(P, C, x_src, nw_row, W, mixO_d)
                emit_B(P, C, 2560, mixO_d, W['wout'], x_src, xs_d, last, fnw_d.t.ap(), out_d)
        P.barrier()
        P.emit()
    return nc


def pack_inputs(inp):
    cbt, cft = even_const_tables()
    shared = {"nw": c_(inp['norm_w']), "fnw": c_(inp['final_norm_w']), "cb": cbt, "cf": cft}
    lg = inp['hgrn_lb_logits']
    shared["lbl"] = c_(lg.reshape(2, 16, 128).transpose(2, 1, 0).reshape(128, 32))
    for e in range(2):
        w_in = inp['even_w_in'][e]
        cols = []
        for gh in range(16):
            for k in ('a_q', 'a_f', 'a_i', 'a_g'):
                cols.append(w_in[:, EV_OFF[k] + gh * 128:EV_OFF[k] + (gh + 1) * 128])
        shared["e%d_whg" % e] = np.concatenate(cols, axis=1)
        kv = []
        for g in range(4):
            for k in ('b_kc', 'b_vc', 'b_ks', 'b_vs', 'b_kw', 'b_vw'):
                kv.append(w_in[:, EV_OFF[k] + g * 128:EV_OFF[k] + (g + 1) * 128])
        shared["e%d_wnkv" % e] = np.concatenate(kv, axis=1)
        shared["e%d_wnq" % e] = c_(w_in[:, EV_OFF['b_q']:EV_OFF['b_q'] + 2048])
        shared["e%d_wngt" % e] = c_(w_in[:, EV_OFF['b_gate']:EV_OFF['b_gate'] + 48])
        shared["e%d_wnbg" % e] = c_(w_in[:, EV_OFF['b_g']:EV_OFF['b_g'] + 2048])
        shared["e%d_gain" % e] = c_(inp['hgrn_norm_w'][e])
        shared["e%d_w1k" % e] = c_(inp['cmp_w1_k'][e])
        shared["e%d_w1v" % e] = c_(inp['cmp_w1_v'][e])
        shared["e%d_w2k" % e] = c_(inp['cmp_w2_k'][e])
        shared["e%d_w2v" % e] = c_(inp['cmp_w2_v'][e])
        shared["e%d_pekT" % e] = c_(inp['cmp_pe_k'][e].T)
        shared["e%d_pevT" % e] = c_(inp['cmp_pe_v'][e].T)
        shared["e%d_wout" % e] = c_(inp['even_w_out'][e])
    for o in range(2):
        w_in = inp['odd_w_in'][o]
        shared["o%d_wx" % o] = c_(w_in[:, 0:D_RNN])
        shared["o%d_wg" % o] = c_(w_in[:, D_RNN:2 * D_RNN])
        shared["o%d_wa" % o] = c_(inp['rg_w_a'][o])
        shared["o%d_wi" % o] = c_(inp['rg_w_i'][o])
        cw = inp['rg_conv_w'][o]
        vec = np.stack([cw[0], cw[1], cw[2], cw[3], inp['rg_conv_b'][o], inp['rg_b_a'][o], inp['rg_b_i'][o],
                        inp['rg_lambda'][o]], axis=-1)
        shared["o%d_vec" % o] = c_(vec.reshape(20, 128, 8).transpose(1, 0, 2).reshape(128, 160).astype(np.float32))
        shared["o%d_wout" % o] = c_(inp['odd_w_out'][o])
    return shared


def kernel(**inputs):
    inp = {k: np.asarray(v) for k, v in inputs.items()}
    shared = pack_inputs(inp)
    x = np.ascontiguousarray(inp['x'], dtype=np.float32)
    nc = _get("fused", build_fused)
    in_maps = []
    for c in range(NCORES):
        m = dict(shared)
        m["x"] = c_(x[c // 2])
        in_maps.append(m)
    res = run_bass_kernel_spmd(nc, in_maps, core_ids=list(range(NCORES)))
    out = np.stack([res.results[2 * b]["out"] for b in range(4)], axis=0)
    return out.astype(np.float32)
```

```python
import numpy as np
from contextlib import ExitStack
import concourse.bass as bass
import concourse.mybir as mybir
from concourse.alu_op_type import AluOpType as ALU
from concourse.bass_utils import run_bass_kernel_spmd

AF = mybir.ActivationFunctionType
F32 = mybir.dt.float32
BF16 = mybir.dt.bfloat16

D = 2048
T = 2048
NT = T // 128
KC = D // 128
D_RNN = 2560
EPS = 1e-6
NCORES = 8


class Buf:
    def __init__(self, t, name=""):
        self.t = t
        self.name = name
        self.w = None
        self.r = []
        self.dsem = None
        self.dcnt = 0
        self.psum = False

    def __getitem__(self, idx):
        return self.t[idx]


ENGS = ('pe', 'act', 'dve', 'pool', 'sp')


class Prog:
    def __init__(self, nc, stack):
        self.nc = nc
        self.stack = stack
        self.ops = {k: [] for k in ENGS}
        self.esem = {k: stack.enter_context(nc.semaphore("es_" + k)) for k in ENGS}
        self.ecnt = {k: 0 for k in ENGS}
        self.waited = {k: {} for k in ENGS}
        self.nsem = 0
        self.nops = 0
        self.dtoks = {}
        self.keep = []
        self.uid = 0
        self.sem_pool = {'sw': [], 'hw': []}
        self.sec_bufs = []

    def sb(self, stack, name, shape, dtype):
        self.uid += 1
        return Buf(stack.enter_context(self.nc.sbuf_tensor("s%d_%s" % (self.uid, name), list(shape), dtype)), name)

    def ps(self, stack, name, shape, dtype=F32):
        self.uid += 1
        b = Buf(stack.enter_context(self.nc.psum_tensor("p%d_%s" % (self.uid, name), list(shape), dtype)), name)
        b.psum = True
        return b

    def dram(self, name, shape, dtype, kind):
        return Buf(self.nc.dram_tensor(name, list(shape), dtype, kind=kind), name)

    def view(self, b, name=""):
        return Buf(b.t, name or b.name)

    def _dsem(self, b, eng):
        kind = 'sw' if eng == 'pool' else 'hw'
        if b.dsem is None:
            b.dsem = {}
            b.dcnt = {}
        if kind not in b.dsem:
            if self.sem_pool[kind]:
                b.dsem[kind], b.dcnt[kind] = self.sem_pool[kind].pop()
            else:
                sem = self.stack.enter_context(self.nc.semaphore("ds%s_%d" % (kind, self.nsem)))
                self.keep.append(sem)
                self.nsem += 1
                b.dsem[kind], b.dcnt[kind] = sem, 0
            self.sec_bufs.append((b, kind))
        return kind

    def end_section(self):
        self.barrier()
        for b, kind in self.sec_bufs:
            self.sem_pool[kind].append((b.dsem.pop(kind), b.dcnt.pop(kind)))
        self.sec_bufs = []

    def _deps(self, eng, reads, writes, waw=True):
        need = {}
        own = self.esem[eng]

        def add(d, raw):
            if d is None:
                return
            s, v = d
            if s is own and (eng == 'pe' or raw == 'war'):
                return
            k = id(s)
            if k not in need or need[k][1] < v:
                need[k] = (s, v)

        for b in reads:
            add(b.w, 'raw')
            if b.psum:
                for d in b.r:
                    if d[0] is not own:
                        add(d, 'rar')
        for b in writes:
            if waw:
                add(b.w, 'waw')
            for d in b.r:
                add(d, 'war')
        out = []
        wd = self.waited[eng]
        for k, (s, v) in need.items():
            if wd.get(k, 0) >= v:
                continue
            wd[k] = v
            out.append((s, v))
        return out

    def _commit(self, reads, writes, tok):
        for b in reads:
            b.r.append(tok)
            if len(b.r) > 64:
                b.r = b.r[-64:] if False else b.r
        for b in writes:
            b.w = tok
            b.r = []

    def op(self, eng, fn, reads=(), writes=(), waw=True):
        waits = self._deps(eng, reads, writes, waw)
        self.ecnt[eng] += 1
        tok = (self.esem[eng], self.ecnt[eng])
        self.ops[eng].append((waits, fn, self.esem[eng], 1))
        self._commit(reads, writes, tok)
        self.nops += 1

    def dma(self, eng, out_ap, in_ap, reads=(), writes=(), owner=None):
        waits = self._deps(eng, reads, writes)
        kind = self._dsem(owner, eng)
        sem = owner.dsem[kind]
        owner.dcnt[kind] += 16
        tok = (sem, owner.dcnt[kind])
        self.dtoks[id(sem)] = tok

        def fn(e):
            return e.dma_start(out=out_ap, in_=in_ap)
        self.ops[eng].append((waits, fn, sem, 16))
        self._commit(reads, writes, tok)
        self.nops += 1

    def barrier(self):
        toks = [(self.esem[k], self.ecnt[k]) for k in ENGS if self.ecnt[k] > 0]
        toks += list(self.dtoks.values())
        for k in ENGS:
            waits = []
            wd = self.waited[k]
            for (s, v) in toks:
                if s is self.esem[k]:
                    continue
                if wd.get(id(s), 0) >= v:
                    continue
                wd[id(s)] = v
                waits.append((s, v))
            if waits:
                self.ops[k].append((waits, None, None, 0))

    def wait_all(self, eng, bufs):
        waits = self._deps(eng, bufs, bufs)
        self.ops[eng].append((waits, None, None, 0))

    def mm(self, out, lhsT, rhs, start, stop, reads, writes):
        self.op('pe', lambda e: e.matmul(out, lhsT=lhsT, rhs=rhs, start=start, stop=stop),
                reads, writes)

    def tr(self, out, in_, ident, reads, writes):
        self.op('pe', lambda e: e.transpose(out=out, in_=in_, identity=ident), reads, writes)

    def act(self, out, in_, func, reads, writes, waw=True, **kw):
        self.op('act', lambda e: e.activation(out=out, in_=in_, func=func, **kw), reads, writes, waw)

    def tt(self, eng, out, in0, in1, op, reads, writes, waw=True):
        self.op(eng, lambda e: e.tensor_tensor(out=out, in0=in0, in1=in1, op=op), reads, writes, waw)

    def ts(self, eng, out, in0, s1, s2, op0, op1, reads, writes, waw=True):
        if op1 is None:
            self.op(eng, lambda e: e.tensor_scalar(out=out, in0=in0, scalar1=s1, scalar2=None, op0=op0),
                    reads, writes, waw)
        else:
            self.op(eng, lambda e: e.tensor_scalar(out=out, in0=in0, scalar1=s1, scalar2=s2, op0=op0, op1=op1),
                    reads, writes, waw)

    def stt(self, out, in0, scalar, in1, op0, op1, reads, writes, waw=True):
        self.op('dve', lambda e: e.scalar_tensor_tensor(out=out, in0=in0, scalar=scalar, in1=in1,
                                                         op0=op0, op1=op1), reads, writes, waw)

    def cp(self, eng, out, in_, reads, writes, waw=True):
        if eng == 'act':
            self.op('act', lambda e: e.copy(out=out, in_=in_), reads, writes, waw)
        else:
            self.op(eng, lambda e: e.tensor_copy(out=out, in_=in_), reads, writes, waw)

    def memset(self, eng, ap, val, writes):
        self.op(eng, lambda e: e.memset(ap, val), (), writes)

    def scan(self, out, d0, d1, init, reads, writes):
        self.op('dve', lambda e: e.tensor_tensor_scan(out=out, data0=d0, data1=d1, initial=init,
                                                       op0=ALU.mult, op1=ALU.add), reads, writes)

    def emit(self):
        nc = self.nc
        ops = self.ops

        def replay(k, e):
            for waits, fn, sem, inc in ops[k]:
                for s, v in waits:
                    e.wait_ge(s, v)
                if fn is not None:
                    fn(e).then_inc(sem, inc)

        with nc.Block() as block:
            @block.tensor
            def _(e):
                replay('pe', e)

            @block.scalar
            def _(e):
                replay('act', e)

            @block.vector
            def _(e):
                replay('dve', e)

            @block.gpsimd
            def _(e):
                replay('pool', e)

            @block.sync
            def _(e):
                replay('sp', e)


def make_ident(P, st):
    idf = P.sb(st, "idf", [128, 128], F32)
    ident = P.sb(st, "ident", [128, 128], BF16)
    P.op('pool', lambda e: e.iota(idf[:], pattern=[[1, 128]], base=0, channel_multiplier=-1,
                                  allow_small_or_imprecise_dtypes=True), (), [idf])
    P.ts('dve', ident[:], idf[:], 0.0, None, ALU.is_equal, None, [idf], [ident])
    return ident


def emit_hT(P, st, x_d, nwb, ident, hT, hTv, xt, xn, ptr, small):
    eps, ss = small['eps'], small['ss']
    for tt in range(NT):
        xb = xt[tt % 2]
        xnb = xn[tt % 2]
        P.dma('sp', xb[:, 0:D], x_d.t[tt * 128:(tt + 1) * 128, :], (), [xb], owner=xb)
        s0 = ss[:, 2 * (tt % 2):2 * (tt % 2) + 1]
        s1 = ss[:, 2 * (tt % 2) + 1:2 * (tt % 2) + 2]
        P.act(xnb[:, 0:D], xb[:, 0:D], AF.Square, [xb], [xnb, ss], accum_out=s0)
        P.act(s1, s0, AF.Sqrt, [ss, eps], [ss], scale=1.0 / D, bias=eps[:])
        P.op('dve', lambda e, s1=s1: e.reciprocal(out=s1, in_=s1), [ss], [ss])
        P.stt(xnb[:, 0:D], xb[:, 0:D], s1, nwb[:, 0:D], ALU.mult, ALU.mult, [xb, ss, nwb], [xnb])
        for q in range(4):
            pt = ptr[q % 2]
            for j in range(4):
                kc = q * 4 + j
                P.tr(pt[:, j, :], xnb[:, kc * 128:(kc + 1) * 128], ident[:], [xnb, ident], [pt])
            eng = 'act' if q % 2 == 0 else 'dve'
            P.cp(eng, hT[:, q * 4:(q + 1) * 4, tt * 128:(tt + 1) * 128], pt[:, :, :], [pt], [hTv[tt]])


_CACHE = {}


def _get(name, fn):
    if name not in _CACHE:
        _CACHE[name] = fn()
    return _CACHE[name]


def c_(a):
    return np.ascontiguousarray(a)


BIG = 30000.0
SCALE = 128 ** -0.5


def _bf16_split(a):
    import ml_dtypes
    a = np.asarray(a, np.float32)
    hi = a.astype(ml_dtypes.bfloat16).astype(np.float32)
    lo = (a - hi).astype(ml_dtypes.bfloat16).astype(np.float32)
    return hi, lo


def even_const_tables():
    tb = {}
    s_l = np.arange(128, dtype=np.float32)
    La = np.zeros((128, 16, 128), np.float32)
    for m in range(16):
        dj = -m
        La[0, m] = s_l
        La[2, m] = s_l
        La[1, m] = 64.0 * (2 * dj - 1)
        La[3, m] = 64.0 * (2 * dj - 1)
    tb['La'] = La.reshape(128, 16 * 128)
    n = np.arange(127, dtype=np.float32)
    Lc = np.zeros((128, 16, 127), np.float32)
    for i in range(16):
        Lc[0, i] = 16.0 * (n - 8 * i)
        Lc[2, i] = 16.0 * (n - 8 * i)
        Lc[1, i] = -33.0
        Lc[3, i] = -33.0
    tb['Lc'] = Lc.reshape(128, 16 * 127)
    slopes = (2.0 ** (-8.0 * np.arange(1, 17) / 16)).astype(np.float32)
    Ra = np.zeros((128, 4, 4, 128), np.float32)
    for gg in range(4):
        g = gg
        for h in range(4):
            hi, lo = _bf16_split(slopes[4 * g + h])
            Ra[0, gg, h] = hi
            Ra[1, gg, h] = hi
            Ra[2, gg, h] = lo
            Ra[3, gg, h] = lo
    tb['Ra'] = Ra.reshape(128, 4 * 512)
    Sel = np.zeros((128, 16, 127), np.float32)
    for i in range(16):
        k = np.clip(np.arange(127) - 8 * i + 64, 0, 127)
        Sel[k, i, np.arange(127)] = 1.0
    tb['Sel'] = Sel.reshape(128, 16 * 127)
    kk = np.arange(128)[:, None]
    tl = np.arange(128)[None, :]
    G = np.where(16 * (kk - 64) + 31 > tl, -BIG, 0.0).astype(np.float32)
    tb['G'] = np.tile(G[:, None, :], (1, 4, 1)).reshape(128, 512)
    tric = np.where(kk > tl, -BIG, 0.0).astype(np.float32)
    trib = np.where(kk <= tl, -BIG, 0.0).astype(np.float32)
    tb['TRIc'] = np.tile(tric[:, None, :], (1, 4, 1)).reshape(128, 512)
    tb['TRIb'] = np.tile(trib[:, None, :], (1, 4, 1)).reshape(128, 512)
    E = np.zeros((128, 16, 128), np.float32)
    for jt in range(16):
        E[2 * jt, jt, 0:64] = 1.0
        E[2 * jt + 1, jt, 64:128] = 1.0
    tb['E'] = E.reshape(128, 16 * 128)
    VAL = np.zeros((128, 16, 32), np.float32)
    ADD = np.zeros((128, 16, 32), np.float32)
    for i in range(16):
        for t_l in range(128):
            cur = (128 * i + t_l) // 64
            for j in range(32):
                if j == 0:
                    ADD[t_l, i, j] = 3e30
                elif j == cur:
                    ADD[t_l, i, j] = 2e30
                elif j == cur - 1:
                    ADD[t_l, i, j] = 1e30
                elif j <= cur:
                    VAL[t_l, i, j] = 1.0
                else:
                    ADD[t_l, i, j] = -1e30
    tb['VAL'] = VAL.reshape(128, 512)
    tb['ADD'] = ADD.reshape(128, 512)
    c_start = np.arange(127) * 16
    s_start = np.arange(32) * 64
    ov = ((c_start[:, None] <= s_start[None, :] + 63) & (c_start[:, None] + 31 >= s_start[None, :]))
    OV = np.zeros((128, 32), np.float32)
    OV[:127] = ov
    tb['OV'] = OV
    rst = np.ones((128, 512), np.float32)
    rst[:, 0::128] = 0.0
    tb['RST'] = rst
    cat = np.concatenate([tb[k] for k in ('La', 'Lc', 'Ra', 'Sel', 'G', 'TRIc', 'TRIb', 'E', 'OV')], axis=1)
    f32 = np.concatenate([tb[k] for k in ('VAL', 'ADD', 'RST')], axis=1)
    return c_(cat.astype(np.float32)), c_(f32)


CB_OFF = {}
_o = 0
for _k, _w in (('La', 2048), ('Lc', 16 * 127), ('Ra', 2048), ('Sel', 16 * 127), ('G', 512), ('TRIc', 512),
               ('TRIb', 512), ('E', 2048), ('OV', 32)):
    CB_OFF[_k] = _o
    _o += _w
CB_W = _o


def emit_hgrn2(P, s1, e_idx, hT, hTv, ident, cm, RST, eps, ptr, whg_d, lbl_d, gain_d, out_d, nh):
    f32b = lambda n: P.sb(s1, n, [128, 512], F32)
    bfb = lambda n: P.sb(s1, n, [128, 512], BF16)
    whg = [P.sb(s1, "whg%d" % i, [128, KC, 512], BF16) for i in range(2)]
    lbl = P.sb(s1, "lbl", [128, 16, 2], F32)
    lbt = P.sb(s1, "lbt", [128, 16, 4], F32)
    lb = P.sb(s1, "lb", [128, 16], F32)
    oml = P.sb(s1, "oml", [128, 16], F32)
    gainb = P.sb(s1, "gainb", [128, 2048], F32)
    qs, t1, t2, gg_, kk, b128, dd, d3 = [f32b("hg_f%d" % i) for i in range(8)]
    EA1, EA2, EB1, EB2, EQB, EKA, E3, E5 = [f32b("hg_e%d" % i) for i in range(8)]
    QA, QB, KA, KB, QOB, KOA, QP, KH = [bfb("hg_o%d" % i) for i in range(8)]
    vtok = [P.sb(s1, "vtok%d" % i, [128, 4, 128], BF16) for i in range(2)]
    ag = [P.sb(s1, "ag%d" % i, [128, 4, 128], F32) for i in range(2)]
    at4 = P.sb(s1, "at4", [128, 4, 128], BF16)
    kht4 = P.sb(s1, "kht4", [128, 4, 128], BF16)
    S5 = P.sb(s1, "S5", [128, 5, 128], F32)
    Sb5 = P.sb(s1, "Sb5", [128, 5, 128], BF16)
    ob4 = P.sb(s1, "ob4", [128, 4, 128], F32)
    junk4 = P.sb(s1, "junk4", [128, 4, 128], F32)
    yb4 = P.sb(s1, "yb4", [128, 4, 128], BF16)
    sst4 = P.sb(s1, "sst4", [128, 8], F32)
    yaT = [P.sb(s1, "yaT%d" % i, [128, T], BF16) for i in range(2)]
    pq = [P.ps(s1, "pq%d" % i, [128, 512], F32) for i in range(2)]
    pv = [P.ps(s1, "pv%d" % i, [128, 512], F32) for i in range(2)]
    patt = P.ps(s1, "patt", [128, 4, 128], F32)
    po = P.ps(s1, "po", [128, 4, 128], F32)
    pS = P.ps(s1, "pS", [128, 4, 128], F32)

    P.dma('sp', lbl[:], lbl_d.t.ap().rearrange("p (h e) -> p h e", e=2), (), [lbl], owner=lbl)
    P.dma('sp', gainb[:], gain_d.t.ap().partition_broadcast(128), (), [gainb], owner=gainb)
    P.tt('dve', lbt[:, :, 0], lbl[:, :, 0], lbl[:, :, 1], ALU.max, [lbl], [lbt])
    P.tt('dve', lbt[:, :, 1], lbl[:, :, 0], lbt[:, :, 0], ALU.subtract, [lbl, lbt], [lbt])
    P.tt('dve', lbt[:, :, 2], lbl[:, :, 1], lbt[:, :, 0], ALU.subtract, [lbl, lbt], [lbt])
    P.act(lbt[:, :, 1:3], lbt[:, :, 1:3], AF.Exp, [lbt], [lbt])
    P.tt('dve', lbt[:, :, 3], lbt[:, :, 1], lbt[:, :, 2], ALU.add, [lbt], [lbt])
    P.op('dve', lambda e: e.reciprocal(out=lbt[:, :, 3], in_=lbt[:, :, 3]), [lbt], [lbt])
    P.tt('dve', lbt[:, :, 1], lbt[:, :, 1], lbt[:, :, 3], ALU.mult, [lbt], [lbt])
    P.tt('dve', lbt[:, :, 2], lbt[:, :, 2], lbt[:, :, 3], ALU.mult, [lbt], [lbt])
    if e_idx == 0:
        P.tt('dve', lb[:], lbt[:, :, 1], lbt[:, :, 1], ALU.subtract, [lbt], [lb])
    else:
        P.tt('dve', lbt[:, :, 3], lbt[:, :, 1], lbt[:, :, 2], ALU.add, [lbt], [lbt])
        P.tt('dve', lb[:], lbt[:, :, 3], lbt[:, :, 1], ALU.subtract, [lbt], [lb])
    P.ts('dve', oml[:], lb[:], -1.0, 1.0, ALU.mult, ALU.add, [lb], [oml])
    for b_ in (EA1, EA2, EB1, EB2, EQB, EKA):
        P.memset('pool', b_[:], 0.0, [b_])

    def v4(buf):
        return buf[:].rearrange("p (j c) -> p j c", c=128)

    def load_w(hd):
        w = whg[hd % 2]
        P.dma('pool', w[:], whg_d.t[:, hd * 512:(hd + 1) * 512].rearrange("(kc p) c -> p kc c", p=128),
              (), [w], owner=w)

    pt0, pt1 = ptr
    steps = [(hd, tb) for hd in range(nh) for tb in range(4)]

    def Pm(k):
        hd, tb = steps[k]
        w = whg[hd % 2]
        if tb == 0 and hd + 1 < nh:
            load_w(hd + 1)
        hts = hTv[4 * tb:4 * tb + 4]
        tsl = slice(tb * 512, (tb + 1) * 512)
        for kc in range(KC):
            P.mm(pq[0][:], w[:, kc, 0:128], hT[:, kc, tsl], kc == 0, kc == KC - 1, [w] + hts, [pq[0]])
        for kc in range(KC):
            P.mm(pq[1][:], w[:, kc, 128:256], hT[:, kc, tsl], kc == 0, kc == KC - 1, [w] + hts, [pq[1]])
        for j in range(4):
            pvb = pv[j // 2]
            for kc in range(KC):
                P.mm(pvb[:, (j % 2) * 256:(j % 2 + 1) * 256], hT[:, kc, tb * 512 + j * 128:tb * 512 + (j + 1) * 128],
                     w[:, kc, 256:512], kc == 0, kc == KC - 1, [w, hts[j]], [pvb])

    def Pe(k):
        vt, agk = vtok[k % 2], ag[k % 2]
        P.act(qs[:], pq[0][:], AF.Silu, [pq[0]], [qs])
        P.act(t1[:], pq[1][:], AF.Exp, [pq[1]], [t1], scale=-1.0)
        for j in range(4):
            pvb = pv[j // 2]
            o0 = (j % 2) * 256
            P.cp('dve', vt[:, j, :], pvb[:, o0:o0 + 128], [pvb], [vt])
            P.act(agk[:, j, :], pvb[:, o0 + 128:o0 + 256], AF.Silu, [pvb], [agk])

    def E(k):
        hd, tb = steps[k]
        P.act(t1[:], t1[:], AF.Ln, [t1], [t1], bias=1.0)
        P.act(t1[:], t1[:], AF.Exp, [t1], [t1], scale=-1.0)
        P.ts('dve', t2[:], t1[:], oml[:, hd:hd + 1], lb[:, hd:hd + 1], ALU.mult, ALU.add,
             [t1, oml, lb], [t2])
        P.ts('dve', t2[:], t2[:], 1e-30, None, ALU.max, None, [t2], [t2])
        P.act(gg_[:], t2[:], AF.Ln, [t2], [gg_])
        P.ts('dve', kk[:], t2[:], -1.0, 1.0, ALU.mult, ALU.add, [t2], [kk])
        P.scan(b128[:], RST, gg_[:], 0.0, [gg_], [b128])
        bv = v4(b128)
        dv_ = v4(dd)
        P.tt('dve', dv_[:, :, 0:64], bv[:, :, 0:64], bv[:, :, 31:32].to_broadcast([128, 4, 64]),
             ALU.subtract, [b128], [dd])
        P.tt('dve', dv_[:, :, 64:128], bv[:, :, 64:128], bv[:, :, 95:96].to_broadcast([128, 4, 64]),
             ALU.subtract, [b128], [dd])
        P.act(v4(EA1)[:, :, 0:64], dv_[:, :, 0:64], AF.Exp, [dd], [EA1])
        P.act(v4(EA2)[:, :, 0:64], dv_[:, :, 0:64], AF.Exp, [dd], [EA2], scale=-1.0)
        P.act(v4(EB1)[:, :, 64:128], dv_[:, :, 64:128], AF.Exp, [dd], [EB1])
        P.act(v4(EB2)[:, :, 64:128], dv_[:, :, 64:128], AF.Exp, [dd], [EB2], scale=-1.0)
        d3v = v4(d3)
        P.tt('dve', d3v[:, :, :], bv[:, :, :], bv[:, :, 63:64].to_broadcast([128, 4, 128]),
             ALU.subtract, [b128], [d3])
        P.act(v4(EQB)[:, :, 64:128], d3v[:, :, 64:128], AF.Exp, [d3], [EQB])
        P.act(v4(EKA)[:, :, 0:64], d3v[:, :, 0:64], AF.Exp, [d3], [EKA], scale=-1.0)
        P.act(E3[:], b128[:], AF.Exp, [b128], [E3])
        P.tt('dve', d3v[:, :, :], bv[:, :, :], bv[:, :, 127:128].to_broadcast([128, 4, 128]),
             ALU.subtract, [b128], [d3])
        P.act(E5[:], d3[:], AF.Exp, [d3], [E5], scale=-1.0)
        P.tt('dve', QA[:], qs[:], EA1[:], ALU.mult, [qs, EA1], [QA])
        P.tt('dve', QB[:], qs[:], EB1[:], ALU.mult, [qs, EB1], [QB])
        P.tt('dve', KA[:], kk[:], EA2[:], ALU.mult, [kk, EA2], [KA])
        P.tt('dve', KB[:], kk[:], EB2[:], ALU.mult, [kk, EB2], [KB])
        P.tt('pool', QOB[:], qs[:], EQB[:], ALU.mult, [qs, EQB], [QOB])
        P.tt('pool', KOA[:], kk[:], EKA[:], ALU.mult, [kk, EKA], [KOA])
        P.tt('pool', QP[:], qs[:], E3[:], ALU.mult, [qs, E3], [QP])
        P.tt('pool', KH[:], kk[:], E5[:], ALU.mult, [kk, E5], [KH])

    def L(k):
        hd, tb = steps[k]
        vt, agk = vtok[k % 2], ag[k % 2]
        yT = yaT[hd % 2]
        if tb == 0:
            P.memset('dve', S5[:, 0, :], 0.0, [S5])
            P.memset('dve', Sb5[:, 0, :], 0.0, [Sb5])
        cs_ = [slice(j * 128, (j + 1) * 128) for j in range(4)]
        for j in range(4):
            P.mm(patt[:, j, :], KA[:, cs_[j]], QA[:, cs_[j]], True, False, [KA, QA], [patt])
            P.mm(patt[:, j, :], KB[:, cs_[j]], QB[:, cs_[j]], False, False, [KB, QB], [patt])
            P.mm(patt[:, j, :], KOA[:, cs_[j]], QOB[:, cs_[j]], False, True, [KOA, QOB], [patt])
        P.tt('dve', at4[:], patt[:], cm[:].unsqueeze(1).to_broadcast([128, 4, 128]), ALU.mult, [patt, cm], [at4])
        for j in range(4):
            P.tr(pt0[:, j, :], KH[:, cs_[j]], ident[:], [KH, ident], [pt0])
        P.cp('act', kht4[:], pt0[:, 0:4, :], [pt0], [kht4])
        for j in range(4):
            P.mm(pS[:, j, :], kht4[:, j, :], vt[:, j, :], True, True, [kht4, vt], [pS])
        for j in range(4):
            P.stt(S5[:, j + 1, :], S5[:, j, :], E3[:, (j + 1) * 128 - 1:(j + 1) * 128], pS[:, j, :],
                  ALU.mult, ALU.add, [S5, E3, pS], [S5])
        P.cp('act', Sb5[:, 1:5, :], S5[:, 1:5, :], [S5], [Sb5])
        for j in range(4):
            P.mm(po[:, j, :], at4[:, j, :], vt[:, j, :], True, False, [at4, vt], [po])
            P.mm(po[:, j, :], QP[:, cs_[j]], Sb5[:, j, :], False, True, [QP, Sb5], [po])
        P.cp('pool', S5[:, 0, :], S5[:, 4, :], [S5], [S5])
        P.cp('pool', Sb5[:, 0, :], Sb5[:, 4, :], [Sb5], [Sb5])
        P.cp('dve', ob4[:], po[:], [po], [ob4])
        P.act(junk4[:], ob4[:], AF.Square, [ob4], [junk4])
        P.op('dve', lambda e: e.tensor_reduce(out=sst4[:, 0:4], in_=junk4[:], axis=mybir.AxisListType.X,
                                              op=ALU.add), [junk4], [sst4])
        P.act(sst4[:, 4:8], sst4[:, 0:4], AF.Sqrt, [sst4, eps], [sst4], scale=1.0 / 128, bias=eps[:])
        P.op('dve', lambda e: e.reciprocal(out=sst4[:, 4:8], in_=sst4[:, 4:8]), [sst4], [sst4])
        P.tt('dve', ob4[:], ob4[:], sst4[:, 4:8].unsqueeze(2).to_broadcast([128, 4, 128]), ALU.mult,
             [ob4, sst4], [ob4])
        P.tt('dve', ob4[:], ob4[:],
             gainb[:, hd * 128:(hd + 1) * 128].unsqueeze(1).to_broadcast([128, 4, 128]), ALU.mult,
             [ob4, gainb], [ob4])
        P.tt('dve', yb4[:], ob4[:], agk[:], ALU.mult, [ob4, agk], [yb4])
        for j in range(4):
            P.tr(pt1[:, 4 + j, :], yb4[:, j, :], ident[:], [yb4, ident], [pt1])
        P.cp('act', yT[:, tb * 512:(tb + 1) * 512].rearrange("p (j c) -> p j c", c=128), pt1[:, 4:8, :], [pt1], [yT])
        if tb == 3:
            P.dma('sp', out_d.t[hd * 128:(hd + 1) * 128, :], yT[:], [yT], (), owner=yT)

    load_w(0)
    Pm(0)
    Pe(0)
    for k in range(len(steps)):
        if k + 1 < len(steps):
            Pm(k + 1)
        E(k)
        if k + 1 < len(steps):
            Pe(k + 1)
        L(k)


def emit_nsa(P, s2, hT, hTv, ident, ptr, VAL, ADD, wnq_d, wnkv_d, wngt_d, wnbg_d, w1k_d, w1v_d,
             w2k_d, w2v_d, pek_d, pev_d, cb_d, out_d, ng=4, nqt=NT):
    cf_reads = []
    cb = P.sb(s2, "cb", [128, CB_W], BF16)
    P.dma('pool', cb[:], cb_d.t.ap(), (), [cb], owner=cb)

    def CB(name, lo, hi):
        return cb[:, CB_OFF[name] + lo:CB_OFF[name] + hi]

    wbuf = P.sb(s2, "wbuf", [128, KC, 768], BF16)
    wgt = P.sb(s2, "wgt", [128, KC, 12], BF16)
    w1k = P.sb(s2, "w1k", [128, 32, 128], BF16)
    w1v = P.sb(s2, "w1v", [128, 32, 128], BF16)
    w2k = P.sb(s2, "w2k", [128, 128], BF16)
    w2v = P.sb(s2, "w2v", [128, 128], BF16)
    pek = P.sb(s2, "pek", [128, 32], BF16)
    pev = P.sb(s2, "pev", [128, 32], BF16)
    P.dma('pool', w1k[:], w1k_d.t.ap().rearrange("(j d) o -> d j o", d=128), (), [w1k], owner=w1k)
    P.dma('pool', w1v[:], w1v_d.t.ap().rearrange("(j d) o -> d j o", d=128), (), [w1v], owner=w1v)
    P.dma('pool', w2k[:], w2k_d.t.ap(), (), [w2k], owner=w2k)
    P.dma('pool', w2v[:], w2v_d.t.ap(), (), [w2v], owner=w2v)
    P.dma('pool', pek[:], pek_d.t.ap(), (), [pek], owner=pek)
    P.dma('pool', pev[:], pev_d.t.ap(), (), [pev], owner=pev)

    qT = P.sb(s2, "qT", [128, 4, T], BF16)
    kcT = P.sb(s2, "kcT", [128, T], BF16)
    vcT = P.sb(s2, "vcT", [128, T], BF16)
    ksT = P.sb(s2, "ksT", [128, T], BF16)
    kwT = P.sb(s2, "kwT", [128, T], BF16)
    vsA = P.sb(s2, "vsA", [128, NT, 129], BF16)
    vwA = P.sb(s2, "vwA", [128, NT, 129], BF16)
    hidk = P.sb(s2, "hidk", [128, 127], BF16)
    hidv = P.sb(s2, "hidv", [128, 127], BF16)
    cbias = P.sb(s2, "cbias", [128, 2], F32)
    KcT = P.sb(s2, "KcT", [128, 127], BF16)
    VcA = P.sb(s2, "VcA", [128, 161], BF16)
    Rsel = P.sb(s2, "Rsel", [128, 4, 128], BF16)
    Pt = [P.sb(s2, "Pt%d" % i, [128, 4, 128], BF16) for i in range(3)]
    gts = P.sb(s2, "gts", [128, 12], F32)
    sgate = P.sb(s2, "sgate", [128, 512], F32)
    zz = P.sb(s2, "zz", [128, 3, 4], F32)
    cs = P.sb(s2, "cs", [128, 3, 4], F32)
    acc = P.sb(s2, "acc", [128, 4, 128], F32)
    imp = P.sb(s2, "imp", [128, 32], F32)
    score = P.sb(s2, "score", [128, 32], F32)
    work = P.sb(s2, "work", [128, 32], F32)
    m8 = P.sb(s2, "m8", [128, 16], F32)
    nm = P.sb(s2, "nm", [128, 32], F32)
    ybf = P.sb(s2, "ybf", [128, 512], BF16)
    yTt = [P.sb(s2, "yTt%d" % i, [128, 4, 128], BF16) for i in range(2)]
    psA = [P.ps(s2, "psA%d" % i, [128, 4, 128], F32) for i in range(2)]
    ident, identf = ident
    psO4 = P.ps(s2, "psO", [128, 4, 512], F32)
    psO = [psO4] * 4
    ocp = P.sb(s2, "ocp", [128, 4, 161], F32)
    impm = P.sb(s2, "impm", [128, 4, 32], F32)
    tmpo = P.sb(s2, "tmpo", [128, 4, 128], F32)
    psM = P.ps(s2, "psM", [128, 512], F32)
    psG = psM

    P.memset('dve', vsA[:, :, 128:129], 1.0, [vsA])
    P.memset('dve', vwA[:, :, 128:129], 1.0, [vwA])
    P.memset('dve', Rsel[:], 0.0, [Rsel])
    P.memset('dve', VcA[:], 0.0, [VcA])
    P.memset('dve', VcA[:, 128:129], 1.0, [VcA])
    P.cp('dve', VcA[:, 129:161], CB('OV', 0, 32), [cb], [VcA])

    def Oh(h):
        return psO4[:, h, 0:256]

    npa = [0]
    npt = [1]

    def nextA():
        p = psA[npa[0] % 2]
        npa[0] += 1
        return p

    for gg in range(ng):
        Ra = CB('Ra', gg * 512, (gg + 1) * 512)
        P.dma('pool', wbuf[:, :, 0:512], wnq_d.t[:, gg * 512:(gg + 1) * 512].rearrange("(kc p) c -> p kc c", p=128),
              (), [wbuf], owner=wbuf)
        for h in range(4):
            for tb in range(4):
                pp = nextA()
                ppf = pp[:].rearrange("p a b -> p (a b)")
                for kc in range(KC):
                    P.mm(ppf, wbuf[:, kc, h * 128:(h + 1) * 128], hT[:, kc, tb * 512:(tb + 1) * 512],
                         kc == 0, kc == KC - 1, [wbuf] + hTv[4 * tb:4 * tb + 4], [pp])
                P.act(qT[:, h, tb * 512:(tb + 1) * 512], ppf, AF.Copy, [pp], [qT], scale=SCALE)
        P.dma('pool', wbuf[:], wnkv_d.t[:, gg * 768:(gg + 1) * 768].rearrange("(kc p) c -> p kc c", p=128),
              (), [wbuf], owner=wbuf)
        P.dma('pool', wgt[:], wngt_d.t[:, gg * 12:(gg + 1) * 12].rearrange("(kc p) c -> p kc c", p=128),
              (), [wgt], owner=wgt)
        for (dst, col) in ((kcT, 0), (vcT, 128), (ksT, 256), (kwT, 384)):
            for tb in range(4):
                pp = nextA()
                ppf = pp[:].rearrange("p a b -> p (a b)")
                for kc in range(KC):
                    P.mm(ppf, wbuf[:, kc, col:col + 128], hT[:, kc, tb * 512:(tb + 1) * 512],
                         kc == 0, kc == KC - 1, [wbuf] + hTv[4 * tb:4 * tb + 4], [pp])
                P.cp('dve' if tb % 2 else 'act', dst[:, tb * 512:(tb + 1) * 512], ppf, [pp], [dst])
        for tt in range(NT):
            pp = nextA()
            ppf = pp[:].rearrange("p a b -> p (a b)")
            for kc in range(KC):
                P.mm(ppf[:, 0:256], hT[:, kc, tt * 128:(tt + 1) * 128], wbuf[:, kc, 512:768],
                     kc == 0, kc == KC - 1, [wbuf, hTv[tt]], [pp])
            eng = 'dve' if tt % 2 else 'act'
            P.cp(eng, vsA[:, tt, 0:128], ppf[:, 0:128], [pp], [vsA])
            P.cp(eng, vwA[:, tt, 0:128], ppf[:, 128:256], [pp], [vwA])
        P.dma('pool', wbuf[:, :, 0:512], wnbg_d.t[:, gg * 512:(gg + 1) * 512].rearrange("(kc p) c -> p kc c", p=128),
              (), [wbuf], owner=wbuf)
        for (src, w1, w2, pe, hid, col) in ((kcT, w1k, w2k, pek, hidk, 0), (vcT, w1v, w2v, pev, hidv, 1)):
            srcv = src[:].rearrange("p (n r) -> p n r", r=16)
            for j in range(32):
                P.mm(psM[:, 0:127], w1[:, j, :], srcv[:, j // 16:j // 16 + 127, j % 16], j == 0, j == 31,
                     [w1, src], [psM])
            for j in range(32):
                P.mm(psM[:, 128:129], w1[:, j, :], pe[:, j:j + 1], j == 0, j == 31, [w1, pe], [psM])
            P.cp('dve', cbias[:, col:col + 1], psM[:, 128:129], [psM], [cbias])
            P.act(hid[:], psM[:, 0:127], AF.Silu, [psM, cbias], [hid], bias=cbias[:, col:col + 1])
        P.mm(psM[:, 256:383], w2k[:], hidk[:], True, True, [w2k, hidk], [psM])
        P.cp('dve', KcT[:], psM[:, 256:383], [psM], [KcT])
        P.mm(psM[0:127, 384:512], hidv[:], w2v[:], True, True, [hidv, w2v], [psM])
        P.cp('dve', VcA[0:127, 0:128], psM[0:127, 384:512], [psM], [VcA])

        for i in range(nqt):
            isl = slice(i * 128, (i + 1) * 128)
            qrhs = qT[:, :, isl]
            for kc in range(KC):
                P.mm(psG[:], hT[:, kc, isl], wbuf[:, kc, 0:512], kc == 0, kc == KC - 1, [wbuf, hTv[i]], [psG])
            P.act(sgate[:], psG[:], AF.Silu, [psG], [sgate])
            for kc in range(KC):
                P.mm(psM[:, 0:12], hT[:, kc, isl], wgt[:, kc, :], kc == 0, kc == KC - 1, [wgt, hTv[i]], [psM])
            P.act(gts[:], psM[:, 0:12], AF.Sigmoid, [psM], [gts])
            gv = gts[:].rearrange("p (h j) -> p j h", j=3)

            def finish(br, width):
                P.cp('dve', ocp[:, :, 0:width], psO4[:, :, 0:width], [psO4], [ocp])
                P.ts('dve', zz[:, br, :], ocp[:, :, 128], 1e-37, None, ALU.max, None, [ocp], [zz])
                P.op('dve', lambda e, br=br: e.reciprocal(out=zz[:, br, :], in_=zz[:, br, :]), [zz], [zz])
                if br == 0:
                    P.tt('dve', impm[:], ocp[:, :, 129:161], zz[:, 0, :].unsqueeze(2).to_broadcast([128, 4, 32]),
                         ALU.mult, [ocp, zz], [impm])
                    P.op('dve', lambda e: e.tensor_reduce(out=imp[:], in_=impm[:].rearrange("p h j -> p j h"),
                                                          axis=mybir.AxisListType.X, op=ALU.add), [impm], [imp])
                P.tt('dve', cs[:, br, :], zz[:, br, :], gv[:, br, :], ALU.mult, [zz, gts], [cs])
                if br == 0:
                    P.tt('dve', acc[:], ocp[:, :, 0:128], cs[:, br, :].unsqueeze(2).to_broadcast([128, 4, 128]),
                         ALU.mult, [ocp, cs], [acc])
                else:
                    P.tt('dve', tmpo[:], ocp[:, :, 0:128], cs[:, br, :].unsqueeze(2).to_broadcast([128, 4, 128]),
                         ALU.mult, [ocp, cs], [tmpo])
                    P.tt('dve', acc[:], acc[:], tmpo[:], ALU.add, [acc, tmpo], [acc])

            pp = nextA()
            P.mm(pp[0:127, :, :], KcT[:], qrhs, True, False, [KcT, qT], [pp])
            P.mm(pp[0:127, :, :], CB('Lc', i * 127, (i + 1) * 127), Ra.rearrange("p (a b) -> p a b", b=128),
                 False, False, [cb], [pp])
            P.mm(pp[0:127, :, :], CB('Sel', i * 127, (i + 1) * 127),
                 CB('G', 0, 512).rearrange("p (a b) -> p a b", b=128), False, True, [cb], [pp])
            pt_ = Pt[0]
            P.act(pt_[0:127, :, :], pp[0:127, :, :], AF.Exp, [pp], [pt_])
            for h in range(4):
                P.mm(Oh(h)[:, 0:161], pt_[0:127, h, :], VcA[0:127, :], True, True, [pt_, VcA], [psO[h]])
            finish(0, 161)
            P.tt('dve', score[:], imp[:], VAL[:, i * 32:(i + 1) * 32], ALU.mult, [imp], [score])
            P.tt('dve', score[:], score[:], ADD[:, i * 32:(i + 1) * 32], ALU.add, [score], [score])
            P.op('dve', lambda e: e.max(out=m8[:, 0:8], in_=score[:]), [score], [m8])
            P.op('dve', lambda e: e.match_replace(out=work[:], in_to_replace=m8[:, 0:8], in_values=score[:],
                                                  imm_value=-3e38), [score, m8], [work])
            P.op('dve', lambda e: e.max(out=m8[:, 8:16], in_=work[:]), [work], [m8])
            P.ts('dve', nm[:], score[:], m8[:, 15:16], -BIG, ALU.is_lt, ALU.mult, [score, m8], [nm])
            P.tr(psM[0:32, 128:256], nm[:], identf[:], [nm, identf], [psM])
            P.cp('dve', Rsel[0:32, :, :], psM[0:32, 128:256].rearrange("p (a b) -> p a b", a=1).to_broadcast([32, 4, 128]),
                 [psM], [Rsel])

            for br, (kT_, vA_) in ((2, (kwT, vwA)), (1, (ksT, vsA))):
                jlo = 0 if br == 1 else max(0, i - 4)

                def emit_qk(jt, br=br, kT_=kT_, jlo=jlo):
                    dj = jt - i
                    pp = nextA()
                    extra = []
                    if br == 1:
                        extra.append((CB('E', jt * 128, (jt + 1) * 128), Rsel[:], [cb, Rsel]))
                    if jt == i:
                        extra.append((ident[:], CB('TRIc', 0, 512).rearrange("p (a b) -> p a b", b=128), [ident, cb]))
                    if br == 2 and jt == i - 4:
                        extra.append((ident[:], CB('TRIb', 0, 512).rearrange("p (a b) -> p a b", b=128), [ident, cb]))
                    P.mm(pp[:], kT_[:, jt * 128:(jt + 1) * 128], qrhs, True, False, [kT_, qT], [pp])
                    P.mm(pp[:], CB('La', (-dj) * 128, (-dj + 1) * 128), Ra.rearrange("p (a b) -> p a b", b=128),
                         False, len(extra) == 0, [cb], [pp])
                    for xi, (l_, r_, rd) in enumerate(extra):
                        P.mm(pp[:], l_, r_, False, xi == len(extra) - 1, rd, [pp])
                    return pp

                def emit_pv(jt, pp, vA_=vA_, jlo=jlo):
                    pt_ = Pt[npt[0] % 3]
                    npt[0] += 1
                    P.act(pt_[:], pp[:], AF.Exp, [pp], [pt_])
                    for h in range(4):
                        P.mm(Oh(h)[:, 0:129], pt_[:, h, :], vA_[:, jt, :], jt == jlo, jt == i,
                             [pt_, vA_], [psO[h]])

                pend = None
                for jt in range(jlo, i + 1):
                    pp = emit_qk(jt)
                    if pend is not None:
                        emit_pv(*pend)
                    pend = (jt, pp)
                emit_pv(*pend)
                finish(br, 129)
            P.tt('dve', ybf[:], acc[:].rearrange("p a b -> p (a b)"), sgate[:], ALU.mult, [acc, sgate], [ybf])
            ptb = ptr
            for h in range(4):
                P.tr(ptb[:, h, :], ybf[:, h * 128:(h + 1) * 128], ident[:], [ybf, ident], [ptb])
            yt = yTt[i % 2]
            P.cp('act', yt[:], ptb[:], [ptb], [yt])
            P.dma('sp', out_d.t[2048 + gg * 512:2048 + (gg + 1) * 512, isl].rearrange("(h c) t -> c h t", c=128),
                  yt[:], [yt], (), owner=yt)


EV_OFF = dict(a_q=0, a_f=2048, a_i=4096, a_g=6144, b_q=8192, b_kc=10240, b_vc=10752, b_ks=11264, b_vs=11776,
              b_kw=12288, b_vw=12800, b_gate=13312, b_g=13360)


def emit_even_A(P, C, e_idx, x_src, nw_row, W, out_d):
    ident, identf, cm, eps, cf = C['ident'], C['identf'], C['cm'], C['eps'], C['cf']
    VAL = cf[:, 0:512]
    ADD = cf[:, 512:1024]
    RST = cf[:, 1024:1536]
    with ExitStack() as sa:
        hT = P.sb(sa, "hT", [128, KC, T], BF16)
        hTv = [P.view(hT, "hT%d" % i) for i in range(NT)]
        ss = P.sb(sa, "ss", [128, 4], F32)
        with ExitStack() as s0:
            xt = [P.sb(s0, "xt%d" % i, [128, D], F32) for i in range(2)]
            xn = [P.sb(s0, "xn%d" % i, [128, D], BF16) for i in range(2)]
            nwb = P.sb(s0, "nwb", [128, D], F32)
            ptr = [P.ps(s0, "ptr%d" % i, [128, 4, 128], BF16) for i in range(2)]
            P.dma('sp', nwb[:], nw_row.partition_broadcast(128), (), [nwb], owner=nwb)
            emit_hT(P, s0, x_src, nwb, ident, hT, hTv, xt, xn, ptr, {'eps': eps, 'ss': ss})
            P.end_section()
        with ExitStack() as s1:
            ptb = P.ps(s1, "ptrh", [128, 8, 128], BF16)
            emit_hgrn2(P, s1, e_idx, hT, hTv, ident, cm, RST, eps, (ptb, P.view(ptb)), W['whg'], W['lbl'], W['gain'],
                       out_d, 16)
            P.end_section()
        with ExitStack() as s2:
            ptr = P.ps(s2, "ptrn", [128, 4, 128], BF16)
            emit_nsa(P, s2, hT, hTv, (ident, identf), ptr, VAL, ADD, W['wnq'], W['wnkv'], W['wngt'], W['wnbg'],
                     W['w1k'], W['w1v'], W['w2k'], W['w2v'], W['pekT'], W['pevT'], C['cb_d'], out_d, 4, NT)
            P.end_section()


def emit_odd_A(P, C, x_src, nw_row, W, out_d):
    ident, eps = C['ident'], C['eps']
    NBLK = 10
    NCT = 20
    wx_d, wg_d, wa_d, wi_d, vec_d = W['wx'], W['wg'], W['wa'], W['wi'], W['vec']
    with ExitStack() as st:
        hT = P.sb(st, "hT", [128, KC, T], BF16)
        hTv = [P.view(hT, "hT%d" % i) for i in range(NT)]
        Tb = [P.sb(st, "T%d" % i, [128, T], F32) for i in range(4)]
        Bb = [P.sb(st, "B%d" % i, [128, T], BF16) for i in range(2)]
        xraw = [P.sb(st, "xraw%d" % i, [128, T + 3], F32) for i in range(2)]
        xc = [P.sb(st, "xc%d" % i, [128, T], F32) for i in range(2)]
        mix = [P.sb(st, "mix%d" % i, [128, T], BF16) for i in range(2)]
        wxs = [P.sb(st, "wxs%d" % i, [128, KC, 256], BF16) for i in range(2)]
        wgs = [P.sb(st, "wgs%d" % i, [128, KC, 256], BF16) for i in range(2)]
        was = [P.sb(st, "was%d" % i, [128, 2, 256], BF16) for i in range(2)]
        wis = [P.sb(st, "wis%d" % i, [128, 2, 256], BF16) for i in range(2)]
        vec = P.sb(st, "vec", [128, NCT, 8], F32)
        cl = P.sb(st, "cl", [128, NCT], F32)
        ss = P.sb(st, "ss", [128, 4], F32)
        ptr = [P.ps(st, "ptr%d" % i, [128, 4, 128], BF16) for i in range(2)]
        pj = [P.ps(st, "pj%d" % i, [128, 512], F32) for i in range(2)]
        pg = [P.ps(st, "pg%d" % i, [128, 512], F32) for i in range(2)]

        nwb = xraw[0]
        P.dma('sp', nwb[:, 0:D], nw_row.partition_broadcast(128), (), [nwb], owner=nwb)
        P.dma('sp', vec[:], vec_d.t.ap().rearrange("p (c k) -> p c k", k=8), (), [vec], owner=vec)
        emit_hT(P, st, x_src, nwb, ident, hT, hTv, Tb[0:2], Bb, ptr, {'eps': eps, 'ss': ss})
        P.act(cl[:], vec[:, :, 7], AF.Exp, [vec], [cl], scale=-1.0)
        P.act(cl[:], cl[:], AF.Ln, [cl], [cl], bias=1.0)
        P.ts('dve', cl[:], cl[:], -8.0, None, ALU.mult, None, [cl], [cl])
        for i in range(2):
            P.memset('dve', xraw[i][:, 0:3], 0.0, [xraw[i]])

        def load_w(n):
            P.dma('pool', wxs[n % 2][:], wx_d.t[:, n * 256:(n + 1) * 256].rearrange("(kc p) c -> p kc c", p=128),
                  (), [wxs[n % 2]], owner=wxs[n % 2])
            P.dma('pool', wgs[n % 2][:], wg_d.t[:, n * 256:(n + 1) * 256].rearrange("(kc p) c -> p kc c", p=128),
                  (), [wgs[n % 2]], owner=wgs[n % 2])
            P.dma('pool', was[n % 2][:], wa_d.t[n].rearrange("(dt p) e -> p dt e", p=128), (), [was[n % 2]],
                  owner=was[n % 2])
            P.dma('pool', wis[n % 2][:], wi_d.t[n].rearrange("(dt p) e -> p dt e", p=128), (), [wis[n % 2]],
                  owner=wis[n % 2])

        npj = 0
        load_w(0)
        for n in range(NBLK):
            w_x, w_g, w_a, w_i = wxs[n % 2], wgs[n % 2], was[n % 2], wis[n % 2]
            if n + 1 < NBLK:
                load_w(n + 1)
            for hf in range(2):
                ct = 2 * n + hf
                xr = xraw[hf]
                for tb in range(4):
                    pp = pj[npj % 2]
                    npj += 1
                    for kc in range(KC):
                        P.mm(pp[:], w_x[:, kc, hf * 128:(hf + 1) * 128], hT[:, kc, tb * 512:(tb + 1) * 512],
                             kc == 0, kc == KC - 1, [w_x] + hTv[4 * tb:4 * tb + 4], [pp])
                    P.cp('act', xr[:, 3 + tb * 512:3 + (tb + 1) * 512], pp[:], [pp], [xr])
                c = xc[hf]
                P.ts('dve', c[:], xr[:, 3:3 + T], vec[:, ct, 3:4], vec[:, ct, 4:5], ALU.mult, ALU.add,
                     [xr, vec], [c])
                for j in range(3):
                    P.stt(c[:], xr[:, j:j + T], vec[:, ct, j:j + 1], c[:], ALU.mult, ALU.add, [xr, vec, c], [c])
                P.cp('pool', Bb[hf][:], c[:], [c], [Bb[hf]])
            for eh in range(2):
                ct = 2 * n + eh
                R, I, S, G = Tb
                for tb in range(4):
                    pp = pj[npj % 2]
                    npj += 1
                    for kc in range(KC):
                        P.mm(pp[:], w_g[:, kc, eh * 128:(eh + 1) * 128], hT[:, kc, tb * 512:(tb + 1) * 512],
                             kc == 0, kc == KC - 1, [w_g] + hTv[4 * tb:4 * tb + 4], [pp])
                    P.act(G[:, tb * 512:(tb + 1) * 512], pp[:], AF.Silu, [pp], [G])
                for (dst, wsb, bcol) in ((R, w_a, 5), (I, w_i, 6)):
                    for tb in range(4):
                        pp = pg[npj % 2]
                        npj += 1
                        for dt_ in range(2):
                            P.mm(pp[:], wsb[:, dt_, eh * 128:(eh + 1) * 128],
                                 Bb[dt_][:, tb * 512:(tb + 1) * 512], dt_ == 0, dt_ == 1, [wsb, Bb[dt_]], [pp])
                        P.act(dst[:, tb * 512:(tb + 1) * 512], pp[:], AF.Sigmoid, [pp, vec], [dst],
                              bias=vec[:, ct, bcol:bcol + 1])
                P.act(R[:], R[:], AF.Exp, [R, cl], [R], scale=cl[:, ct:ct + 1])
                P.tt('pool', S[:], R[:], R[:], ALU.mult, [R], [S])
                P.act(S[:], S[:], AF.Sqrt, [S], [S], scale=-1.0, bias=1.0)
                P.tt('dve', I[:], I[:], xc[eh][:], ALU.mult, [I, xc[eh]], [I])
                P.tt('dve', I[:], I[:], S[:], ALU.mult, [I, S], [I])
                P.scan(S[:], R[:], I[:], 0.0, [R, I], [S])
                m = mix[eh]
                P.tt('dve', m[:], S[:], G[:], ALU.mult, [S, G], [m])
                P.dma('sp', out_d.t[ct * 128:(ct + 1) * 128, :], m[:], [m], (), owner=m)
        P.end_section()


def emit_B(P, C, Cdim, mix_d, w_d, x_src, x_dst, final, fn_row=None, out_d=None):
    KCC = Cdim // 128
    NB = 512
    TB = 1024
    eps = C['eps']
    with ExitStack() as st:
        mix = P.sb(st, "bmix", [128, KCC, TB], BF16)
        ws = [P.sb(st, "bws%d" % i, [128, KCC, NB], BF16) for i in range(2)]
        xb = [P.sb(st, "bx%d" % i, [128, NB], F32) for i in range(4)]
        pp = [P.ps(st, "bpp%d" % i, [128, 512], F32) for i in range(4)]
        k = 0
        nw = 0
        for th in range(2):
            P.dma('sp', mix[:], mix_d.t[:, th * TB:(th + 1) * TB].rearrange("(kc p) t -> p kc t", p=128),
                  (), [mix], owner=mix)
            for nb in range(D // NB):
                w = ws[nw % 2]
                nw += 1
                P.dma('pool', w[:], w_d.t[:, nb * NB:(nb + 1) * NB].rearrange("(kc p) c -> p kc c", p=128),
                      (), [w], owner=w)
                for tt in range(TB // 128):
                    rows = slice(th * TB + tt * 128, th * TB + (tt + 1) * 128)
                    cols = slice(nb * NB, (nb + 1) * NB)
                    p_ = pp[k % 4]
                    xv = xb[k % 4]
                    k += 1
                    P.dma('sp', xv[:], x_src.t[rows, cols], (), [xv], owner=xv)
                    for kc in range(KCC):
                        P.mm(p_[:, 0:NB], mix[:, kc, tt * 128:(tt + 1) * 128], w[:, kc, :], kc == 0, kc == KCC - 1,
                             [mix, w], [p_])
                    P.tt('dve', xv[:], xv[:], p_[:, 0:NB], ALU.add, [xv, p_], [xv])
                    P.dma('sp', x_dst.t[rows, cols], xv[:], [xv], (), owner=xv)
        P.end_section()
    if final:
        with ExitStack() as st:
            xt = [P.sb(st, "fx%d" % i, [128, D], F32) for i in range(2)]
            junk = P.sb(st, "fjunk", [128, D], BF16)
            fnb = P.sb(st, "fnb", [128, D], F32)
            ss = P.sb(st, "fss", [128, 4], F32)
            P.dma('sp', fnb[:], fn_row.partition_broadcast(128), (), [fnb], owner=fnb)
            for tt in range(NT):
                xv = xt[tt % 2]
                s0 = ss[:, 2 * (tt % 2):2 * (tt % 2) + 1]
                s1 = ss[:, 2 * (tt % 2) + 1:2 * (tt % 2) + 2]
                P.dma('sp', xv[:], x_dst.t[tt * 128:(tt + 1) * 128, :], (), [xv], owner=xv)
                P.act(junk[:], xv[:], AF.Square, [xv], [junk, ss], accum_out=s0)
                P.act(s1, s0, AF.Sqrt, [ss, eps], [ss], scale=1.0 / D, bias=eps[:])
                P.op('dve', lambda e, s1=s1: e.reciprocal(out=s1, in_=s1), [ss], [ss])
                P.stt(xv[:], xv[:], s1, fnb[:], ALU.mult, ALU.mult, [xv, ss, fnb], [xv])
                P.dma('sp', out_d.t[tt * 128:(tt + 1) * 128, :], xv[:], [xv], (), owner=xv)
            P.end_section()


def build_fused(nlayers=4):
    nc = bass.Bass("TRN2", target_bir_lowering=False)
    with ExitStack() as st:
        P = Prog(nc, st)
        x_d = P.dram("x", [T, D], F32, "ExternalInput")
        nw_d = P.dram("nw", [4, D], F32, "ExternalInput")
        fnw_d = P.dram("fnw", [D], F32, "ExternalInput")
        cb_d = P.dram("cb", [128, CB_W], F32, "ExternalInput")
        cf_d = P.dram("cf", [128, 1536], F32, "ExternalInput")
        lbl_d = P.dram("lbl", [128, 32], F32, "ExternalInput")
        WE, WO = [], []
        for e in range(2):
            W = {'lbl': lbl_d}
            for nm, shp in (('whg', [D, 16 * 512]), ('gain', [2048]), ('wnq', [D, 2048]), ('wnkv', [D, 3072]),
                            ('wngt', [D, 48]), ('wnbg', [D, 2048]), ('w1k', [4096, 128]), ('w1v', [4096, 128]),
                            ('w2k', [128, 128]), ('w2v', [128, 128]), ('pekT', [128, 32]), ('pevT', [128, 32]),
                            ('wout', [4096, D])):
                W[nm] = P.dram("e%d_%s" % (e, nm), shp, F32, "ExternalInput")
            WE.append(W)
        for o in range(2):
            W = {}
            for nm, shp in (('wx', [D, 2560]), ('wg', [D, 2560]), ('wa', [10, 256, 256]), ('wi', [10, 256, 256]),
                            ('vec', [128, 160]), ('wout', [2560, D])):
                W[nm] = P.dram("o%d_%s" % (o, nm), shp, F32, "ExternalInput")
            WO.append(W)
        xs_d = P.dram("xs_scratch", [T, D], F32, "Internal")
        mixE_d = P.dram("mixE_scratch", [4096, T], BF16, "Internal")
        mixO_d = P.dram("mixO_scratch", [2560, T], BF16, "Internal")
        out_d = P.dram("out", [T, D], F32, "ExternalOutput")

        ident = make_ident(P, st)
        idf2 = P.sb(st, "idf2", [128, 128], F32)
        cm = P.sb(st, "cm", [128, 128], F32)
        identf = P.sb(st, "identf", [128, 128], F32)
        P.op('pool', lambda e: e.iota(idf2[:], pattern=[[1, 128]], base=0, channel_multiplier=-1,
                                      allow_small_or_imprecise_dtypes=True), (), [idf2])
        P.ts('dve', cm[:], idf2[:], 0.0, None, ALU.is_ge, None, [idf2], [cm])
        P.ts('dve', identf[:], idf2[:], 0.0, None, ALU.is_equal, None, [idf2], [identf])
        eps = P.sb(st, "eps", [128, 1], F32)
        P.memset('dve', eps[:], EPS, [eps])
        cf = P.sb(st, "cf", [128, 1536], F32)
        P.dma('sp', cf[:], cf_d.t.ap(), (), [cf], owner=cf)
        C = dict(ident=ident, identf=identf, cm=cm, eps=eps, cf=cf, cb_d=cb_d)
        P.end_section()

        for layer in range(nlayers):
            x_src = x_d if layer == 0 else xs_d
            nw_row = nw_d.t[layer]
            last = layer == nlayers - 1
            if layer % 2 == 0:
                W = WE[layer // 2]
                emit_even_A(P, C, layer // 2, x_src, nw_row, W, mixE_d)
                emit_B(P, C, 4096, mixE_d, W['wout'], x_src, xs_d, last, fnw_d.t.ap(), out_d)
            else:
                W = WO[layer // 2]
                emit_odd_A(P, C, x_src, nw_row, W, mixO_d)
                emit_B(P, C, 2560, mixO_d, W['wout'], x_src, xs_d, last, fnw_d.t.ap(), out_d)
        P.barrier()
        P.emit()
    return nc


def pack_inputs(inp):
    cbt, cft = even_const_tables()
    shared = {"nw": c_(inp['norm_w']), "fnw": c_(inp['final_norm_w']), "cb": cbt, "cf": cft}
    lg = inp['hgrn_lb_logits']
    shared["lbl"] = c_(lg.reshape(2, 16, 128).transpose(2, 1, 0).reshape(128, 32))
    for e in range(2):
        w_in = inp['even_w_in'][e]
        cols = []
        for gh in range(16):
            for k in ('a_q', 'a_f', 'a_i', 'a_g'):
                cols.append(w_in[:, EV_OFF[k] + gh * 128:EV_OFF[k] + (gh + 1) * 128])
        shared["e%d_whg" % e] = np.concatenate(cols, axis=1)
        kv = []
        for g in range(4):
            for k in ('b_kc', 'b_vc', 'b_ks', 'b_kw', 'b_vs', 'b_vw'):
                kv.append(w_in[:, EV_OFF[k] + g * 128:EV_OFF[k] + (g + 1) * 128])
        shared["e%d_wnkv" % e] = np.concatenate(kv, axis=1)
        shared["e%d_wnq" % e] = c_(w_in[:, EV_OFF['b_q']:EV_OFF['b_q'] + 2048])
        shared["e%d_wngt" % e] = c_(w_in[:, EV_OFF['b_gate']:EV_OFF['b_gate'] + 48])
        shared["e%d_wnbg" % e] = c_(w_in[:, EV_OFF['b_g']:EV_OFF['b_g'] + 2048])
        shared["e%d_gain" % e] = c_(inp['hgrn_norm_w'][e])
        shared["e%d_w1k" % e] = c_(inp['cmp_w1_k'][e])
        shared["e%d_w1v" % e] = c_(inp['cmp_w1_v'][e])
        shared["e%d_w2k" % e] = c_(inp['cmp_w2_k'][e])
        shared["e%d_w2v" % e] = c_(inp['cmp_w2_v'][e])
        shared["e%d_pekT" % e] = c_(inp['cmp_pe_k'][e].T)
        shared["e%d_pevT" % e] = c_(inp['cmp_pe_v'][e].T)
        shared["e%d_wout" % e] = c_(inp['even_w_out'][e])
    for o in range(2):
        w_in = inp['odd_w_in'][o]
        shared["o%d_wx" % o] = c_(w_in[:, 0:D_RNN])
        shared["o%d_wg" % o] = c_(w_in[:, D_RNN:2 * D_RNN])
        shared["o%d_wa" % o] = c_(inp['rg_w_a'][o])
        shared["o%d_wi" % o] = c_(inp['rg_w_i'][o])
        cw = inp['rg_conv_w'][o]
        vec = np.stack([cw[0], cw[1], cw[2], cw[3], inp['rg_conv_b'][o], inp['rg_b_a'][o], inp['rg_b_i'][o],
                        inp['rg_lambda'][o]], axis=-1)
        shared["o%d_vec" % o] = c_(vec.reshape(20, 128, 8).transpose(1, 0, 2).reshape(128, 160).astype(np.float32))
        shared["o%d_wout" % o] = c_(inp['odd_w_out'][o])
    return shared


def kernel(**inputs):
    inp = {k: np.asarray(v) for k, v in inputs.items()}
    shared = pack_inputs(inp)
    x = np.ascontiguousarray(inp['x'], dtype=np.float32)
    nc = _get("fused", build_fused)
    in_maps = []
    for c in range(NCORES):
        m = dict(shared)
        m["x"] = c_(x[c // 2])
        in_maps.append(m)
    res = run_bass_kernel_spmd(nc, in_maps, core_ids=list(range(NCORES)))
    out = np.stack([res.results[2 * b]["out"] for b in range(4)], axis=0)
    return out.astype(np.float32)
```

```python
import numpy as np
from contextlib import ExitStack
import concourse.bass as bass
import concourse.mybir as mybir
from concourse.alu_op_type import AluOpType as ALU
from concourse.bass_utils import run_bass_kernel_spmd

AF = mybir.ActivationFunctionType
F32 = mybir.dt.float32
BF16 = mybir.dt.bfloat16

D = 2048
T = 2048
NT = T // 128
KC = D // 128
D_RNN = 2560
EPS = 1e-6
NCORES = 8


class Buf:
    def __init__(self, t, name=""):
        self.t = t
        self.name = name
        self.w = None
        self.r = []
        self.dsem = None
        self.dcnt = 0
        self.psum = False

    def __getitem__(self, idx):
        return self.t[idx]


ENGS = ('pe', 'act', 'dve', 'pool', 'sp')


class Prog:
    def __init__(self, nc, stack):
        self.nc = nc
        self.stack = stack
        self.ops = {k: [] for k in ENGS}
        self.esem = {k: stack.enter_context(nc.semaphore("es_" + k)) for k in ENGS}
        self.ecnt = {k: 0 for k in ENGS}
        self.waited = {k: {} for k in ENGS}
        self.nsem = 0
        self.nops = 0
        self.dtoks = {}
        self.keep = []
        self.uid = 0
        self.sem_pool = {'sw': [], 'hw': []}
        self.sec_bufs = []

    def sb(self, stack, name, shape, dtype):
        self.uid += 1
        return Buf(stack.enter_context(self.nc.sbuf_tensor("s%d_%s" % (self.uid, name), list(shape), dtype)), name)

    def ps(self, stack, name, shape, dtype=F32):
        self.uid += 1
        b = Buf(stack.enter_context(self.nc.psum_tensor("p%d_%s" % (self.uid, name), list(shape), dtype)), name)
        b.psum = True
        return b

    def dram(self, name, shape, dtype, kind):
        return Buf(self.nc.dram_tensor(name, list(shape), dtype, kind=kind), name)

    def view(self, b, name=""):
        return Buf(b.t, name or b.name)

    def _dsem(self, b, eng):
        kind = 'sw' if eng == 'pool' else 'hw'
        if b.dsem is None:
            b.dsem = {}
            b.dcnt = {}
        if kind not in b.dsem:
            if self.sem_pool[kind]:
                b.dsem[kind], b.dcnt[kind] = self.sem_pool[kind].pop()
            else:
                sem = self.stack.enter_context(self.nc.semaphore("ds%s_%d" % (kind, self.nsem)))
                self.keep.append(sem)
                self.nsem += 1
                b.dsem[kind], b.dcnt[kind] = sem, 0
            self.sec_bufs.append((b, kind))
        return kind

    def end_section(self):
        self.barrier()
        for b, kind in self.sec_bufs:
            self.sem_pool[kind].append((b.dsem.pop(kind), b.dcnt.pop(kind)))
        self.sec_bufs = []

    def _deps(self, eng, reads, writes, waw=True):
        need = {}
        own = self.esem[eng]

        def add(d, raw):
            if d is None:
                return
            s, v = d
            if s is own and (eng == 'pe' or raw == 'war'):
                return
            k = id(s)
            if k not in need or need[k][1] < v:
                need[k] = (s, v)

        for b in reads:
            add(b.w, 'raw')
            if b.psum:
                for d in b.r:
                    if d[0] is not own:
                        add(d, 'rar')
        for b in writes:
            if waw:
                add(b.w, 'waw')
            for d in b.r:
                add(d, 'war')
        out = []
        wd = self.waited[eng]
        for k, (s, v) in need.items():
            if wd.get(k, 0) >= v:
                continue
            wd[k] = v
            out.append((s, v))
        return out

    def _commit(self, reads, writes, tok):
        for b in reads:
            b.r.append(tok)
            if len(b.r) > 64:
                b.r = b.r[-64:] if False else b.r
        for b in writes:
            b.w = tok
            b.r = []

    def op(self, eng, fn, reads=(), writes=(), waw=True):
        waits = self._deps(eng, reads, writes, waw)
        self.ecnt[eng] += 1
        tok = (self.esem[eng], self.ecnt[eng])
        self.ops[eng].append((waits, fn, self.esem[eng], 1))
        self._commit(reads, writes, tok)
        self.nops += 1

    def dma(self, eng, out_ap, in_ap, reads=(), writes=(), owner=None):
        waits = self._deps(eng, reads, writes)
        kind = self._dsem(owner, eng)
        sem = owner.dsem[kind]
        owner.dcnt[kind] += 16
        tok = (sem, owner.dcnt[kind])
        self.dtoks[id(sem)] = tok

        def fn(e):
            return e.dma_start(out=out_ap, in_=in_ap)
        self.ops[eng].append((waits, fn, sem, 16))
        self._commit(reads, writes, tok)
        self.nops += 1

    def barrier(self):
        toks = [(self.esem[k], self.ecnt[k]) for k in ENGS if self.ecnt[k] > 0]
        toks += list(self.dtoks.values())
        for k in ENGS:
            waits = []
            wd = self.waited[k]
            for (s, v) in toks:
                if s is self.esem[k]:
                    continue
                if wd.get(id(s), 0) >= v:
                    continue
                wd[id(s)] = v
                waits.append((s, v))
            if waits:
                self.ops[k].append((waits, None, None, 0))

    def wait_all(self, eng, bufs):
        waits = self._deps(eng, bufs, bufs)
        self.ops[eng].append((waits, None, None, 0))

    def mm(self, out, lhsT, rhs, start, stop, reads, writes):
        self.op('pe', lambda e: e.matmul(out, lhsT=lhsT, rhs=rhs, start=start, stop=stop),
                reads, writes)

    def tr(self, out, in_, ident, reads, writes):
        self.op('pe', lambda e: e.transpose(out=out, in_=in_, identity=ident), reads, writes)

    def act(self, out, in_, func, reads, writes, waw=True, **kw):
        self.op('act', lambda e: e.activation(out=out, in_=in_, func=func, **kw), reads, writes, waw)

    def tt(self, eng, out, in0, in1, op, reads, writes, waw=True):
        self.op(eng, lambda e: e.tensor_tensor(out=out, in0=in0, in1=in1, op=op), reads, writes, waw)

    def ts(self, eng, out, in0, s1, s2, op0, op1, reads, writes, waw=True):
        if op1 is None:
            self.op(eng, lambda e: e.tensor_scalar(out=out, in0=in0, scalar1=s1, scalar2=None, op0=op0),
                    reads, writes, waw)
        else:
            self.op(eng, lambda e: e.tensor_scalar(out=out, in0=in0, scalar1=s1, scalar2=s2, op0=op0, op1=op1),
                    reads, writes, waw)

    def stt(self, out, in0, scalar, in1, op0, op1, reads, writes, waw=True):
        self.op('dve', lambda e: e.scalar_tensor_tensor(out=out, in0=in0, scalar=scalar, in1=in1,
                                                         op0=op0, op1=op1), reads, writes, waw)

    def cp(self, eng, out, in_, reads, writes, waw=True):
        if eng == 'act':
            self.op('act', lambda e: e.copy(out=out, in_=in_), reads, writes, waw)
        else:
            self.op(eng, lambda e: e.tensor_copy(out=out, in_=in_), reads, writes, waw)

    def memset(self, eng, ap, val, writes):
        self.op(eng, lambda e: e.memset(ap, val), (), writes)

    def scan(self, out, d0, d1, init, reads, writes):
        self.op('dve', lambda e: e.tensor_tensor_scan(out=out, data0=d0, data1=d1, initial=init,
                                                       op0=ALU.mult, op1=ALU.add), reads, writes)

    def emit(self):
        nc = self.nc
        ops = self.ops

        def replay(k, e):
            for waits, fn, sem, inc in ops[k]:
                for s, v in waits:
                    e.wait_ge(s, v)
                if fn is not None:
                    fn(e).then_inc(sem, inc)

        with nc.Block() as block:
            @block.tensor
            def _(e):
                replay('pe', e)

            @block.scalar
            def _(e):
                replay('act', e)

            @block.vector
            def _(e):
                replay('dve', e)

            @block.gpsimd
            def _(e):
                replay('pool', e)

            @block.sync
            def _(e):
                replay('sp', e)


def make_ident(P, st):
    idf = P.sb(st, "idf", [128, 128], F32)
    ident = P.sb(st, "ident", [128, 128], BF16)
    P.op('pool', lambda e: e.iota(idf[:], pattern=[[1, 128]], base=0, channel_multiplier=-1,
                                  allow_small_or_imprecise_dtypes=True), (), [idf])
    P.ts('dve', ident[:], idf[:], 0.0, None, ALU.is_equal, None, [idf], [ident])
    return ident


def emit_hT(P, st, x_d, nwb, ident, hT, hTv, xt, xn, ptr, small):
    eps, ss = small['eps'], small['ss']
    for tt in range(NT):
        xb = xt[tt % 2]
        xnb = xn[tt % 2]
        P.dma('sp', xb[:, 0:D], x_d.t[tt * 128:(tt + 1) * 128, :], (), [xb], owner=xb)
        s0 = ss[:, 2 * (tt % 2):2 * (tt % 2) + 1]
        s1 = ss[:, 2 * (tt % 2) + 1:2 * (tt % 2) + 2]
        P.act(xnb[:, 0:D], xb[:, 0:D], AF.Square, [xb], [xnb, ss], accum_out=s0)
        P.act(s1, s0, AF.Sqrt, [ss, eps], [ss], scale=1.0 / D, bias=eps[:])
        P.op('dve', lambda e, s1=s1: e.reciprocal(out=s1, in_=s1), [ss], [ss])
        P.stt(xnb[:, 0:D], xb[:, 0:D], s1, nwb[:, 0:D], ALU.mult, ALU.mult, [xb, ss, nwb], [xnb])
        for q in range(4):
            pt = ptr[q % 2]
            for j in range(4):
                kc = q * 4 + j
                P.tr(pt[:, j, :], xnb[:, kc * 128:(kc + 1) * 128], ident[:], [xnb, ident], [pt])
            eng = 'act' if q % 2 == 0 else 'dve'
            P.cp(eng, hT[:, q * 4:(q + 1) * 4, tt * 128:(tt + 1) * 128], pt[:, :, :], [pt], [hTv[tt]])


_CACHE = {}


def _get(name, fn):
    if name not in _CACHE:
        _CACHE[name] = fn()
    return _CACHE[name]


def c_(a):
    return np.ascontiguousarray(a)


BIG = 30000.0
SCALE = 128 ** -0.5


def _bf16_split(a):
    import ml_dtypes
    a = np.asarray(a, np.float32)
    hi = a.astype(ml_dtypes.bfloat16).astype(np.float32)
    lo = (a - hi).astype(ml_dtypes.bfloat16).astype(np.float32)
    return hi, lo


def even_const_tables():
    tb = {}
    s_l = np.arange(128, dtype=np.float32)
    La = np.zeros((128, 16, 128), np.float32)
    for m in range(16):
        dj = -m
        La[0, m] = s_l
        La[2, m] = s_l
        La[1, m] = 64.0 * (2 * dj - 1)
        La[3, m] = 64.0 * (2 * dj - 1)
    tb['La'] = La.reshape(128, 16 * 128)
    n = np.arange(127, dtype=np.float32)
    Lc = np.zeros((128, 16, 127), np.float32)
    for i in range(16):
        Lc[0, i] = 16.0 * (n - 8 * i)
        Lc[2, i] = 16.0 * (n - 8 * i)
        Lc[1, i] = -33.0
        Lc[3, i] = -33.0
    tb['Lc'] = Lc.reshape(128, 16 * 127)
    slopes = (2.0 ** (-8.0 * np.arange(1, 17) / 16)).astype(np.float32)
    Ra = np.zeros((128, 4, 4, 128), np.float32)
    for gg in range(4):
        g = gg
        for h in range(4):
            hi, lo = _bf16_split(slopes[4 * g + h])
            Ra[0, gg, h] = hi
            Ra[1, gg, h] = hi
            Ra[2, gg, h] = lo
            Ra[3, gg, h] = lo
    tb['Ra'] = Ra.reshape(128, 4 * 512)
    Sel = np.zeros((128, 16, 127), np.float32)
    for i in range(16):
        k = np.clip(np.arange(127) - 8 * i + 64, 0, 127)
        Sel[k, i, np.arange(127)] = 1.0
    tb['Sel'] = Sel.reshape(128, 16 * 127)
    kk = np.arange(128)[:, None]
    tl = np.arange(128)[None, :]
    G = np.where(16 * (kk - 64) + 31 > tl, -BIG, 0.0).astype(np.float32)
    tb['G'] = np.tile(G[:, None, :], (1, 4, 1)).reshape(128, 512)
    tric = np.where(kk > tl, -BIG, 0.0).astype(np.float32)
    trib = np.where(kk <= tl, -BIG, 0.0).astype(np.float32)
    tb['TRIc'] = np.tile(tric[:, None, :], (1, 4, 1)).reshape(128, 512)
    tb['TRIb'] = np.tile(trib[:, None, :], (1, 4, 1)).reshape(128, 512)
    E = np.zeros((128, 16, 128), np.float32)
    for jt in range(16):
        E[2 * jt, jt, 0:64] = 1.0
        E[2 * jt + 1, jt, 64:128] = 1.0
    tb['E'] = E.reshape(128, 16 * 128)
    VAL = np.zeros((128, 16, 32), np.float32)
    ADD = np.zeros((128, 16, 32), np.float32)
    for i in range(16):
        for t_l in range(128):
            cur = (128 * i + t_l) // 64
            for j in range(32):
                if j == 0:
                    ADD[t_l, i, j] = 3e30
                elif j == cur:
                    ADD[t_l, i, j] = 2e30
                elif j == cur - 1:
                    ADD[t_l, i, j] = 1e30
                elif j <= cur:
                    VAL[t_l, i, j] = 1.0
                else:
                    ADD[t_l, i, j] = -1e30
    tb['VAL'] = VAL.reshape(128, 512)
    tb['ADD'] = ADD.reshape(128, 512)
    c_start = np.arange(127) * 16
    s_start = np.arange(32) * 64
    ov = ((c_start[:, None] <= s_start[None, :] + 63) & (c_start[:, None] + 31 >= s_start[None, :]))
    OV = np.zeros((128, 32), np.float32)
    OV[:127] = ov
    tb['OV'] = OV
    rst = np.ones((128, 512), np.float32)
    rst[:, 0::128] = 0.0
    tb['RST'] = rst
    cat = np.concatenate([tb[k] for k in ('La', 'Lc', 'Ra', 'Sel', 'G', 'TRIc', 'TRIb', 'E', 'OV')], axis=1)
    f32 = np.concatenate([tb[k] for k in ('VAL', 'ADD', 'RST')], axis=1)
    return c_(cat.astype(np.float32)), c_(f32)


CB_OFF = {}
_o = 0
for _k, _w in (('La', 2048), ('Lc', 16 * 127), ('Ra', 2048), ('Sel', 16 * 127), ('G', 512), ('TRIc', 512),
               ('TRIb', 512), ('E', 2048), ('OV', 32)):
    CB_OFF[_k] = _o
    _o += _w
CB_W = _o


def emit_hgrn2(P, s1, e_idx, hT, hTv, ident, cm, RST, eps, ptr, whg_d, lbl_d, gain_d, out_d, nh):
    f32b = lambda n: P.sb(s1, n, [128, 512], F32)
    bfb = lambda n: P.sb(s1, n, [128, 512], BF16)
    whg = [P.sb(s1, "whg%d" % i, [128, KC, 512], BF16) for i in range(2)]
    lbl = P.sb(s1, "lbl", [128, 16, 2], F32)
    lbt = P.sb(s1, "lbt", [128, 16, 4], F32)
    lb = P.sb(s1, "lb", [128, 16], F32)
    oml = P.sb(s1, "oml", [128, 16], F32)
    gainb = P.sb(s1, "gainb", [128, 2048], F32)
    qs, t1, t2, gg_, kk, b128, dd, d3 = [f32b("hg_f%d" % i) for i in range(8)]
    EA1, EA2, EB1, EB2, EQB, EKA, E3, E5 = [f32b("hg_e%d" % i) for i in range(8)]
    QA, QB, KA, KB, QOB, KOA, QP, KH = [bfb("hg_o%d" % i) for i in range(8)]
    vtok = [P.sb(s1, "vtok%d" % i, [128, 4, 128], BF16) for i in range(2)]
    ag = [P.sb(s1, "ag%d" % i, [128, 4, 128], F32) for i in range(2)]
    at4 = P.sb(s1, "at4", [128, 4, 128], BF16)
    kht4 = P.sb(s1, "kht4", [128, 4, 128], BF16)
    S5 = P.sb(s1, "S5", [128, 5, 128], F32)
    Sb5 = P.sb(s1, "Sb5", [128, 5, 128], BF16)
    ob4 = P.sb(s1, "ob4", [128, 4, 128], F32)
    junk4 = P.sb(s1, "junk4", [128, 4, 128], F32)
    yb4 = P.sb(s1, "yb4", [128, 4, 128], BF16)
    sst4 = P.sb(s1, "sst4", [128, 8], F32)
    yaT = [P.sb(s1, "yaT%d" % i, [128, T], BF16) for i in range(2)]
    pq = [P.ps(s1, "pq%d" % i, [128, 512], F32) for i in range(2)]
    pv = P.ps(s1, "pv", [128, 2, 512], F32)
    patt = P.ps(s1, "patt", [128, 4, 128], F32)
    po = P.ps(s1, "po", [128, 4, 128], F32)
    pS = P.ps(s1, "pS", [128, 4, 128], F32)

    P.dma('sp', lbl[:], lbl_d.t.ap().rearrange("p (h e) -> p h e", e=2), (), [lbl], owner=lbl)
    P.dma('sp', gainb[:], gain_d.t.ap().partition_broadcast(128), (), [gainb], owner=gainb)
    P.tt('dve', lbt[:, :, 0], lbl[:, :, 0], lbl[:, :, 1], ALU.max, [lbl], [lbt])
    P.tt('dve', lbt[:, :, 1], lbl[:, :, 0], lbt[:, :, 0], ALU.subtract, [lbl, lbt], [lbt])
    P.tt('dve', lbt[:, :, 2], lbl[:, :, 1], lbt[:, :, 0], ALU.subtract, [lbl, lbt], [lbt])
    P.act(lbt[:, :, 1:3], lbt[:, :, 1:3], AF.Exp, [lbt], [lbt])
    P.tt('dve', lbt[:, :, 3], lbt[:, :, 1], lbt[:, :, 2], ALU.add, [lbt], [lbt])
    P.op('dve', lambda e: e.reciprocal(out=lbt[:, :, 3], in_=lbt[:, :, 3]), [lbt], [lbt])
    P.tt('dve', lbt[:, :, 1], lbt[:, :, 1], lbt[:, :, 3], ALU.mult, [lbt], [lbt])
    P.tt('dve', lbt[:, :, 2], lbt[:, :, 2], lbt[:, :, 3], ALU.mult, [lbt], [lbt])
    if e_idx == 0:
        P.tt('dve', lb[:], lbt[:, :, 1], lbt[:, :, 1], ALU.subtract, [lbt], [lb])
    else:
        P.tt('dve', lbt[:, :, 3], lbt[:, :, 1], lbt[:, :, 2], ALU.add, [lbt], [lbt])
        P.tt('dve', lb[:], lbt[:, :, 3], lbt[:, :, 1], ALU.subtract, [lbt], [lb])
    P.ts('dve', oml[:], lb[:], -1.0, 1.0, ALU.mult, ALU.add, [lb], [oml])
    for b_ in (EA1, EA2, EB1, EB2, EQB, EKA):
        P.memset('pool', b_[:], 0.0, [b_])

    def v4(buf):
        return buf[:].rearrange("p (j c) -> p j c", c=128)

    def load_w(hd):
        w = whg[hd % 2]
        P.dma('pool', w[:], whg_d.t[:, hd * 512:(hd + 1) * 512].rearrange("(kc p) c -> p kc c", p=128),
              (), [w], owner=w)

    pt0, pt1 = ptr
    steps = [(hd, tb) for hd in range(nh) for tb in range(4)]

    def Pm(k):
        hd, tb = steps[k]
        w = whg[hd % 2]
        if tb == 0 and hd + 1 < nh:
            load_w(hd + 1)
        hts = hTv[4 * tb:4 * tb + 4]
        tsl = slice(tb * 512, (tb + 1) * 512)
        for kc in range(KC):
            P.mm(pq[0][:], w[:, kc, 0:128], hT[:, kc, tsl], kc == 0, kc == KC - 1, [w] + hts, [pq[0]])
        for kc in range(KC):
            P.mm(pq[1][:], w[:, kc, 128:256], hT[:, kc, tsl], kc == 0, kc == KC - 1, [w] + hts, [pq[1]])
        for j in range(4):
            for kc in range(KC):
                P.mm(pv[:, j // 2, (j % 2) * 256:(j % 2 + 1) * 256],
                     hT[:, kc, tb * 512 + j * 128:tb * 512 + (j + 1) * 128],
                     w[:, kc, 256:512], kc == 0, kc == KC - 1, [w, hts[j]], [pv])

    def Pe(k):
        vt, agk = vtok[k % 2], ag[k % 2]
        P.act(qs[:], pq[0][:], AF.Silu, [pq[0]], [qs])
        P.act(t1[:], pq[1][:], AF.Exp, [pq[1]], [t1], scale=-1.0)
        pv4 = pv[:].rearrange("p b (t c) -> p (b t) c", c=256)
        P.cp('dve', vt[:], pv4[:, :, 0:128], [pv], [vt])
        P.act(agk[:], pv4[:, :, 128:256], AF.Silu, [pv], [agk])

    def E(k):
        hd, tb = steps[k]
        P.act(t1[:], t1[:], AF.Ln, [t1], [t1], bias=1.0)
        P.act(t1[:], t1[:], AF.Exp, [t1], [t1], scale=-1.0)
        P.ts('dve', t2[:], t1[:], oml[:, hd:hd + 1], lb[:, hd:hd + 1], ALU.mult, ALU.add,
             [t1, oml, lb], [t2])
        P.ts('dve', t2[:], t2[:], 1e-30, None, ALU.max, None, [t2], [t2])
        P.act(gg_[:], t2[:], AF.Ln, [t2], [gg_])
        P.ts('dve', kk[:], t2[:], -1.0, 1.0, ALU.mult, ALU.add, [t2], [kk])
        P.scan(b128[:], RST, gg_[:], 0.0, [gg_], [b128])
        bv = v4(b128)
        dv_ = v4(dd)
        P.tt('dve', dv_[:, :, 0:64], bv[:, :, 0:64], bv[:, :, 31:32].to_broadcast([128, 4, 64]),
             ALU.subtract, [b128], [dd])
        P.tt('dve', dv_[:, :, 64:128], bv[:, :, 64:128], bv[:, :, 95:96].to_broadcast([128, 4, 64]),
             ALU.subtract, [b128], [dd])
        P.act(v4(EA1)[:, :, 0:64], dv_[:, :, 0:64], AF.Exp, [dd], [EA1])
        P.act(v4(EA2)[:, :, 0:64], dv_[:, :, 0:64], AF.Exp, [dd], [EA2], scale=-1.0)
        P.act(v4(EB1)[:, :, 64:128], dv_[:, :, 64:128], AF.Exp, [dd], [EB1])
        P.act(v4(EB2)[:, :, 64:128], dv_[:, :, 64:128], AF.Exp, [dd], [EB2], scale=-1.0)
        d3v = v4(d3)
        P.tt('dve', d3v[:, :, :], bv[:, :, :], bv[:, :, 63:64].to_broadcast([128, 4, 128]),
             ALU.subtract, [b128], [d3])
        P.act(v4(EQB)[:, :, 64:128], d3v[:, :, 64:128], AF.Exp, [d3], [EQB])
        P.act(v4(EKA)[:, :, 0:64], d3v[:, :, 0:64], AF.Exp, [d3], [EKA], scale=-1.0)
        P.act(E3[:], b128[:], AF.Exp, [b128], [E3])
        P.tt('dve', d3v[:, :, :], bv[:, :, :], bv[:, :, 127:128].to_broadcast([128, 4, 128]),
             ALU.subtract, [b128], [d3])
        P.act(E5[:], d3[:], AF.Exp, [d3], [E5], scale=-1.0)
        P.tt('dve', QA[:], qs[:], EA1[:], ALU.mult, [qs, EA1], [QA])
        P.tt('dve', QB[:], qs[:], EB1[:], ALU.mult, [qs, EB1], [QB])
        P.tt('dve', KA[:], kk[:], EA2[:], ALU.mult, [kk, EA2], [KA])
        P.tt('dve', KB[:], kk[:], EB2[:], ALU.mult, [kk, EB2], [KB])
        P.tt('pool', QOB[:], qs[:], EQB[:], ALU.mult, [qs, EQB], [QOB])
        P.tt('pool', KOA[:], kk[:], EKA[:], ALU.mult, [kk, EKA], [KOA])
        P.tt('pool', QP[:], qs[:], E3[:], ALU.mult, [qs, E3], [QP])
        P.tt('pool', KH[:], kk[:], E5[:], ALU.mult, [kk, E5], [KH])

    def L(k):
        hd, tb = steps[k]
        vt, agk = vtok[k % 2], ag[k % 2]
        yT = yaT[hd % 2]
        if tb == 0:
            P.memset('dve', S5[:, 0, :], 0.0, [S5])
            P.memset('dve', Sb5[:, 0, :], 0.0, [Sb5])
        cs_ = [slice(j * 128, (j + 1) * 128) for j in range(4)]
        for j in range(4):
            P.mm(patt[:, j, :], KA[:, cs_[j]], QA[:, cs_[j]], True, False, [KA, QA], [patt])
            P.mm(patt[:, j, :], KB[:, cs_[j]], QB[:, cs_[j]], False, False, [KB, QB], [patt])
            P.mm(patt[:, j, :], KOA[:, cs_[j]], QOB[:, cs_[j]], False, True, [KOA, QOB], [patt])
        P.tt('dve', at4[:], patt[:], cm[:].unsqueeze(1).to_broadcast([128, 4, 128]), ALU.mult, [patt, cm], [at4])
        for j in range(4):
            P.tr(pt0[:, j, :], KH[:, cs_[j]], ident[:], [KH, ident], [pt0])
        P.cp('act', kht4[:], pt0[:, 0:4, :], [pt0], [kht4])
        for j in range(4):
            P.mm(pS[:, j, :], kht4[:, j, :], vt[:, j, :], True, True, [kht4, vt], [pS])
        for j in range(4):
            P.stt(S5[:, j + 1, :], S5[:, j, :], E3[:, (j + 1) * 128 - 1:(j + 1) * 128], pS[:, j, :],
                  ALU.mult, ALU.add, [S5, E3, pS], [S5])
        P.cp('act', Sb5[:, 1:5, :], S5[:, 1:5, :], [S5], [Sb5])
        for j in range(4):
            P.mm(po[:, j, :], at4[:, j, :], vt[:, j, :], True, False, [at4, vt], [po])
            P.mm(po[:, j, :], QP[:, cs_[j]], Sb5[:, j, :], False, True, [QP, Sb5], [po])
        P.cp('pool', S5[:, 0, :], S5[:, 4, :], [S5], [S5])
        P.cp('pool', Sb5[:, 0, :], Sb5[:, 4, :], [Sb5], [Sb5])
        P.cp('dve', ob4[:], po[:], [po], [ob4])
        P.act(junk4[:], ob4[:], AF.Square, [ob4], [junk4])
        P.op('dve', lambda e: e.tensor_reduce(out=sst4[:, 0:4], in_=junk4[:], axis=mybir.AxisListType.X,
                                              op=ALU.add), [junk4], [sst4])
        P.act(sst4[:, 4:8], sst4[:, 0:4], AF.Sqrt, [sst4, eps], [sst4], scale=1.0 / 128, bias=eps[:])
        P.op('dve', lambda e: e.reciprocal(out=sst4[:, 4:8], in_=sst4[:, 4:8]), [sst4], [sst4])
        P.tt('dve', ob4[:], ob4[:], sst4[:, 4:8].unsqueeze(2).to_broadcast([128, 4, 128]), ALU.mult,
             [ob4, sst4], [ob4])
        P.tt('dve', ob4[:], ob4[:],
             gainb[:, hd * 128:(hd + 1) * 128].unsqueeze(1).to_broadcast([128, 4, 128]), ALU.mult,
             [ob4, gainb], [ob4])
        P.tt('dve', yb4[:], ob4[:], agk[:], ALU.mult, [ob4, agk], [yb4])
        for j in range(4):
            P.tr(pt1[:, 4 + j, :], yb4[:, j, :], ident[:], [yb4, ident], [pt1])
        P.cp('act', yT[:, tb * 512:(tb + 1) * 512].rearrange("p (j c) -> p j c", c=128), pt1[:, 4:8, :], [pt1], [yT])
        if tb == 3:
            P.dma('sp', out_d.t[hd * 128:(hd + 1) * 128, :], yT[:], [yT], (), owner=yT)

    load_w(0)
    Pm(0)
    Pe(0)
    for k in range(len(steps)):
        if k + 1 < len(steps):
            Pm(k + 1)
        E(k)
        if k + 1 < len(steps):
            Pe(k + 1)
        L(k)


def emit_nsa(P, s2, hT, hTv, ident, ptr, VAL, ADD, wnq_d, wnkv_d, wngt_d, wnbg_d, w1k_d, w1v_d,
             w2k_d, w2v_d, pek_d, pev_d, cb_d, out_d, ng=4, nqt=NT):
    cf_reads = []
    cb = P.sb(s2, "cb", [128, CB_W], BF16)
    P.dma('pool', cb[:], cb_d.t.ap(), (), [cb], owner=cb)

    def CB(name, lo, hi):
        return cb[:, CB_OFF[name] + lo:CB_OFF[name] + hi]

    wbuf = P.sb(s2, "wbuf", [128, KC, 768], BF16)
    wgt = P.sb(s2, "wgt", [128, KC, 12], BF16)
    w1k = P.sb(s2, "w1k", [128, 32, 128], BF16)
    w1v = P.sb(s2, "w1v", [128, 32, 128], BF16)
    w2k = P.sb(s2, "w2k", [128, 128], BF16)
    w2v = P.sb(s2, "w2v", [128, 128], BF16)
    pek = P.sb(s2, "pek", [128, 32], BF16)
    pev = P.sb(s2, "pev", [128, 32], BF16)
    P.dma('pool', w1k[:], w1k_d.t.ap().rearrange("(j d) o -> d j o", d=128), (), [w1k], owner=w1k)
    P.dma('pool', w1v[:], w1v_d.t.ap().rearrange("(j d) o -> d j o", d=128), (), [w1v], owner=w1v)
    P.dma('pool', w2k[:], w2k_d.t.ap(), (), [w2k], owner=w2k)
    P.dma('pool', w2v[:], w2v_d.t.ap(), (), [w2v], owner=w2v)
    P.dma('pool', pek[:], pek_d.t.ap(), (), [pek], owner=pek)
    P.dma('pool', pev[:], pev_d.t.ap(), (), [pev], owner=pev)

    qT = P.sb(s2, "qT", [128, 4, T], BF16)
    kcT = P.sb(s2, "kcT", [128, T], BF16)
    vcT = P.sb(s2, "vcT", [128, T], BF16)
    ksT = P.sb(s2, "ksT", [128, T], BF16)
    kwT = P.sb(s2, "kwT", [128, T], BF16)
    vsA = P.sb(s2, "vsA", [128, NT, 129], BF16)
    vwA = P.sb(s2, "vwA", [128, NT, 129], BF16)
    hidk = P.sb(s2, "hidk", [128, 127], BF16)
    hidv = P.sb(s2, "hidv", [128, 127], BF16)
    cbias = P.sb(s2, "cbias", [128, 2], F32)
    KcT = P.sb(s2, "KcT", [128, 127], BF16)
    VcA = P.sb(s2, "VcA", [128, 161], BF16)
    Rsel = P.sb(s2, "Rsel", [128, 4, 128], BF16)
    Pt = [P.sb(s2, "Pt%d" % i, [128, 4, 128], BF16) for i in range(3)]
    gts = P.sb(s2, "gts", [128, 12], F32)
    sgate = P.sb(s2, "sgate", [128, 512], F32)
    zz = P.sb(s2, "zz", [128, 3, 4], F32)
    cs = P.sb(s2, "cs", [128, 3, 4], F32)
    acc = P.sb(s2, "acc", [128, 4, 128], F32)
    imp = P.sb(s2, "imp", [128, 32], F32)
    score = P.sb(s2, "score", [128, 32], F32)
    work = P.sb(s2, "work", [128, 32], F32)
    m8 = P.sb(s2, "m8", [128, 16], F32)
    nm = P.sb(s2, "nm", [128, 32], F32)
    ybf = P.sb(s2, "ybf", [128, 512], BF16)
    yTt = [P.sb(s2, "yTt%d" % i, [128, 4, 128], BF16) for i in range(2)]
    psA = [P.ps(s2, "psA%d" % i, [128, 4, 128], F32) for i in range(2)]
    ident, identf = ident
    psO4 = P.ps(s2, "psO", [128, 4, 512], F32)
    psO = [psO4] * 4
    ocp = P.sb(s2, "ocp", [128, 4, 161], F32)
    impm = P.sb(s2, "impm", [128, 4, 32], F32)
    tmpo = P.sb(s2, "tmpo", [128, 4, 128], F32)
    psM = P.ps(s2, "psM", [128, 512], F32)
    psG = psM

    P.memset('dve', vsA[:, :, 128:129], 1.0, [vsA])
    P.memset('dve', vwA[:, :, 128:129], 1.0, [vwA])
    P.memset('dve', Rsel[:], 0.0, [Rsel])
    P.memset('dve', VcA[:], 0.0, [VcA])
    P.memset('dve', VcA[:, 128:129], 1.0, [VcA])
    P.cp('dve', VcA[:, 129:161], CB('OV', 0, 32), [cb], [VcA])

    def Oh(h):
        return psO4[:, h, 0:256]

    npa = [0]
    npt = [1]

    def nextA():
        p = psA[npa[0] % 2]
        npa[0] += 1
        return p

    for gg in range(ng):
        Ra = CB('Ra', gg * 512, (gg + 1) * 512)
        P.dma('pool', wbuf[:, :, 0:512], wnq_d.t[:, gg * 512:(gg + 1) * 512].rearrange("(kc p) c -> p kc c", p=128),
              (), [wbuf], owner=wbuf)
        for h in range(4):
            for tb in range(4):
                pp = nextA()
                ppf = pp[:].rearrange("p a b -> p (a b)")
                for kc in range(KC):
                    P.mm(ppf, wbuf[:, kc, h * 128:(h + 1) * 128], hT[:, kc, tb * 512:(tb + 1) * 512],
                         kc == 0, kc == KC - 1, [wbuf] + hTv[4 * tb:4 * tb + 4], [pp])
                P.act(qT[:, h, tb * 512:(tb + 1) * 512], ppf, AF.Copy, [pp], [qT], scale=SCALE)
        P.dma('pool', wbuf[:], wnkv_d.t[:, gg * 768:(gg + 1) * 768].rearrange("(kc p) c -> p kc c", p=128),
              (), [wbuf], owner=wbuf)
        P.dma('pool', wgt[:], wngt_d.t[:, gg * 12:(gg + 1) * 12].rearrange("(kc p) c -> p kc c", p=128),
              (), [wgt], owner=wgt)
        for (dst, col) in ((kcT, 0), (vcT, 128), (ksT, 256), (kwT, 384)):
            for tb in range(4):
                pp = nextA()
                ppf = pp[:].rearrange("p a b -> p (a b)")
                for kc in range(KC):
                    P.mm(ppf, wbuf[:, kc, col:col + 128], hT[:, kc, tb * 512:(tb + 1) * 512],
                         kc == 0, kc == KC - 1, [wbuf] + hTv[4 * tb:4 * tb + 4], [pp])
                P.cp('dve' if tb % 2 else 'act', dst[:, tb * 512:(tb + 1) * 512], ppf, [pp], [dst])
        for tt in range(NT):
            pp = nextA()
            ppf = pp[:].rearrange("p a b -> p (a b)")
            for kc in range(KC):
                P.mm(ppf[:, 0:256], hT[:, kc, tt * 128:(tt + 1) * 128], wbuf[:, kc, 512:768],
                     kc == 0, kc == KC - 1, [wbuf, hTv[tt]], [pp])
            eng = 'dve' if tt % 2 else 'act'
            P.cp(eng, vsA[:, tt, 0:128], ppf[:, 0:128], [pp], [vsA])
            P.cp(eng, vwA[:, tt, 0:128], ppf[:, 128:256], [pp], [vwA])
        P.dma('pool', wbuf[:, :, 0:512], wnbg_d.t[:, gg * 512:(gg + 1) * 512].rearrange("(kc p) c -> p kc c", p=128),
              (), [wbuf], owner=wbuf)
        for (src, w1, w2, pe, hid, col) in ((kcT, w1k, w2k, pek, hidk, 0), (vcT, w1v, w2v, pev, hidv, 1)):
            srcv = src[:].rearrange("p (n r) -> p n r", r=16)
            for j in range(32):
                P.mm(psM[:, 0:127], w1[:, j, :], srcv[:, j // 16:j // 16 + 127, j % 16], j == 0, j == 31,
                     [w1, src], [psM])
            for j in range(32):
                P.mm(psM[:, 128:129], w1[:, j, :], pe[:, j:j + 1], j == 0, j == 31, [w1, pe], [psM])
            P.cp('dve', cbias[:, col:col + 1], psM[:, 128:129], [psM], [cbias])
            P.act(hid[:], psM[:, 0:127], AF.Silu, [psM, cbias], [hid], bias=cbias[:, col:col + 1])
        P.mm(psM[:, 256:383], w2k[:], hidk[:], True, True, [w2k, hidk], [psM])
        P.cp('dve', KcT[:], psM[:, 256:383], [psM], [KcT])
        P.mm(psM[0:127, 384:512], hidv[:], w2v[:], True, True, [hidv, w2v], [psM])
        P.cp('dve', VcA[0:127, 0:128], psM[0:127, 384:512], [psM], [VcA])

        for i in range(nqt):
            isl = slice(i * 128, (i + 1) * 128)
            qrhs = qT[:, :, isl]
            for kc in range(KC):
                P.mm(psG[:], hT[:, kc, isl], wbuf[:, kc, 0:512], kc == 0, kc == KC - 1, [wbuf, hTv[i]], [psG])
            P.act(sgate[:], psG[:], AF.Silu, [psG], [sgate])
            for kc in range(KC):
                P.mm(psM[:, 0:12], hT[:, kc, isl], wgt[:, kc, :], kc == 0, kc == KC - 1, [wgt, hTv[i]], [psM])
            P.act(gts[:], psM[:, 0:12], AF.Sigmoid, [psM], [gts])
            gv = gts[:].rearrange("p (h j) -> p j h", j=3)

            def finish(br, width):
                P.cp('dve', ocp[:, :, 0:width], psO4[:, :, 0:width], [psO4], [ocp])
                P.ts('dve', zz[:, br, :], ocp[:, :, 128], 1e-37, None, ALU.max, None, [ocp], [zz])
                P.op('dve', lambda e, br=br: e.reciprocal(out=zz[:, br, :], in_=zz[:, br, :]), [zz], [zz])
                if br == 0:
                    P.tt('dve', impm[:], ocp[:, :, 129:161], zz[:, 0, :].unsqueeze(2).to_broadcast([128, 4, 32]),
                         ALU.mult, [ocp, zz], [impm])
                    P.op('dve', lambda e: e.tensor_reduce(out=imp[:], in_=impm[:].rearrange("p h j -> p j h"),
                                                          axis=mybir.AxisListType.X, op=ALU.add), [impm], [imp])
                P.tt('dve', cs[:, br, :], zz[:, br, :], gv[:, br, :], ALU.mult, [zz, gts], [cs])
                if br == 0:
                    P.tt('dve', acc[:], ocp[:, :, 0:128], cs[:, br, :].unsqueeze(2).to_broadcast([128, 4, 128]),
                         ALU.mult, [ocp, cs], [acc])
                else:
                    P.tt('dve', tmpo[:], ocp[:, :, 0:128], cs[:, br, :].unsqueeze(2).to_broadcast([128, 4, 128]),
                         ALU.mult, [ocp, cs], [tmpo])
                    P.tt('dve', acc[:], acc[:], tmpo[:], ALU.add, [acc, tmpo], [acc])

            pp = nextA()
            P.mm(pp[0:127, :, :], KcT[:], qrhs, True, False, [KcT, qT], [pp])
            P.mm(pp[0:127, :, :], CB('Lc', i * 127, (i + 1) * 127), Ra.rearrange("p (a b) -> p a b", b=128),
                 False, False, [cb], [pp])
            P.mm(pp[0:127, :, :], CB('Sel', i * 127, (i + 1) * 127),
                 CB('G', 0, 512).rearrange("p (a b) -> p a b", b=128), False, True, [cb], [pp])
            pt_ = Pt[0]
            P.act(pt_[0:127, :, :], pp[0:127, :, :], AF.Exp, [pp], [pt_])
            for h in range(4):
                P.mm(Oh(h)[:, 0:161], pt_[0:127, h, :], VcA[0:127, :], True, True, [pt_, VcA], [psO[h]])
            finish(0, 161)
            P.tt('dve', score[:], imp[:], VAL[:, i * 32:(i + 1) * 32], ALU.mult, [imp], [score])
            P.tt('dve', score[:], score[:], ADD[:, i * 32:(i + 1) * 32], ALU.add, [score], [score])
            P.op('dve', lambda e: e.max(out=m8[:, 0:8], in_=score[:]), [score], [m8])
            P.op('dve', lambda e: e.match_replace(out=work[:], in_to_replace=m8[:, 0:8], in_values=score[:],
                                                  imm_value=-3e38), [score, m8], [work])
            P.op('dve', lambda e: e.max(out=m8[:, 8:16], in_=work[:]), [work], [m8])
            P.ts('dve', nm[:], score[:], m8[:, 15:16], -BIG, ALU.is_lt, ALU.mult, [score, m8], [nm])
            P.tr(psM[0:32, 128:256], nm[:], identf[:], [nm, identf], [psM])
            P.cp('dve', Rsel[0:32, :, :], psM[0:32, 128:256].rearrange("p (a b) -> p a b", a=1).to_broadcast([32, 4, 128]),
                 [psM], [Rsel])

            for br, (kT_, vA_) in ((2, (kwT, vwA)), (1, (ksT, vsA))):
                jlo = 0 if br == 1 else max(0, i - 4)

                def emit_qk(jt, br=br, kT_=kT_, jlo=jlo):
                    dj = jt - i
                    pp = nextA()
                    extra = []
                    if br == 1:
                        extra.append((CB('E', jt * 128, (jt + 1) * 128), Rsel[:], [cb, Rsel]))
                    if jt == i:
                        extra.append((ident[:], CB('TRIc', 0, 512).rearrange("p (a b) -> p a b", b=128), [ident, cb]))
                    if br == 2 and jt == i - 4:
                        extra.append((ident[:], CB('TRIb', 0, 512).rearrange("p (a b) -> p a b", b=128), [ident, cb]))
                    P.mm(pp[:], kT_[:, jt * 128:(jt + 1) * 128], qrhs, True, False, [kT_, qT], [pp])
                    P.mm(pp[:], CB('La', (-dj) * 128, (-dj + 1) * 128), Ra.rearrange("p (a b) -> p a b", b=128),
                         False, len(extra) == 0, [cb], [pp])
                    for xi, (l_, r_, rd) in enumerate(extra):
                        P.mm(pp[:], l_, r_, False, xi == len(extra) - 1, rd, [pp])
                    return pp

                def emit_pv(jt, pp, vA_=vA_, jlo=jlo):
                    pt_ = Pt[npt[0] % 3]
                    npt[0] += 1
                    P.act(pt_[:], pp[:], AF.Exp, [pp], [pt_])
                    for h in range(4):
                        P.mm(Oh(h)[:, 0:129], pt_[:, h, :], vA_[:, jt, :], jt == jlo, jt == i,
                             [pt_, vA_], [psO[h]])

                pend = None
                for jt in range(jlo, i + 1):
                    pp = emit_qk(jt)
                    if pend is not None:
                        emit_pv(*pend)
                    pend = (jt, pp)
                emit_pv(*pend)
                finish(br, 129)
            P.tt('dve', ybf[:], acc[:].rearrange("p a b -> p (a b)"), sgate[:], ALU.mult, [acc, sgate], [ybf])
            ptb = ptr
            for h in range(4):
                P.tr(ptb[:, h, :], ybf[:, h * 128:(h + 1) * 128], ident[:], [ybf, ident], [ptb])
            yt = yTt[i % 2]
            P.cp('act', yt[:], ptb[:], [ptb], [yt])
            P.dma('sp', out_d.t[2048 + gg * 512:2048 + (gg + 1) * 512, isl].rearrange("(h c) t -> c h t", c=128),
                  yt[:], [yt], (), owner=yt)


EV_OFF = dict(a_q=0, a_f=2048, a_i=4096, a_g=6144, b_q=8192, b_kc=10240, b_vc=10752, b_ks=11264, b_vs=11776,
              b_kw=12288, b_vw=12800, b_gate=13312, b_g=13360)


def emit_even_A(P, C, e_idx, x_src, nw_row, W, out_d):
    ident, identf, cm, eps, cf = C['ident'], C['identf'], C['cm'], C['eps'], C['cf']
    VAL = cf[:, 0:512]
    ADD = cf[:, 512:1024]
    RST = cf[:, 1024:1536]
    with ExitStack() as sa:
        hT = P.sb(sa, "hT", [128, KC, T], BF16)
        hTv = [P.view(hT, "hT%d" % i) for i in range(NT)]
        ss = P.sb(sa, "ss", [128, 4], F32)
        with ExitStack() as s0:
            xt = [P.sb(s0, "xt%d" % i, [128, D], F32) for i in range(2)]
            xn = [P.sb(s0, "xn%d" % i, [128, D], BF16) for i in range(2)]
            nwb = P.sb(s0, "nwb", [128, D], F32)
            ptr = [P.ps(s0, "ptr%d" % i, [128, 4, 128], BF16) for i in range(2)]
            P.dma('sp', nwb[:], nw_row.partition_broadcast(128), (), [nwb], owner=nwb)
            emit_hT(P, s0, x_src, nwb, ident, hT, hTv, xt, xn, ptr, {'eps': eps, 'ss': ss})
            P.end_section()
        with ExitStack() as s1:
            ptb = P.ps(s1, "ptrh", [128, 8, 128], BF16)
            emit_hgrn2(P, s1, e_idx, hT, hTv, ident, cm, RST, eps, (ptb, P.view(ptb)), W['whg'], W['lbl'], W['gain'],
                       out_d, 16)
            P.end_section()
        with ExitStack() as s2:
            ptr = P.ps(s2, "ptrn", [128, 4, 128], BF16)
            emit_nsa(P, s2, hT, hTv, (ident, identf), ptr, VAL, ADD, W['wnq'], W['wnkv'], W['wngt'], W['wnbg'],
                     W['w1k'], W['w1v'], W['w2k'], W['w2v'], W['pekT'], W['pevT'], C['cb_d'], out_d, 4, NT)
            P.end_section()


def emit_odd_A(P, C, x_src, nw_row, W, out_d):
    ident, eps = C['ident'], C['eps']
    NBLK = 10
    NCT = 20
    wx_d, wg_d, wa_d, wi_d, vec_d = W['wx'], W['wg'], W['wa'], W['wi'], W['vec']
    with ExitStack() as st:
        hT = P.sb(st, "hT", [128, KC, T], BF16)
        hTv = [P.view(hT, "hT%d" % i) for i in range(NT)]
        Tb = [P.sb(st, "T%d" % i, [128, T], F32) for i in range(4)]
        Bb = [P.sb(st, "B%d" % i, [128, T], BF16) for i in range(2)]
        xraw = [P.sb(st, "xraw%d" % i, [128, T + 3], F32) for i in range(2)]
        xc = [P.sb(st, "xc%d" % i, [128, T], F32) for i in range(2)]
        mix = [P.sb(st, "mix%d" % i, [128, T], BF16) for i in range(2)]
        wxs = [P.sb(st, "wxs%d" % i, [128, KC, 256], BF16) for i in range(2)]
        wgs = [P.sb(st, "wgs%d" % i, [128, KC, 256], BF16) for i in range(2)]
        was = [P.sb(st, "was%d" % i, [128, 2, 256], BF16) for i in range(2)]
        wis = [P.sb(st, "wis%d" % i, [128, 2, 256], BF16) for i in range(2)]
        vec = P.sb(st, "vec", [128, NCT, 8], F32)
        cl = P.sb(st, "cl", [128, NCT], F32)
        ss = P.sb(st, "ss", [128, 4], F32)
        ptr = [P.ps(st, "ptr%d" % i, [128, 4, 128], BF16) for i in range(2)]
        pj = [P.ps(st, "pj%d" % i, [128, 512], F32) for i in range(2)]
        pg = [P.ps(st, "pg%d" % i, [128, 512], F32) for i in range(2)]

        nwb = xraw[0]
        P.dma('sp', nwb[:, 0:D], nw_row.partition_broadcast(128), (), [nwb], owner=nwb)
        P.dma('sp', vec[:], vec_d.t.ap().rearrange("p (c k) -> p c k", k=8), (), [vec], owner=vec)
        emit_hT(P, st, x_src, nwb, ident, hT, hTv, Tb[0:2], Bb, ptr, {'eps': eps, 'ss': ss})
        P.act(cl[:], vec[:, :, 7], AF.Exp, [vec], [cl], scale=-1.0)
        P.act(cl[:], cl[:], AF.Ln, [cl], [cl], bias=1.0)
        P.ts('dve', cl[:], cl[:], -8.0, None, ALU.mult, None, [cl], [cl])
        for i in range(2):
            P.memset('dve', xraw[i][:, 0:3], 0.0, [xraw[i]])

        def load_w(n):
            P.dma('pool', wxs[n % 2][:], wx_d.t[:, n * 256:(n + 1) * 256].rearrange("(kc p) c -> p kc c", p=128),
                  (), [wxs[n % 2]], owner=wxs[n % 2])
            P.dma('pool', wgs[n % 2][:], wg_d.t[:, n * 256:(n + 1) * 256].rearrange("(kc p) c -> p kc c", p=128),
                  (), [wgs[n % 2]], owner=wgs[n % 2])
            P.dma('pool', was[n % 2][:], wa_d.t[n].rearrange("(dt p) e -> p dt e", p=128), (), [was[n % 2]],
                  owner=was[n % 2])
            P.dma('pool', wis[n % 2][:], wi_d.t[n].rearrange("(dt p) e -> p dt e", p=128), (), [wis[n % 2]],
                  owner=wis[n % 2])

        npj = 0
        load_w(0)
        for n in range(NBLK):
            w_x, w_g, w_a, w_i = wxs[n % 2], wgs[n % 2], was[n % 2], wis[n % 2]
            if n + 1 < NBLK:
                load_w(n + 1)
            for hf in range(2):
                ct = 2 * n + hf
                xr = xraw[hf]
                for tb in range(4):
                    pp = pj[npj % 2]
                    npj += 1
                    for kc in range(KC):
                        P.mm(pp[:], w_x[:, kc, hf * 128:(hf + 1) * 128], hT[:, kc, tb * 512:(tb + 1) * 512],
                             kc == 0, kc == KC - 1, [w_x] + hTv[4 * tb:4 * tb + 4], [pp])
                    P.cp('act', xr[:, 3 + tb * 512:3 + (tb + 1) * 512], pp[:], [pp], [xr])
                c = xc[hf]
                P.ts('dve', c[:], xr[:, 3:3 + T], vec[:, ct, 3:4], vec[:, ct, 4:5], ALU.mult, ALU.add,
                     [xr, vec], [c])
                for j in range(3):
                    P.stt(c[:], xr[:, j:j + T], vec[:, ct, j:j + 1], c[:], ALU.mult, ALU.add, [xr, vec, c], [c])
                P.cp('pool', Bb[hf][:], c[:], [c], [Bb[hf]])
            for eh in range(2):
                ct = 2 * n + eh
                R, I, S, G = Tb
                for tb in range(4):
                    pp = pj[npj % 2]
                    npj += 1
                    for kc in range(KC):
                        P.mm(pp[:], w_g[:, kc, eh * 128:(eh + 1) * 128], hT[:, kc, tb * 512:(tb + 1) * 512],
                             kc == 0, kc == KC - 1, [w_g] + hTv[4 * tb:4 * tb + 4], [pp])
                    P.act(G[:, tb * 512:(tb + 1) * 512], pp[:], AF.Silu, [pp], [G])
                for (dst, wsb, bcol) in ((R, w_a, 5), (I, w_i, 6)):
                    for tb in range(4):
                        pp = pg[npj % 2]
                        npj += 1
                        for dt_ in range(2):
                            P.mm(pp[:], wsb[:, dt_, eh * 128:(eh + 1) * 128],
                                 Bb[dt_][:, tb * 512:(tb + 1) * 512], dt_ == 0, dt_ == 1, [wsb, Bb[dt_]], [pp])
                        P.act(dst[:, tb * 512:(tb + 1) * 512], pp[:], AF.Sigmoid, [pp, vec], [dst],
                              bias=vec[:, ct, bcol:bcol + 1])
                P.act(R[:], R[:], AF.Exp, [R, cl], [R], scale=cl[:, ct:ct + 1])
                P.tt('pool', S[:], R[:], R[:], ALU.mult, [R], [S])
                P.act(S[:], S[:], AF.Sqrt, [S], [S], scale=-1.0, bias=1.0)
                P.tt('dve', I[:], I[:], xc[eh][:], ALU.mult, [I, xc[eh]], [I])
                P.tt('dve', I[:], I[:], S[:], ALU.mult, [I, S], [I])
                P.scan(S[:], R[:], I[:], 0.0, [R, I], [S])
                m = mix[eh]
                P.tt('dve', m[:], S[:], G[:], ALU.mult, [S, G], [m])
                P.dma('sp', out_d.t[ct * 128:(ct + 1) * 128, :], m[:], [m], (), owner=m)
        P.end_section()


def emit_B(P, C, Cdim, mix_d, w_d, x_src, x_dst, final, fn_row=None, out_d=None):
    KCC = Cdim // 128
    NB = 512
    TB = 1024
    eps = C['eps']
    with ExitStack() as st:
        mix = P.sb(st, "bmix", [128, KCC, TB], BF16)
        ws = [P.sb(st, "bws%d" % i, [128, KCC, NB], BF16) for i in range(2)]
        xb = [P.sb(st, "bx%d" % i, [128, NB], F32) for i in range(4)]
        pp = [P.ps(st, "bpp%d" % i, [128, 512], F32) for i in range(4)]
        k = 0
        nw = 0
        for th in range(2):
            P.dma('sp', mix[:], mix_d.t[:, th * TB:(th + 1) * TB].rearrange("(kc p) t -> p kc t", p=128),
                  (), [mix], owner=mix)
            for nb in range(D // NB):
                w = ws[nw % 2]
                nw += 1
                P.dma('pool', w[:], w_d.t[:, nb * NB:(nb + 1) * NB].rearrange("(kc p) c -> p kc c", p=128),
                      (), [w], owner=w)
                for tt in range(TB // 128):
                    rows = slice(th * TB + tt * 128, th * TB + (tt + 1) * 128)
                    cols = slice(nb * NB, (nb + 1) * NB)
                    p_ = pp[k % 4]
                    xv = xb[k % 4]
                    k += 1
                    P.dma('sp', xv[:], x_src.t[rows, cols], (), [xv], owner=xv)
                    for kc in range(KCC):
                        P.mm(p_[:, 0:NB], mix[:, kc, tt * 128:(tt + 1) * 128], w[:, kc, :], kc == 0, kc == KCC - 1,
                             [mix, w], [p_])
                    P.tt('dve', xv[:], xv[:], p_[:, 0:NB], ALU.add, [xv, p_], [xv])
                    P.dma('sp', x_dst.t[rows, cols], xv[:], [xv], (), owner=xv)
        P.end_section()
    if final:
        with ExitStack() as st:
            xt = [P.sb(st, "fx%d" % i, [128, D], F32) for i in range(2)]
            junk = P.sb(st, "fjunk", [128, D], BF16)
            fnb = P.sb(st, "fnb", [128, D], F32)
            ss = P.sb(st, "fss", [128, 4], F32)
            P.dma('sp', fnb[:], fn_row.partition_broadcast(128), (), [fnb], owner=fnb)
            for tt in range(NT):
                xv = xt[tt % 2]
                s0 = ss[:, 2 * (tt % 2):2 * (tt % 2) + 1]
                s1 = ss[:, 2 * (tt % 2) + 1:2 * (tt % 2) + 2]
                P.dma('sp', xv[:], x_dst.t[tt * 128:(tt + 1) * 128, :], (), [xv], owner=xv)
                P.act(junk[:], xv[:], AF.Square, [xv], [junk, ss], accum_out=s0)
                P.act(s1, s0, AF.Sqrt, [ss, eps], [ss], scale=1.0 / D, bias=eps[:])
                P.op('dve', lambda e, s1=s1: e.reciprocal(out=s1, in_=s1), [ss], [ss])
                P.stt(xv[:], xv[:], s1, fnb[:], ALU.mult, ALU.mult, [xv, ss, fnb], [xv])
                P.dma('sp', out_d.t[tt * 128:(tt + 1) * 128, :], xv[:], [xv], (), owner=xv)
            P.end_section()


def build_fused(nlayers=4):
    nc = bass.Bass("TRN2", target_bir_lowering=False)
    with ExitStack() as st:
        P = Prog(nc, st)
        x_d = P.dram("x", [T, D], F32, "ExternalInput")
        nw_d = P.dram("nw", [4, D], F32, "ExternalInput")
        fnw_d = P.dram("fnw", [D], F32, "ExternalInput")
        cb_d = P.dram("cb", [128, CB_W], F32, "ExternalInput")
        cf_d = P.dram("cf", [128, 1536], F32, "ExternalInput")
        lbl_d = P.dram("lbl", [128, 32], F32, "ExternalInput")
        WE, WO = [], []
        for e in range(2):
            W = {'lbl': lbl_d}
            for nm, shp in (('whg', [D, 16 * 512]), ('gain', [2048]), ('wnq', [D, 2048]), ('wnkv', [D, 3072]),
                            ('wngt', [D, 48]), ('wnbg', [D, 2048]), ('w1k', [4096, 128]), ('w1v', [4096, 128]),
                            ('w2k', [128, 128]), ('w2v', [128, 128]), ('pekT', [128, 32]), ('pevT', [128, 32]),
                            ('wout', [4096, D])):
                W[nm] = P.dram("e%d_%s" % (e, nm), shp, F32, "ExternalInput")
            WE.append(W)
        for o in range(2):
            W = {}
            for nm, shp in (('wx', [D, 2560]), ('wg', [D, 2560]), ('wa', [10, 256, 256]), ('wi', [10, 256, 256]),
                            ('vec', [128, 160]), ('wout', [2560, D])):
                W[nm] = P.dram("o%d_%s" % (o, nm), shp, F32, "ExternalInput")
            WO.append(W)
        xs_d = P.dram("xs_scratch", [T, D], F32, "Internal")
        mixE_d = P.dram("mixE_scratch", [4096, T], BF16, "Internal")
        mixO_d = P.dram("mixO_scratch", [2560, T], BF16, "Internal")
        out_d = P.dram("out", [T, D], F32, "ExternalOutput")

        ident = make_ident(P, st)
        idf2 = P.sb(st, "idf2", [128, 128], F32)
        cm = P.sb(st, "cm", [128, 128], F32)
        identf = P.sb(st, "identf", [128, 128], F32)
        P.op('pool', lambda e: e.iota(idf2[:], pattern=[[1, 128]], base=0, channel_multiplier=-1,
                                      allow_small_or_imprecise_dtypes=True), (), [idf2])
        P.ts('dve', cm[:], idf2[:], 0.0, None, ALU.is_ge, None, [idf2], [cm])
        P.ts('dve', identf[:], idf2[:], 0.0, None, ALU.is_equal, None, [idf2], [identf])
        eps = P.sb(st, "eps", [128, 1], F32)
        P.memset('dve', eps[:], EPS, [eps])
        cf = P.sb(st, "cf", [128, 1536], F32)
        P.dma('sp', cf[:], cf_d.t.ap(), (), [cf], owner=cf)
        C = dict(ident=ident, identf=identf, cm=cm, eps=eps, cf=cf, cb_d=cb_d)
        P.end_section()

        for layer in range(nlayers):
            x_src = x_d if layer == 0 else xs_d
            nw_row = nw_d.t[layer]
            last = layer == nlayers - 1
            if layer % 2 == 0:
                W = WE[layer // 2]
                emit_even_A(P, C, layer // 2, x_src, nw_row, W, mixE_d)
                emit_B(P, C, 4096, mixE_d, W['wout'], x_src, xs_d, last, fnw_d.t.ap(), out_d)
            else:
                W = WO[layer // 2]
                emit_odd_A(P, C, x_src, nw_row, W, mixO_d)
                emit_B(P, C, 2560, mixO_d, W['wout'], x_src, xs_d, last, fnw_d.t.ap(), out_d)
        P.barrier()
        P.emit()
    return nc


def pack_inputs(inp):
    cbt, cft = even_const_tables()
    shared = {"nw": c_(inp['norm_w']), "fnw": c_(inp['final_norm_w']), "cb": cbt, "cf": cft}
    lg = inp['hgrn_lb_logits']
    shared["lbl"] = c_(lg.reshape(2, 16, 128).transpose(2, 1, 0).reshape(128, 32))
    for e in range(2):
        w_in = inp['even_w_in'][e]
        cols = []
        for gh in range(16):
            for k in ('a_q', 'a_f', 'a_i', 'a_g'):
                cols.append(w_in[:, EV_OFF[k] + gh * 128:EV_OFF[k] + (gh + 1) * 128])
        shared["e%d_whg" % e] = np.concatenate(cols, axis=1)
        kv = []
        for g in range(4):
            for k in ('b_kc', 'b_vc', 'b_ks', 'b_kw', 'b_vs', 'b_vw'):
                kv.append(w_in[:, EV_OFF[k] + g * 128:EV_OFF[k] + (g + 1) * 128])
        shared["e%d_wnkv" % e] = np.concatenate(kv, axis=1)
        shared["e%d_wnq" % e] = c_(w_in[:, EV_OFF['b_q']:EV_OFF['b_q'] + 2048])
        shared["e%d_wngt" % e] = c_(w_in[:, EV_OFF['b_gate']:EV_OFF['b_gate'] + 48])
        shared["e%d_wnbg" % e] = c_(w_in[:, EV_OFF['b_g']:EV_OFF['b_g'] + 2048])
        shared["e%d_gain" % e] = c_(inp['hgrn_norm_w'][e])
        shared["e%d_w1k" % e] = c_(inp['cmp_w1_k'][e])
        shared["e%d_w1v" % e] = c_(inp['cmp_w1_v'][e])
        shared["e%d_w2k" % e] = c_(inp['cmp_w2_k'][e])
        shared["e%d_w2v" % e] = c_(inp['cmp_w2_v'][e])
        shared["e%d_pekT" % e] = c_(inp['cmp_pe_k'][e].T)
        shared["e%d_pevT" % e] = c_(inp['cmp_pe_v'][e].T)
        shared["e%d_wout" % e] = c_(inp['even_w_out'][e])
    for o in range(2):
        w_in = inp['odd_w_in'][o]
        shared["o%d_wx" % o] = c_(w_in[:, 0:D_RNN])
        shared["o%d_wg" % o] = c_(w_in[:, D_RNN:2 * D_RNN])
        shared["o%d_wa" % o] = c_(inp['rg_w_a'][o])
        shared["o%d_wi" % o] = c_(inp['rg_w_i'][o])
        cw = inp['rg_conv_w'][o]
        vec = np.stack([cw[0], cw[1], cw[2], cw[3], inp['rg_conv_b'][o], inp['rg_b_a'][o], inp['rg_b_i'][o],
                        inp['rg_lambda'][o]], axis=-1)
        shared["o%d_vec" % o] = c_(vec.reshape(20, 128, 8).transpose(1, 0, 2).reshape(128, 160).astype(np.float32))
        shared["o%d_wout" % o] = c_(inp['odd_w_out'][o])
    return shared


def kernel(**inputs):
    inp = {k: np.asarray(v) for k, v in inputs.items()}
    shared = pack_inputs(inp)
    x = np.ascontiguousarray(inp['x'], dtype=np.float32)
    nc = _get("fused", build_fused)
    in_maps = []
    for c in range(NCORES):
        m = dict(shared)
        m["x"] = c_(x[c // 2])
        in_maps.append(m)
    res = run_bass_kernel_spmd(nc, in_maps, core_ids=list(range(NCORES)))
    out = np.stack([res.results[2 * b]["out"] for b in range(4)], axis=0)
    return out.astype(np.float32)
```

```python
import numpy as np
from contextlib import ExitStack
import concourse.bass as bass
import concourse.mybir as mybir
from concourse.alu_op_type import AluOpType as ALU
from concourse.bass_utils import run_bass_kernel_spmd

AF = mybir.ActivationFunctionType
F32 = mybir.dt.float32
BF16 = mybir.dt.bfloat16

D = 2048
T = 2048
NT = T // 128
KC = D // 128
D_RNN = 2560
EPS = 1e-6
NCORES = 8


class Buf:
    def __init__(self, t, name=""):
        self.t = t
        self.name = name
        self.w = None
        self.r = []
        self.dsem = None
        self.dcnt = 0
        self.psum = False

    def __getitem__(self, idx):
        return self.t[idx]


ENGS = ('pe', 'act', 'dve', 'pool', 'sp')


class Prog:
    def __init__(self, nc, stack):
        self.nc = nc
        self.stack = stack
        self.ops = {k: [] for k in ENGS}
        self.esem = {k: stack.enter_context(nc.semaphore("es_" + k)) for k in ENGS}
        self.ecnt = {k: 0 for k in ENGS}
        self.waited = {k: {} for k in ENGS}
        self.nsem = 0
        self.nops = 0
        self.dtoks = {}
        self.keep = []
        self.uid = 0
        self.sem_pool = {'sw': [], 'hw': []}
        self.sec_bufs = []

    def sb(self, stack, name, shape, dtype):
        self.uid += 1
        return Buf(stack.enter_context(self.nc.sbuf_tensor("s%d_%s" % (self.uid, name), list(shape), dtype)), name)

    def ps(self, stack, name, shape, dtype=F32):
        self.uid += 1
        b = Buf(stack.enter_context(self.nc.psum_tensor("p%d_%s" % (self.uid, name), list(shape), dtype)), name)
        b.psum = True
        return b

    def dram(self, name, shape, dtype, kind):
        return Buf(self.nc.dram_tensor(name, list(shape), dtype, kind=kind), name)

    def view(self, b, name=""):
        return Buf(b.t, name or b.name)

    def _dsem(self, b, eng):
        kind = 'sw' if eng == 'pool' else 'hw'
        if b.dsem is None:
            b.dsem = {}
            b.dcnt = {}
        if kind not in b.dsem:
            if self.sem_pool[kind]:
                b.dsem[kind], b.dcnt[kind] = self.sem_pool[kind].pop()
            else:
                sem = self.stack.enter_context(self.nc.semaphore("ds%s_%d" % (kind, self.nsem)))
                self.keep.append(sem)
                self.nsem += 1
                b.dsem[kind], b.dcnt[kind] = sem, 0
            self.sec_bufs.append((b, kind))
        return kind

    def end_section(self):
        self.barrier()
        for b, kind in self.sec_bufs:
            self.sem_pool[kind].append((b.dsem.pop(kind), b.dcnt.pop(kind)))
        self.sec_bufs = []

    def _deps(self, eng, reads, writes, waw=True):
        need = {}
        own = self.esem[eng]

        def add(d, raw):
            if d is None:
                return
            s, v = d
            if s is own and eng == 'pe':
                return
            k = id(s)
            if k not in need or need[k][1] < v:
                need[k] = (s, v)

        for b in reads:
            add(b.w, 'raw')
            if b.psum:
                for d in b.r:
                    if d[0] is not own:
                        add(d, 'rar')
        for b in writes:
            if waw:
                add(b.w, 'waw')
            for d in b.r:
                add(d, 'war')
        out = []
        wd = self.waited[eng]
        for k, (s, v) in need.items():
            if wd.get(k, 0) >= v:
                continue
            wd[k] = v
            out.append((s, v))
        return out

    def _commit(self, reads, writes, tok):
        for b in reads:
            b.r.append(tok)
            if len(b.r) > 64:
                b.r = b.r[-64:] if False else b.r
        for b in writes:
            b.w = tok
            b.r = []

    def op(self, eng, fn, reads=(), writes=(), waw=True):
        waits = self._deps(eng, reads, writes, waw)
        self.ecnt[eng] += 1
        tok = (self.esem[eng], self.ecnt[eng])
        self.ops[eng].append((waits, fn, self.esem[eng], 1))
        self._commit(reads, writes, tok)
        self.nops += 1

    def dma(self, eng, out_ap, in_ap, reads=(), writes=(), owner=None):
        waits = self._deps(eng, reads, writes)
        kind = self._dsem(owner, eng)
        sem = owner.dsem[kind]
        owner.dcnt[kind] += 16
        tok = (sem, owner.dcnt[kind])
        self.dtoks[id(sem)] = tok

        def fn(e):
            return e.dma_start(out=out_ap, in_=in_ap)
        self.ops[eng].append((waits, fn, sem, 16))
        self._commit(reads, writes, tok)
        self.nops += 1

    def barrier(self):
        toks = [(self.esem[k], self.ecnt[k]) for k in ENGS if self.ecnt[k] > 0]
        toks += list(self.dtoks.values())
        for k in ENGS:
            waits = []
            wd = self.waited[k]
            for (s, v) in toks:
                if s is self.esem[k]:
                    continue
                if wd.get(id(s), 0) >= v:
                    continue
                wd[id(s)] = v
                waits.append((s, v))
            if waits:
                self.ops[k].append((waits, None, None, 0))

    def wait_all(self, eng, bufs):
        waits = self._deps(eng, bufs, bufs)
        self.ops[eng].append((waits, None, None, 0))

    def mm(self, out, lhsT, rhs, start, stop, reads, writes):
        self.op('pe', lambda e: e.matmul(out, lhsT=lhsT, rhs=rhs, start=start, stop=stop),
                reads, writes)

    def tr(self, out, in_, ident, reads, writes):
        self.op('pe', lambda e: e.transpose(out=out, in_=in_, identity=ident), reads, writes)

    def act(self, out, in_, func, reads, writes, waw=True, **kw):
        self.op('act', lambda e: e.activation(out=out, in_=in_, func=func, **kw), reads, writes, waw)

    def tt(self, eng, out, in0, in1, op, reads, writes, waw=True):
        self.op(eng, lambda e: e.tensor_tensor(out=out, in0=in0, in1=in1, op=op), reads, writes, waw)

    def ts(self, eng, out, in0, s1, s2, op0, op1, reads, writes, waw=True):
        if op1 is None:
            self.op(eng, lambda e: e.tensor_scalar(out=out, in0=in0, scalar1=s1, scalar2=None, op0=op0),
                    reads, writes, waw)
        else:
            self.op(eng, lambda e: e.tensor_scalar(out=out, in0=in0, scalar1=s1, scalar2=s2, op0=op0, op1=op1),
                    reads, writes, waw)

    def stt(self, out, in0, scalar, in1, op0, op1, reads, writes, waw=True):
        self.op('dve', lambda e: e.scalar_tensor_tensor(out=out, in0=in0, scalar=scalar, in1=in1,
                                                         op0=op0, op1=op1), reads, writes, waw)

    def cp(self, eng, out, in_, reads, writes, waw=True):
        if eng == 'act':
            self.op('act', lambda e: e.copy(out=out, in_=in_), reads, writes, waw)
        else:
            self.op(eng, lambda e: e.tensor_copy(out=out, in_=in_), reads, writes, waw)

    def memset(self, eng, ap, val, writes):
        self.op(eng, lambda e: e.memset(ap, val), (), writes)

    def scan(self, out, d0, d1, init, reads, writes):
        self.op('dve', lambda e: e.tensor_tensor_scan(out=out, data0=d0, data1=d1, initial=init,
                                                       op0=ALU.mult, op1=ALU.add), reads, writes)

    def emit(self):
        nc = self.nc
        ops = self.ops

        def replay(k, e):
            for waits, fn, sem, inc in ops[k]:
                for s, v in waits:
                    e.wait_ge(s, v)
                if fn is not None:
                    fn(e).then_inc(sem, inc)

        with nc.Block() as block:
            @block.tensor
            def _(e):
                replay('pe', e)

            @block.scalar
            def _(e):
                replay('act', e)

            @block.vector
            def _(e):
                replay('dve', e)

            @block.gpsimd
            def _(e):
                replay('pool', e)

            @block.sync
            def _(e):
                replay('sp', e)


def make_ident(P, st):
    idf = P.sb(st, "idf", [128, 128], F32)
    ident = P.sb(st, "ident", [128, 128], BF16)
    P.op('pool', lambda e: e.iota(idf[:], pattern=[[1, 128]], base=0, channel_multiplier=-1,
                                  allow_small_or_imprecise_dtypes=True), (), [idf])
    P.ts('dve', ident[:], idf[:], 0.0, None, ALU.is_equal, None, [idf], [ident])
    return ident


def emit_hT(P, st, x_d, nwb, ident, hT, hTv, xt, xn, ptr, small):
    eps, ss = small['eps'], small['ss']
    for tt in range(NT):
        xb = xt[tt % 2]
        xnb = xn[tt % 2]
        P.dma('sp', xb[:, 0:D], x_d.t[tt * 128:(tt + 1) * 128, :], (), [xb], owner=xb)
        s0 = ss[:, 2 * (tt % 2):2 * (tt % 2) + 1]
        s1 = ss[:, 2 * (tt % 2) + 1:2 * (tt % 2) + 2]
        P.act(xnb[:, 0:D], xb[:, 0:D], AF.Square, [xb], [xnb, ss], accum_out=s0)
        P.act(s1, s0, AF.Sqrt, [ss, eps], [ss], scale=1.0 / D, bias=eps[:])
        P.op('dve', lambda e, s1=s1: e.reciprocal(out=s1, in_=s1), [ss], [ss])
        P.stt(xnb[:, 0:D], xb[:, 0:D], s1, nwb[:, 0:D], ALU.mult, ALU.mult, [xb, ss, nwb], [xnb])
        for q in range(4):
            pt = ptr[q % 2]
            for j in range(4):
                kc = q * 4 + j
                P.tr(pt[:, j, :], xnb[:, kc * 128:(kc + 1) * 128], ident[:], [xnb, ident], [pt])
            eng = 'act' if tt % 2 == 0 else 'dve'
            P.cp(eng, hT[:, q * 4:(q + 1) * 4, tt * 128:(tt + 1) * 128], pt[:, :, :], [pt], [hTv[tt]])


_CACHE = {}


def _get(name, fn):
    if name not in _CACHE:
        _CACHE[name] = fn()
    return _CACHE[name]


def c_(a):
    return np.ascontiguousarray(a)


BIG = 30000.0
SCALE = 128 ** -0.5


def _bf16_split(a):
    import ml_dtypes
    a = np.asarray(a, np.float32)
    hi = a.astype(ml_dtypes.bfloat16).astype(np.float32)
    lo = (a - hi).astype(ml_dtypes.bfloat16).astype(np.float32)
    return hi, lo


def even_const_tables():
    tb = {}
    s_l = np.arange(128, dtype=np.float32)
    La = np.zeros((128, 16, 128), np.float32)
    for m in range(16):
        dj = -m
        La[0, m] = s_l
        La[2, m] = s_l
        La[1, m] = 64.0 * (2 * dj - 1)
        La[3, m] = 64.0 * (2 * dj - 1)
    tb['La'] = La.reshape(128, 16 * 128)
    n = np.arange(127, dtype=np.float32)
    Lc = np.zeros((128, 16, 127), np.float32)
    for i in range(16):
        Lc[0, i] = 16.0 * (n - 8 * i)
        Lc[2, i] = 16.0 * (n - 8 * i)
        Lc[1, i] = -33.0
        Lc[3, i] = -33.0
    tb['Lc'] = Lc.reshape(128, 16 * 127)
    slopes = (2.0 ** (-8.0 * np.arange(1, 17) / 16)).astype(np.float32)
    Ra = np.zeros((128, 4, 4, 128), np.float32)
    for gg in range(4):
        g = gg
        for h in range(4):
            hi, lo = _bf16_split(slopes[4 * g + h])
            Ra[0, gg, h] = hi
            Ra[1, gg, h] = hi
            Ra[2, gg, h] = lo
            Ra[3, gg, h] = lo
    tb['Ra'] = Ra.reshape(128, 4 * 512)
    Sel = np.zeros((128, 16, 127), np.float32)
    for i in range(16):
        k = np.clip(np.arange(127) - 8 * i + 64, 0, 127)
        Sel[k, i, np.arange(127)] = 1.0
    tb['Sel'] = Sel.reshape(128, 16 * 127)
    kk = np.arange(128)[:, None]
    tl = np.arange(128)[None, :]
    G = np.where(16 * (kk - 64) + 31 > tl, -BIG, 0.0).astype(np.float32)
    tb['G'] = np.tile(G[:, None, :], (1, 4, 1)).reshape(128, 512)
    tric = np.where(kk > tl, -BIG, 0.0).astype(np.float32)
    trib = np.where(kk <= tl, -BIG, 0.0).astype(np.float32)
    tb['TRIc'] = np.tile(tric[:, None, :], (1, 4, 1)).reshape(128, 512)
    tb['TRIb'] = np.tile(trib[:, None, :], (1, 4, 1)).reshape(128, 512)
    E = np.zeros((128, 16, 128), np.float32)
    for jt in range(16):
        E[2 * jt, jt, 0:64] = 1.0
        E[2 * jt + 1, jt, 64:128] = 1.0
    tb['E'] = E.reshape(128, 16 * 128)
    VAL = np.zeros((128, 16, 32), np.float32)
    ADD = np.zeros((128, 16, 32), np.float32)
    for i in range(16):
        for t_l in range(128):
            cur = (128 * i + t_l) // 64
            for j in range(32):
                if j == 0:
                    ADD[t_l, i, j] = 3e30
                elif j == cur:
                    ADD[t_l, i, j] = 2e30
                elif j == cur - 1:
                    ADD[t_l, i, j] = 1e30
                elif j <= cur:
                    VAL[t_l, i, j] = 1.0
                else:
                    ADD[t_l, i, j] = -1e30
    tb['VAL'] = VAL.reshape(128, 512)
    tb['ADD'] = ADD.reshape(128, 512)
    c_start = np.arange(127) * 16
    s_start = np.arange(32) * 64
    ov = ((c_start[:, None] <= s_start[None, :] + 63) & (c_start[:, None] + 31 >= s_start[None, :]))
    OV = np.zeros((128, 32), np.float32)
    OV[:127] = ov
    tb['OV'] = OV
    rst = np.ones((128, 512), np.float32)
    rst[:, 0::128] = 0.0
    tb['RST'] = rst
    cat = np.concatenate([tb[k] for k in ('La', 'Lc', 'Ra', 'Sel', 'G', 'TRIc', 'TRIb', 'E', 'OV')], axis=1)
    f32 = np.concatenate([tb[k] for k in ('VAL', 'ADD', 'RST')], axis=1)
    return c_(cat.astype(np.float32)), c_(f32)


CB_OFF = {}
_o = 0
for _k, _w in (('La', 2048), ('Lc', 16 * 127), ('Ra', 2048), ('Sel', 16 * 127), ('G', 512), ('TRIc', 512),
               ('TRIb', 512), ('E', 2048), ('OV', 32)):
    CB_OFF[_k] = _o
    _o += _w
CB_W = _o


def emit_hgrn2(P, s1, e_idx, hT, hTv, ident, cm, RST, eps, ptr, whg_d, lbl_d, gain_d, out_d, nh):
    f32b = lambda n: P.sb(s1, n, [128, 512], F32)
    bfb = lambda n: P.sb(s1, n, [128, 512], BF16)
    whg = [P.sb(s1, "whg%d" % i, [128, KC, 512], BF16) for i in range(2)]
    lbl = P.sb(s1, "lbl", [128, 16, 2], F32)
    lbt = P.sb(s1, "lbt", [128, 16, 4], F32)
    lb = P.sb(s1, "lb", [128, 16], F32)
    oml = P.sb(s1, "oml", [128, 16], F32)
    gainb = P.sb(s1, "gainb", [128, 2048], F32)
    qs, t1, t2, gg_, kk, b128, dd, d3 = [f32b("hg_f%d" % i) for i in range(8)]
    EA1, EA2, EB1, EB2, EQB, EKA, E3, E5 = [f32b("hg_e%d" % i) for i in range(8)]
    QA, QB, KA, KB, QOB, KOA, QP, KH = [bfb("hg_o%d" % i) for i in range(8)]
    vtok = [P.sb(s1, "vtok%d" % i, [128, 4, 128], BF16) for i in range(2)]
    ag = [P.sb(s1, "ag%d" % i, [128, 4, 128], F32) for i in range(2)]
    at4 = P.sb(s1, "at4", [128, 4, 128], BF16)
    kht4 = P.sb(s1, "kht4", [128, 4, 128], BF16)
    S5 = P.sb(s1, "S5", [128, 5, 128], F32)
    Sb5 = P.sb(s1, "Sb5", [128, 5, 128], BF16)
    ob4 = P.sb(s1, "ob4", [128, 4, 128], F32)
    junk4 = P.sb(s1, "junk4", [128, 4, 128], F32)
    yb4 = P.sb(s1, "yb4", [128, 4, 128], BF16)
    sst4 = P.sb(s1, "sst4", [128, 8], F32)
    yaT = [P.sb(s1, "yaT%d" % i, [128, T], BF16) for i in range(2)]
    pq = [P.ps(s1, "pq%d" % i, [128, 512], F32) for i in range(2)]
    pv = P.ps(s1, "pv", [128, 2, 512], F32)
    patt = P.ps(s1, "patt", [128, 4, 128], F32)
    po = P.ps(s1, "po", [128, 4, 128], F32)
    pS = P.ps(s1, "pS", [128, 4, 128], F32)

    P.dma('sp', lbl[:], lbl_d.t.ap().rearrange("p (h e) -> p h e", e=2), (), [lbl], owner=lbl)
    P.dma('sp', gainb[:], gain_d.t.ap().partition_broadcast(128), (), [gainb], owner=gainb)
    P.tt('dve', lbt[:, :, 0], lbl[:, :, 0], lbl[:, :, 1], ALU.max, [lbl], [lbt])
    P.tt('dve', lbt[:, :, 1], lbl[:, :, 0], lbt[:, :, 0], ALU.subtract, [lbl, lbt], [lbt])
    P.tt('dve', lbt[:, :, 2], lbl[:, :, 1], lbt[:, :, 0], ALU.subtract, [lbl, lbt], [lbt])
    P.act(lbt[:, :, 1:3], lbt[:, :, 1:3], AF.Exp, [lbt], [lbt])
    P.tt('dve', lbt[:, :, 3], lbt[:, :, 1], lbt[:, :, 2], ALU.add, [lbt], [lbt])
    P.op('dve', lambda e: e.reciprocal(out=lbt[:, :, 3], in_=lbt[:, :, 3]), [lbt], [lbt])
    P.tt('dve', lbt[:, :, 1], lbt[:, :, 1], lbt[:, :, 3], ALU.mult, [lbt], [lbt])
    P.tt('dve', lbt[:, :, 2], lbt[:, :, 2], lbt[:, :, 3], ALU.mult, [lbt], [lbt])
    if e_idx == 0:
        P.tt('dve', lb[:], lbt[:, :, 1], lbt[:, :, 1], ALU.subtract, [lbt], [lb])
    else:
        P.tt('dve', lbt[:, :, 3], lbt[:, :, 1], lbt[:, :, 2], ALU.add, [lbt], [lbt])
        P.tt('dve', lb[:], lbt[:, :, 3], lbt[:, :, 1], ALU.subtract, [lbt], [lb])
    P.ts('dve', oml[:], lb[:], -1.0, 1.0, ALU.mult, ALU.add, [lb], [oml])
    for b_ in (EA1, EA2, EB1, EB2, EQB, EKA):
        P.memset('pool', b_[:], 0.0, [b_])

    def v4(buf):
        return buf[:].rearrange("p (j c) -> p j c", c=128)

    def load_w(hd):
        w = whg[hd % 2]
        P.dma('pool', w[:], whg_d.t[:, hd * 512:(hd + 1) * 512].rearrange("(kc p) c -> p kc c", p=128),
              (), [w], owner=w)

    pt0, pt1 = ptr
    steps = [(hd, tb) for hd in range(nh) for tb in range(4)]

    def Pm(k):
        hd, tb = steps[k]
        w = whg[hd % 2]
        if tb == 0 and hd + 1 < nh:
            load_w(hd + 1)
        hts = hTv[4 * tb:4 * tb + 4]
        tsl = slice(tb * 512, (tb + 1) * 512)
        for kc in range(KC):
            P.mm(pq[0][:], w[:, kc, 0:128], hT[:, kc, tsl], kc == 0, kc == KC - 1, [w] + hts, [pq[0]])
        for kc in range(KC):
            P.mm(pq[1][:], w[:, kc, 128:256], hT[:, kc, tsl], kc == 0, kc == KC - 1, [w] + hts, [pq[1]])
        for j in range(4):
            for kc in range(KC):
                P.mm(pv[:, j // 2, (j % 2) * 256:(j % 2 + 1) * 256],
                     hT[:, kc, tb * 512 + j * 128:tb * 512 + (j + 1) * 128],
                     w[:, kc, 256:512], kc == 0, kc == KC - 1, [w, hts[j]], [pv])

    def Pe(k):
        vt, agk = vtok[k % 2], ag[k % 2]
        P.act(qs[:], pq[0][:], AF.Silu, [pq[0]], [qs])
        P.act(t1[:], pq[1][:], AF.Exp, [pq[1]], [t1], scale=-1.0)
        pv4 = pv[:].rearrange("p b (t c) -> p (b t) c", c=256)
        P.cp('dve', vt[:], pv4[:, :, 0:128], [pv], [vt])
        P.act(agk[:], pv4[:, :, 128:256], AF.Silu, [pv], [agk])

    def E(k):
        hd, tb = steps[k]
        P.act(t1[:], t1[:], AF.Ln, [t1], [t1], bias=1.0)
        P.act(t1[:], t1[:], AF.Exp, [t1], [t1], scale=-1.0)
        P.ts('dve', t2[:], t1[:], oml[:, hd:hd + 1], lb[:, hd:hd + 1], ALU.mult, ALU.add,
             [t1, oml, lb], [t2])
        P.ts('dve', t2[:], t2[:], 1e-30, None, ALU.max, None, [t2], [t2])
        P.act(gg_[:], t2[:], AF.Ln, [t2], [gg_])
        P.ts('dve', kk[:], t2[:], -1.0, 1.0, ALU.mult, ALU.add, [t2], [kk])
        P.scan(b128[:], RST, gg_[:], 0.0, [gg_], [b128])
        bv = v4(b128)
        dv_ = v4(dd)
        P.tt('dve', dv_[:, :, 0:64], bv[:, :, 0:64], bv[:, :, 31:32].to_broadcast([128, 4, 64]),
             ALU.subtract, [b128], [dd])
        P.tt('dve', dv_[:, :, 64:128], bv[:, :, 64:128], bv[:, :, 95:96].to_broadcast([128, 4, 64]),
             ALU.subtract, [b128], [dd])
        P.act(v4(EA1)[:, :, 0:64], dv_[:, :, 0:64], AF.Exp, [dd], [EA1])
        P.act(v4(EA2)[:, :, 0:64], dv_[:, :, 0:64], AF.Exp, [dd], [EA2], scale=-1.0)
        P.act(v4(EB1)[:, :, 64:128], dv_[:, :, 64:128], AF.Exp, [dd], [EB1])
        P.act(v4(EB2)[:, :, 64:128], dv_[:, :, 64:128], AF.Exp, [dd], [EB2], scale=-1.0)
        d3v = v4(d3)
        P.tt('dve', d3v[:, :, :], bv[:, :, :], bv[:, :, 63:64].to_broadcast([128, 4, 128]),
             ALU.subtract, [b128], [d3])
        P.act(v4(EQB)[:, :, 64:128], d3v[:, :, 64:128], AF.Exp, [d3], [EQB])
        P.act(v4(EKA)[:, :, 0:64], d3v[:, :, 0:64], AF.Exp, [d3], [EKA], scale=-1.0)
        P.act(E3[:], b128[:], AF.Exp, [b128], [E3])
        P.tt('dve', d3v[:, :, :], bv[:, :, :], bv[:, :, 127:128].to_broadcast([128, 4, 128]),
             ALU.subtract, [b128], [d3])
        P.act(E5[:], d3[:], AF.Exp, [d3], [E5], scale=-1.0)
        P.tt('dve', QA[:], qs[:], EA1[:], ALU.mult, [qs, EA1], [QA])
        P.tt('dve', QB[:], qs[:], EB1[:], ALU.mult, [qs, EB1], [QB])
        P.tt('dve', KA[:], kk[:], EA2[:], ALU.mult, [kk, EA2], [KA])
        P.tt('dve', KB[:], kk[:], EB2[:], ALU.mult, [kk, EB2], [KB])
        P.tt('pool', QOB[:], qs[:], EQB[:], ALU.mult, [qs, EQB], [QOB])
        P.tt('pool', KOA[:], kk[:], EKA[:], ALU.mult, [kk, EKA], [KOA])
        P.tt('pool', QP[:], qs[:], E3[:], ALU.mult, [qs, E3], [QP])
        P.tt('pool', KH[:], kk[:], E5[:], ALU.mult, [kk, E5], [KH])

    def L(k):
        hd, tb = steps[k]
        vt, agk = vtok[k % 2], ag[k % 2]
        yT = yaT[hd % 2]
        if tb == 0:
            P.memset('dve', S5[:, 0, :], 0.0, [S5])
            P.memset('dve', Sb5[:, 0, :], 0.0, [Sb5])
        cs_ = [slice(j * 128, (j + 1) * 128) for j in range(4)]
        for j in range(4):
            P.mm(patt[:, j, :], KA[:, cs_[j]], QA[:, cs_[j]], True, False, [KA, QA], [patt])
            P.mm(patt[:, j, :], KB[:, cs_[j]], QB[:, cs_[j]], False, False, [KB, QB], [patt])
            P.mm(patt[:, j, :], KOA[:, cs_[j]], QOB[:, cs_[j]], False, True, [KOA, QOB], [patt])
        P.tt('dve', at4[:], patt[:], cm[:].unsqueeze(1).to_broadcast([128, 4, 128]), ALU.mult, [patt, cm], [at4])
        for j in range(4):
            P.tr(pt0[:, j, :], KH[:, cs_[j]], ident[:], [KH, ident], [pt0])
        P.cp('act', kht4[:], pt0[:, 0:4, :], [pt0], [kht4])
        for j in range(4):
            P.mm(pS[:, j, :], kht4[:, j, :], vt[:, j, :], True, True, [kht4, vt], [pS])
        for j in range(4):
            P.stt(S5[:, j + 1, :], S5[:, j, :], E3[:, (j + 1) * 128 - 1:(j + 1) * 128], pS[:, j, :],
                  ALU.mult, ALU.add, [S5, E3, pS], [S5])
        P.cp('act', Sb5[:, 1:5, :], S5[:, 1:5, :], [S5], [Sb5])
        for j in range(4):
            P.mm(po[:, j, :], at4[:, j, :], vt[:, j, :], True, False, [at4, vt], [po])
            P.mm(po[:, j, :], QP[:, cs_[j]], Sb5[:, j, :], False, True, [QP, Sb5], [po])
        P.cp('pool', S5[:, 0, :], S5[:, 4, :], [S5], [S5])
        P.cp('pool', Sb5[:, 0, :], Sb5[:, 4, :], [Sb5], [Sb5])
        P.cp('dve', ob4[:], po[:], [po], [ob4])
        P.act(junk4[:], ob4[:], AF.Square, [ob4], [junk4])
        P.op('dve', lambda e: e.tensor_reduce(out=sst4[:, 0:4], in_=junk4[:], axis=mybir.AxisListType.X,
                                              op=ALU.add), [junk4], [sst4])
        P.act(sst4[:, 4:8], sst4[:, 0:4], AF.Sqrt, [sst4, eps], [sst4], scale=1.0 / 128, bias=eps[:])
        P.op('dve', lambda e: e.reciprocal(out=sst4[:, 4:8], in_=sst4[:, 4:8]), [sst4], [sst4])
        P.tt('dve', ob4[:], ob4[:], sst4[:, 4:8].unsqueeze(2).to_broadcast([128, 4, 128]), ALU.mult,
             [ob4, sst4], [ob4])
        P.tt('dve', ob4[:], ob4[:],
             gainb[:, hd * 128:(hd + 1) * 128].unsqueeze(1).to_broadcast([128, 4, 128]), ALU.mult,
             [ob4, gainb], [ob4])
        P.tt('dve', yb4[:], ob4[:], agk[:], ALU.mult, [ob4, agk], [yb4])
        for j in range(4):
            P.tr(pt1[:, 4 + j, :], yb4[:, j, :], ident[:], [yb4, ident], [pt1])
        P.cp('act', yT[:, tb * 512:(tb + 1) * 512].rearrange("p (j c) -> p j c", c=128), pt1[:, 4:8, :], [pt1], [yT])
        if tb == 3:
            P.dma('sp', out_d.t[hd * 128:(hd + 1) * 128, :], yT[:], [yT], (), owner=yT)

    load_w(0)
    Pm(0)
    Pe(0)
    for k in range(len(steps)):
        if k + 1 < len(steps):
            Pm(k + 1)
        E(k)
        if k + 1 < len(steps):
            Pe(k + 1)
        L(k)


def emit_nsa(P, s2, hT, hTv, ident, ptr, VAL, ADD, wnq_d, wnkv_d, wngt_d, wnbg_d, w1k_d, w1v_d,
             w2k_d, w2v_d, pek_d, pev_d, cb_d, out_d, ng=4, nqt=NT):
    cf_reads = []
    cb = P.sb(s2, "cb", [128, CB_W], BF16)
    P.dma('pool', cb[:], cb_d.t.ap(), (), [cb], owner=cb)

    def CB(name, lo, hi):
        return cb[:, CB_OFF[name] + lo:CB_OFF[name] + hi]

    wbuf = P.sb(s2, "wbuf", [128, KC, 768], BF16)
    wgt = P.sb(s2, "wgt", [128, KC, 12], BF16)
    w1k = P.sb(s2, "w1k", [128, 32, 128], BF16)
    w1v = P.sb(s2, "w1v", [128, 32, 128], BF16)
    w2k = P.sb(s2, "w2k", [128, 128], BF16)
    w2v = P.sb(s2, "w2v", [128, 128], BF16)
    pek = P.sb(s2, "pek", [128, 32], BF16)
    pev = P.sb(s2, "pev", [128, 32], BF16)
    P.dma('pool', w1k[:], w1k_d.t.ap().rearrange("(j d) o -> d j o", d=128), (), [w1k], owner=w1k)
    P.dma('pool', w1v[:], w1v_d.t.ap().rearrange("(j d) o -> d j o", d=128), (), [w1v], owner=w1v)
    P.dma('pool', w2k[:], w2k_d.t.ap(), (), [w2k], owner=w2k)
    P.dma('pool', w2v[:], w2v_d.t.ap(), (), [w2v], owner=w2v)
    P.dma('pool', pek[:], pek_d.t.ap(), (), [pek], owner=pek)
    P.dma('pool', pev[:], pev_d.t.ap(), (), [pev], owner=pev)

    qT = P.sb(s2, "qT", [128, 4, T], BF16)
    kcT = P.sb(s2, "kcT", [128, T], BF16)
    vcT = P.sb(s2, "vcT", [128, T], BF16)
    ksT = P.sb(s2, "ksT", [128, T], BF16)
    kwT = P.sb(s2, "kwT", [128, T], BF16)
    vsA = P.sb(s2, "vsA", [128, NT, 129], BF16)
    vwA = P.sb(s2, "vwA", [128, NT, 129], BF16)
    hidk = P.sb(s2, "hidk", [128, 127], BF16)
    hidv = P.sb(s2, "hidv", [128, 127], BF16)
    cbias = P.sb(s2, "cbias", [128, 2], F32)
    KcT = P.sb(s2, "KcT", [128, 127], BF16)
    VcA = P.sb(s2, "VcA", [128, 161], BF16)
    Rsel = P.sb(s2, "Rsel", [128, 4, 128], BF16)
    Pt = [P.sb(s2, "Pt%d" % i, [128, 4, 128], BF16) for i in range(3)]
    gts = P.sb(s2, "gts", [128, 12], F32)
    sgate = P.sb(s2, "sgate", [128, 512], F32)
    zz = P.sb(s2, "zz", [128, 3, 4], F32)
    cs = P.sb(s2, "cs", [128, 3, 4], F32)
    acc = P.sb(s2, "acc", [128, 4, 128], F32)
    imp = P.sb(s2, "imp", [128, 32], F32)
    score = P.sb(s2, "score", [128, 32], F32)
    work = P.sb(s2, "work", [128, 32], F32)
    m8 = P.sb(s2, "m8", [128, 16], F32)
    nm = P.sb(s2, "nm", [128, 32], F32)
    ybf = P.sb(s2, "ybf", [128, 512], BF16)
    yTt = [P.sb(s2, "yTt%d" % i, [128, 4, 128], BF16) for i in range(2)]
    psA = [P.ps(s2, "psA%d" % i, [128, 4, 128], F32) for i in range(2)]
    ident, identf = ident
    psO4 = P.ps(s2, "psO", [128, 4, 512], F32)
    psO = [psO4] * 4
    ocp = P.sb(s2, "ocp", [128, 4, 161], F32)
    impm = P.sb(s2, "impm", [128, 4, 32], F32)
    tmpo = P.sb(s2, "tmpo", [128, 4, 128], F32)
    psM = P.ps(s2, "psM", [128, 512], F32)
    psG = psM

    P.memset('dve', vsA[:, :, 128:129], 1.0, [vsA])
    P.memset('dve', vwA[:, :, 128:129], 1.0, [vwA])
    P.memset('dve', Rsel[:], 0.0, [Rsel])
    P.memset('dve', VcA[:], 0.0, [VcA])
    P.memset('dve', VcA[:, 128:129], 1.0, [VcA])
    P.cp('dve', VcA[:, 129:161], CB('OV', 0, 32), [cb], [VcA])

    def Oh(h):
        return psO4[:, h, 0:256]

    npa = [0]
    npt = [1]

    def nextA():
        p = psA[npa[0] % 2]
        npa[0] += 1
        return p

    for gg in range(ng):
        Ra = CB('Ra', gg * 512, (gg + 1) * 512)
        P.dma('pool', wbuf[:, :, 0:512], wnq_d.t[:, gg * 512:(gg + 1) * 512].rearrange("(kc p) c -> p kc c", p=128),
              (), [wbuf], owner=wbuf)
        for h in range(4):
            for tb in range(4):
                pp = nextA()
                ppf = pp[:].rearrange("p a b -> p (a b)")
                for kc in range(KC):
                    P.mm(ppf, wbuf[:, kc, h * 128:(h + 1) * 128], hT[:, kc, tb * 512:(tb + 1) * 512],
                         kc == 0, kc == KC - 1, [wbuf] + hTv[4 * tb:4 * tb + 4], [pp])
                P.act(qT[:, h, tb * 512:(tb + 1) * 512], ppf, AF.Copy, [pp], [qT], scale=SCALE)
        P.dma('pool', wbuf[:], wnkv_d.t[:, gg * 768:(gg + 1) * 768].rearrange("(kc p) c -> p kc c", p=128),
              (), [wbuf], owner=wbuf)
        P.dma('pool', wgt[:], wngt_d.t[:, gg * 12:(gg + 1) * 12].rearrange("(kc p) c -> p kc c", p=128),
              (), [wgt], owner=wgt)
        for (dst, col) in ((kcT, 0), (vcT, 128), (ksT, 256), (kwT, 384)):
            for tb in range(4):
                pp = nextA()
                ppf = pp[:].rearrange("p a b -> p (a b)")
                for kc in range(KC):
                    P.mm(ppf, wbuf[:, kc, col:col + 128], hT[:, kc, tb * 512:(tb + 1) * 512],
                         kc == 0, kc == KC - 1, [wbuf] + hTv[4 * tb:4 * tb + 4], [pp])
                P.cp('dve' if tb % 2 else 'act', dst[:, tb * 512:(tb + 1) * 512], ppf, [pp], [dst])
        for tt in range(NT):
            pp = nextA()
            ppf = pp[:].rearrange("p a b -> p (a b)")
            for kc in range(KC):
                P.mm(ppf[:, 0:256], hT[:, kc, tt * 128:(tt + 1) * 128], wbuf[:, kc, 512:768],
                     kc == 0, kc == KC - 1, [wbuf, hTv[tt]], [pp])
            eng = 'dve' if tt % 2 else 'act'
            P.cp(eng, vsA[:, tt, 0:128], ppf[:, 0:128], [pp], [vsA])
            P.cp(eng, vwA[:, tt, 0:128], ppf[:, 128:256], [pp], [vwA])
        P.dma('pool', wbuf[:, :, 0:512], wnbg_d.t[:, gg * 512:(gg + 1) * 512].rearrange("(kc p) c -> p kc c", p=128),
              (), [wbuf], owner=wbuf)
        for (src, w1, w2, pe, hid, col) in ((kcT, w1k, w2k, pek, hidk, 0), (vcT, w1v, w2v, pev, hidv, 1)):
            srcv = src[:].rearrange("p (n r) -> p n r", r=16)
            for j in range(32):
                P.mm(psM[:, 0:127], w1[:, j, :], srcv[:, j // 16:j // 16 + 127, j % 16], j == 0, j == 31,
                     [w1, src], [psM])
            for j in range(32):
                P.mm(psM[:, 128:129], w1[:, j, :], pe[:, j:j + 1], j == 0, j == 31, [w1, pe], [psM])
            P.cp('dve', cbias[:, col:col + 1], psM[:, 128:129], [psM], [cbias])
            P.act(hid[:], psM[:, 0:127], AF.Silu, [psM, cbias], [hid], bias=cbias[:, col:col + 1])
        P.mm(psM[:, 256:383], w2k[:], hidk[:], True, True, [w2k, hidk], [psM])
        P.cp('dve', KcT[:], psM[:, 256:383], [psM], [KcT])
        P.mm(psM[0:127, 384:512], hidv[:], w2v[:], True, True, [hidv, w2v], [psM])
        P.cp('dve', VcA[0:127, 0:128], psM[0:127, 384:512], [psM], [VcA])

        for i in range(nqt):
            isl = slice(i * 128, (i + 1) * 128)
            qrhs = qT[:, :, isl]
            for kc in range(KC):
                P.mm(psG[:], hT[:, kc, isl], wbuf[:, kc, 0:512], kc == 0, kc == KC - 1, [wbuf, hTv[i]], [psG])
            P.act(sgate[:], psG[:], AF.Silu, [psG], [sgate])
            for kc in range(KC):
                P.mm(psM[:, 0:12], hT[:, kc, isl], wgt[:, kc, :], kc == 0, kc == KC - 1, [wgt, hTv[i]], [psM])
            P.act(gts[:], psM[:, 0:12], AF.Sigmoid, [psM], [gts])
            gv = gts[:].rearrange("p (h j) -> p j h", j=3)

            def finish(br, width):
                P.cp('dve', ocp[:, :, 0:width], psO4[:, :, 0:width], [psO4], [ocp])
                P.ts('dve', zz[:, br, :], ocp[:, :, 128], 1e-37, None, ALU.max, None, [ocp], [zz])
                P.op('dve', lambda e, br=br: e.reciprocal(out=zz[:, br, :], in_=zz[:, br, :]), [zz], [zz])
                if br == 0:
                    P.tt('dve', impm[:], ocp[:, :, 129:161], zz[:, 0, :].unsqueeze(2).to_broadcast([128, 4, 32]),
                         ALU.mult, [ocp, zz], [impm])
                    P.op('dve', lambda e: e.tensor_reduce(out=imp[:], in_=impm[:].rearrange("p h j -> p j h"),
                                                          axis=mybir.AxisListType.X, op=ALU.add), [impm], [imp])
                P.tt('dve', cs[:, br, :], zz[:, br, :], gv[:, br, :], ALU.mult, [zz, gts], [cs])
                if br == 0:
                    P.tt('dve', acc[:], ocp[:, :, 0:128], cs[:, br, :].unsqueeze(2).to_broadcast([128, 4, 128]),
                         ALU.mult, [ocp, cs], [acc])
                else:
                    P.tt('dve', tmpo[:], ocp[:, :, 0:128], cs[:, br, :].unsqueeze(2).to_broadcast([128, 4, 128]),
                         ALU.mult, [ocp, cs], [tmpo])
                    P.tt('dve', acc[:], acc[:], tmpo[:], ALU.add, [acc, tmpo], [acc])

            pp = nextA()
            P.mm(pp[0:127, :, :], KcT[:], qrhs, True, False, [KcT, qT], [pp])
            P.mm(pp[0:127, :, :], CB('Lc', i * 127, (i + 1) * 127), Ra.rearrange("p (a b) -> p a b", b=128),
                 False, False, [cb], [pp])
            P.mm(pp[0:127, :, :], CB('Sel', i * 127, (i + 1) * 127),
                 CB('G', 0, 512).rearrange("p (a b) -> p a b", b=128), False, True, [cb], [pp])
            pt_ = Pt[0]
            P.act(pt_[0:127, :, :], pp[0:127, :, :], AF.Exp, [pp], [pt_])
            for h in range(4):
                P.mm(Oh(h)[:, 0:161], pt_[0:127, h, :], VcA[0:127, :], True, True, [pt_, VcA], [psO[h]])
            finish(0, 161)
            need_sel = (2 * i + 2) > 16
            if need_sel:
                P.tt('dve', score[:], imp[:], VAL[:, i * 32:(i + 1) * 32], ALU.mult, [imp], [score])
                P.tt('dve', score[:], score[:], ADD[:, i * 32:(i + 1) * 32], ALU.add, [score], [score])
                P.op('dve', lambda e: e.max(out=m8[:, 0:8], in_=score[:]), [score], [m8])
                P.op('dve', lambda e: e.match_replace(out=work[:], in_to_replace=m8[:, 0:8], in_values=score[:],
                                                      imm_value=-3e38), [score, m8], [work])
                P.op('dve', lambda e: e.max(out=m8[:, 8:16], in_=work[:]), [work], [m8])
                P.ts('dve', nm[:], score[:], m8[:, 15:16], -BIG, ALU.is_lt, ALU.mult, [score, m8], [nm])
                P.tr(psM[0:32, 128:256], nm[:], identf[:], [nm, identf], [psM])
                P.cp('dve', Rsel[0:32, :, :], psM[0:32, 128:256].rearrange("p (a b) -> p a b", a=1).to_broadcast([32, 4, 128]),
                     [psM], [Rsel])

            for br, (kT_, vA_) in ((2, (kwT, vwA)), (1, (ksT, vsA))):
                jlo = 0 if br == 1 else max(0, i - 4)

                def emit_qk(jt, br=br, kT_=kT_, jlo=jlo):
                    dj = jt - i
                    pp = nextA()
                    extra = []
                    if br == 1 and need_sel and jt != i:
                        extra.append((CB('E', jt * 128, (jt + 1) * 128), Rsel[:], [cb, Rsel]))
                    if jt == i:
                        extra.append((ident[:], CB('TRIc', 0, 512).rearrange("p (a b) -> p a b", b=128), [ident, cb]))
                    if br == 2 and jt == i - 4:
                        extra.append((ident[:], CB('TRIb', 0, 512).rearrange("p (a b) -> p a b", b=128), [ident, cb]))
                    P.mm(pp[:], kT_[:, jt * 128:(jt + 1) * 128], qrhs, True, False, [kT_, qT], [pp])
                    P.mm(pp[:], CB('La', (-dj) * 128, (-dj + 1) * 128), Ra.rearrange("p (a b) -> p a b", b=128),
                         False, len(extra) == 0, [cb], [pp])
                    for xi, (l_, r_, rd) in enumerate(extra):
                        P.mm(pp[:], l_, r_, False, xi == len(extra) - 1, rd, [pp])
                    return pp

                def emit_pv(jt, pp, vA_=vA_, jlo=jlo):
                    pt_ = Pt[npt[0] % 3]
                    npt[0] += 1
                    P.act(pt_[:], pp[:], AF.Exp, [pp], [pt_])
                    for h in range(4):
                        P.mm(Oh(h)[:, 0:129], pt_[:, h, :], vA_[:, jt, :], jt == jlo, jt == i,
                             [pt_, vA_], [psO[h]])

                pend = None
                for jt in range(jlo, i + 1):
                    pp = emit_qk(jt)
                    if pend is not None:
                        emit_pv(*pend)
                    pend = (jt, pp)
                emit_pv(*pend)
                finish(br, 129)
            P.tt('dve', ybf[:], acc[:].rearrange("p a b -> p (a b)"), sgate[:], ALU.mult, [acc, sgate], [ybf])
            ptb = ptr
            for h in range(4):
                P.tr(ptb[:, h, :], ybf[:, h * 128:(h + 1) * 128], ident[:], [ybf, ident], [ptb])
            yt = yTt[i % 2]
            P.cp('act', yt[:], ptb[:], [ptb], [yt])
            P.dma('sp', out_d.t[2048 + gg * 512:2048 + (gg + 1) * 512, isl].rearrange("(h c) t -> c h t", c=128),
                  yt[:], [yt], (), owner=yt)


EV_OFF = dict(a_q=0, a_f=2048, a_i=4096, a_g=6144, b_q=8192, b_kc=10240, b_vc=10752, b_ks=11264, b_vs=11776,
              b_kw=12288, b_vw=12800, b_gate=13312, b_g=13360)


def emit_even_A(P, C, e_idx, x_src, nw_row, W, out_d):
    ident, identf, cm, eps, cf = C['ident'], C['identf'], C['cm'], C['eps'], C['cf']
    VAL = cf[:, 0:512]
    ADD = cf[:, 512:1024]
    RST = cf[:, 1024:1536]
    with ExitStack() as sa:
        hT = P.sb(sa, "hT", [128, KC, T], BF16)
        hTv = [P.view(hT, "hT%d" % i) for i in range(NT)]
        ss = P.sb(sa, "ss", [128, 4], F32)
        with ExitStack() as s0:
            xt = [P.sb(s0, "xt%d" % i, [128, D], F32) for i in range(2)]
            xn = [P.sb(s0, "xn%d" % i, [128, D], BF16) for i in range(2)]
            nwb = P.sb(s0, "nwb", [128, D], F32)
            ptr = [P.ps(s0, "ptr%d" % i, [128, 4, 128], BF16) for i in range(2)]
            P.dma('sp', nwb[:], nw_row.partition_broadcast(128), (), [nwb], owner=nwb)
            emit_hT(P, s0, x_src, nwb, ident, hT, hTv, xt, xn, ptr, {'eps': eps, 'ss': ss})
            P.end_section()
        with ExitStack() as s1:
            ptb = P.ps(s1, "ptrh", [128, 8, 128], BF16)
            emit_hgrn2(P, s1, e_idx, hT, hTv, ident, cm, RST, eps, (ptb, P.view(ptb)), W['whg'], W['lbl'], W['gain'],
                       out_d, 16)
            P.end_section()
        with ExitStack() as s2:
            ptr = P.ps(s2, "ptrn", [128, 4, 128], BF16)
            emit_nsa(P, s2, hT, hTv, (ident, identf), ptr, VAL, ADD, W['wnq'], W['wnkv'], W['wngt'], W['wnbg'],
                     W['w1k'], W['w1v'], W['w2k'], W['w2v'], W['pekT'], W['pevT'], C['cb_d'], out_d, 4, NT)
            P.end_section()


def emit_odd_A(P, C, x_src, nw_row, W, out_d):
    ident, eps = C['ident'], C['eps']
    NBLK = 10
    NCT = 20
    wx_d, wg_d, wa_d, wi_d, vec_d = W['wx'], W['wg'], W['wa'], W['wi'], W['vec']
    with ExitStack() as st:
        hT = P.sb(st, "hT", [128, KC, T], BF16)
        hTv = [P.view(hT, "hT%d" % i) for i in range(NT)]
        Tb = [P.sb(st, "T%d" % i, [128, T], F32) for i in range(4)]
        Bb = [P.sb(st, "B%d" % i, [128, T], BF16) for i in range(2)]
        xraw = [P.sb(st, "xraw%d" % i, [128, T + 3], F32) for i in range(2)]
        xc = [P.sb(st, "xc%d" % i, [128, T], F32) for i in range(2)]
        mix = [P.sb(st, "mix%d" % i, [128, T], BF16) for i in range(2)]
        wxs = [P.sb(st, "wxs%d" % i, [128, KC, 256], BF16) for i in range(2)]
        wgs = [P.sb(st, "wgs%d" % i, [128, KC, 256], BF16) for i in range(2)]
        was = [P.sb(st, "was%d" % i, [128, 2, 256], BF16) for i in range(2)]
        wis = [P.sb(st, "wis%d" % i, [128, 2, 256], BF16) for i in range(2)]
        vec = P.sb(st, "vec", [128, NCT, 8], F32)
        cl = P.sb(st, "cl", [128, NCT], F32)
        ss = P.sb(st, "ss", [128, 4], F32)
        ptr = [P.ps(st, "ptr%d" % i, [128, 4, 128], BF16) for i in range(2)]
        pj = [P.ps(st, "pj%d" % i, [128, 512], F32) for i in range(2)]
        pg = [P.ps(st, "pg%d" % i, [128, 512], F32) for i in range(2)]

        nwb = xraw[0]
        P.dma('sp', nwb[:, 0:D], nw_row.partition_broadcast(128), (), [nwb], owner=nwb)
        P.dma('sp', vec[:], vec_d.t.ap().rearrange("p (c k) -> p c k", k=8), (), [vec], owner=vec)
        emit_hT(P, st, x_src, nwb, ident, hT, hTv, Tb[0:2], Bb, ptr, {'eps': eps, 'ss': ss})
        P.act(cl[:], vec[:, :, 7], AF.Exp, [vec], [cl], scale=-1.0)
        P.act(cl[:], cl[:], AF.Ln, [cl], [cl], bias=1.0)
        P.ts('dve', cl[:], cl[:], -8.0, None, ALU.mult, None, [cl], [cl])
        for i in range(2):
            P.memset('dve', xraw[i][:, 0:3], 0.0, [xraw[i]])

        def load_w(n):
            P.dma('pool', wxs[n % 2][:], wx_d.t[:, n * 256:(n + 1) * 256].rearrange("(kc p) c -> p kc c", p=128),
                  (), [wxs[n % 2]], owner=wxs[n % 2])
            P.dma('pool', wgs[n % 2][:], wg_d.t[:, n * 256:(n + 1) * 256].rearrange("(kc p) c -> p kc c", p=128),
                  (), [wgs[n % 2]], owner=wgs[n % 2])
            P.dma('pool', was[n % 2][:], wa_d.t[n].rearrange("(dt p) e -> p dt e", p=128), (), [was[n % 2]],
                  owner=was[n % 2])
            P.dma('pool', wis[n % 2][:], wi_d.t[n].rearrange("(dt p) e -> p dt e", p=128), (), [wis[n % 2]],
                  owner=wis[n % 2])

        npj = 0
        load_w(0)
        for n in range(NBLK):
            w_x, w_g, w_a, w_i = wxs[n % 2], wgs[n % 2], was[n % 2], wis[n % 2]
            if n + 1 < NBLK:
                load_w(n + 1)
            for hf in range(2):
                ct = 2 * n + hf
                xr = xraw[hf]
                for tb in range(4):
                    pp = pj[npj % 2]
                    npj += 1
                    for kc in range(KC):
                        P.mm(pp[:], w_x[:, kc, hf * 128:(hf + 1) * 128], hT[:, kc, tb * 512:(tb + 1) * 512],
                             kc == 0, kc == KC - 1, [w_x] + hTv[4 * tb:4 * tb + 4], [pp])
                    P.cp('act', xr[:, 3 + tb * 512:3 + (tb + 1) * 512], pp[:], [pp], [xr])
                c = xc[hf]
                P.ts('dve', c[:], xr[:, 3:3 + T], vec[:, ct, 3:4], vec[:, ct, 4:5], ALU.mult, ALU.add,
                     [xr, vec], [c])
                for j in range(3):
                    P.stt(c[:], xr[:, j:j + T], vec[:, ct, j:j + 1], c[:], ALU.mult, ALU.add, [xr, vec, c], [c])
                P.cp('pool', Bb[hf][:], c[:], [c], [Bb[hf]])
            for eh in range(2):
                ct = 2 * n + eh
                R, I, S, G = Tb
                for tb in range(4):
                    pp = pj[npj % 2]
                    npj += 1
                    for kc in range(KC):
                        P.mm(pp[:], w_g[:, kc, eh * 128:(eh + 1) * 128], hT[:, kc, tb * 512:(tb + 1) * 512],
                             kc == 0, kc == KC - 1, [w_g] + hTv[4 * tb:4 * tb + 4], [pp])
                    P.act(G[:, tb * 512:(tb + 1) * 512], pp[:], AF.Silu, [pp], [G])
                for (dst, wsb, bcol) in ((R, w_a, 5), (I, w_i, 6)):
                    for tb in range(4):
                        pp = pg[npj % 2]
                        npj += 1
                        for dt_ in range(2):
                            P.mm(pp[:], wsb[:, dt_, eh * 128:(eh + 1) * 128],
                                 Bb[dt_][:, tb * 512:(tb + 1) * 512], dt_ == 0, dt_ == 1, [wsb, Bb[dt_]], [pp])
                        P.act(dst[:, tb * 512:(tb + 1) * 512], pp[:], AF.Sigmoid, [pp, vec], [dst],
                              bias=vec[:, ct, bcol:bcol + 1])
                P.act(R[:], R[:], AF.Exp, [R, cl], [R], scale=cl[:, ct:ct + 1])
                P.tt('pool', S[:], R[:], R[:], ALU.mult, [R], [S])
                P.act(S[:], S[:], AF.Sqrt, [S], [S], scale=-1.0, bias=1.0)
                P.tt('dve', I[:], I[:], xc[eh][:], ALU.mult, [I, xc[eh]], [I])
                P.tt('dve', I[:], I[:], S[:], ALU.mult, [I, S], [I])
                P.scan(S[:], R[:], I[:], 0.0, [R, I], [S])
                m = mix[eh]
                P.tt('dve', m[:], S[:], G[:], ALU.mult, [S, G], [m])
                P.dma('sp', out_d.t[ct * 128:(ct + 1) * 128, :], m[:], [m], (), owner=m)
        P.end_section()


def emit_B(P, C, Cdim, mix_d, w_d, x_src, x_dst, final, fn_row=None, out_d=None):
    KCC = Cdim // 128
    NB = 512
    TB = 1024
    eps = C['eps']
    with ExitStack() as st:
        mix = P.sb(st, "bmix", [128, KCC, TB], BF16)
        ws = [P.sb(st, "bws%d" % i, [128, KCC, NB], BF16) for i in range(2)]
        xb = [P.sb(st, "bx%d" % i, [128, NB], F32) for i in range(4)]
        pp = [P.ps(st, "bpp%d" % i, [128, 512], F32) for i in range(4)]
        k = 0
        nw = 0
        for th in range(2):
            P.dma('sp', mix[:], mix_d.t[:, th * TB:(th + 1) * TB].rearrange("(kc p) t -> p kc t", p=128),
                  (), [mix], owner=mix)
            for nb in range(D // NB):
                w = ws[nw % 2]
                nw += 1
                P.dma('pool', w[:], w_d.t[:, nb * NB:(nb + 1) * NB].rearrange("(kc p) c -> p kc c", p=128),
                      (), [w], owner=w)
                for tt in range(TB // 128):
                    rows = slice(th * TB + tt * 128, th * TB + (tt + 1) * 128)
                    cols = slice(nb * NB, (nb + 1) * NB)
                    p_ = pp[k % 4]
                    xv = xb[k % 4]
                    k += 1
                    P.dma('sp', xv[:], x_src.t[rows, cols], (), [xv], owner=xv)
                    for kc in range(KCC):
                        P.mm(p_[:, 0:NB], mix[:, kc, tt * 128:(tt + 1) * 128], w[:, kc, :], kc == 0, kc == KCC - 1,
                             [mix, w], [p_])
                    P.tt('dve', xv[:], xv[:], p_[:, 0:NB], ALU.add, [xv, p_], [xv])
                    P.dma('sp', x_dst.t[rows, cols], xv[:], [xv], (), owner=xv)
        P.end_section()
    if final:
        with ExitStack() as st:
            xt = [P.sb(st, "fx%d" % i, [128, D], F32) for i in range(2)]
            junk = P.sb(st, "fjunk", [128, D], BF16)
            fnb = P.sb(st, "fnb", [128, D], F32)
            ss = P.sb(st, "fss", [128, 4], F32)
            P.dma('sp', fnb[:], fn_row.partition_broadcast(128), (), [fnb], owner=fnb)
            for tt in range(NT):
                xv = xt[tt % 2]
                s0 = ss[:, 2 * (tt % 2):2 * (tt % 2) + 1]
                s1 = ss[:, 2 * (tt % 2) + 1:2 * (tt % 2) + 2]
                P.dma('sp', xv[:], x_dst.t[tt * 128:(tt + 1) * 128, :], (), [xv], owner=xv)
                P.act(junk[:], xv[:], AF.Square, [xv], [junk, ss], accum_out=s0)
                P.act(s1, s0, AF.Sqrt, [ss, eps], [ss], scale=1.0 / D, bias=eps[:])
                P.op('dve', lambda e, s1=s1: e.reciprocal(out=s1, in_=s1), [ss], [ss])
                P.stt(xv[:], xv[:], s1, fnb[:], ALU.mult, ALU.mult, [xv, ss, fnb], [xv])
                P.dma('sp', out_d.t[tt * 128:(tt + 1) * 128, :], xv[:], [xv], (), owner=xv)
            P.end_section()


def build_fused(nlayers=4):
    nc = bass.Bass("TRN2", target_bir_lowering=False)
    with ExitStack() as st:
        P = Prog(nc, st)
        x_d = P.dram("x", [T, D], F32, "ExternalInput")
        nw_d = P.dram("nw", [4, D], F32, "ExternalInput")
        fnw_d = P.dram("fnw", [D], F32, "ExternalInput")
        cb_d = P.dram("cb", [128, CB_W], F32, "ExternalInput")
        cf_d = P.dram("cf", [128, 1536], F32, "ExternalInput")
        lbl_d = P.dram("lbl", [128, 32], F32, "ExternalInput")
        WE, WO = [], []
        for e in range(2):
            W = {'lbl': lbl_d}
            for nm, shp in (('whg', [D, 16 * 512]), ('gain', [2048]), ('wnq', [D, 2048]), ('wnkv', [D, 3072]),
                            ('wngt', [D, 48]), ('wnbg', [D, 2048]), ('w1k', [4096, 128]), ('w1v', [4096, 128]),
                            ('w2k', [128, 128]), ('w2v', [128, 128]), ('pekT', [128, 32]), ('pevT', [128, 32]),
                            ('wout', [4096, D])):
                W[nm] = P.dram("e%d_%s" % (e, nm), shp, F32, "ExternalInput")
            WE.append(W)
        for o in range(2):
            W = {}
            for nm, shp in (('wx', [D, 2560]), ('wg', [D, 2560]), ('wa', [10, 256, 256]), ('wi', [10, 256, 256]),
                            ('vec', [128, 160]), ('wout', [2560, D])):
                W[nm] = P.dram("o%d_%s" % (o, nm), shp, F32, "ExternalInput")
            WO.append(W)
        xs_d = P.dram("xs_scratch", [T, D], F32, "Internal")
        mixE_d = P.dram("mixE_scratch", [4096, T], BF16, "Internal")
        mixO_d = P.dram("mixO_scratch", [2560, T], BF16, "Internal")
        out_d = P.dram("out", [T, D], F32, "ExternalOutput")

        ident = make_ident(P, st)
        idf2 = P.sb(st, "idf2", [128, 128], F32)
        cm = P.sb(st, "cm", [128, 128], F32)
        identf = P.sb(st, "identf", [128, 128], F32)
        P.op('pool', lambda e: e.iota(idf2[:], pattern=[[1, 128]], base=0, channel_multiplier=-1,
                                      allow_small_or_imprecise_dtypes=True), (), [idf2])
        P.ts('dve', cm[:], idf2[:], 0.0, None, ALU.is_ge, None, [idf2], [cm])
        P.ts('dve', identf[:], idf2[:], 0.0, None, ALU.is_equal, None, [idf2], [identf])
        eps = P.sb(st, "eps", [128, 1], F32)
        P.memset('dve', eps[:], EPS, [eps])
        cf = P.sb(st, "cf", [128, 1536], F32)
        P.dma('sp', cf[:], cf_d.t.ap(), (), [cf], owner=cf)
        C = dict(ident=ident, identf=identf, cm=cm, eps=eps, cf=cf, cb_d=cb_d)
        P.end_section()

        for layer in range(nlayers):
            x_src = x_d if layer == 0 else xs_d
            nw_row = nw_d.t[layer]
            last = layer == nlayers - 1
            if layer % 2 == 0:
                W = WE[layer // 2]
                emit_even_A(P, C, layer // 2, x_src, nw_row, W, mixE_d)
                emit_B(P, C, 4096, mixE_d, W['wout'], x_src, xs_d, last, fnw_d.t.ap(), out_d)
            else:
                W = WO[layer // 2]
                emit_odd_A(P, C, x_src, nw_row, W, mixO_d)
                emit_B(P, C, 2560, mixO_d, W['wout'], x_src, xs_d, last, fnw_d.t.ap(), out_d)
        P.barrier()
        P.emit()
    return nc


def pack_inputs(inp):
    cbt, cft = even_const_tables()
    shared = {"nw": c_(inp['norm_w']), "fnw": c_(inp['final_norm_w']), "cb": cbt, "cf": cft}
    lg = inp['hgrn_lb_logits']
    shared["lbl"] = c_(lg.reshape(2, 16, 128).transpose(2, 1, 0).reshape(128, 32))
    for e in range(2):
        w_in = inp['even_w_in'][e]
        cols = []
        for gh in range(16):
            for k in ('a_q', 'a_f', 'a_i', 'a_g'):
                cols.append(w_in[:, EV_OFF[k] + gh * 128:EV_OFF[k] + (gh + 1) * 128])
        shared["e%d_whg" % e] = np.concatenate(cols, axis=1)
        kv = []
        for g in range(4):
            for k in ('b_kc', 'b_vc', 'b_ks', 'b_kw', 'b_vs', 'b_vw'):
                kv.append(w_in[:, EV_OFF[k] + g * 128:EV_OFF[k] + (g + 1) * 128])
        shared["e%d_wnkv" % e] = np.concatenate(kv, axis=1)
        shared["e%d_wnq" % e] = c_(w_in[:, EV_OFF['b_q']:EV_OFF['b_q'] + 2048])
        shared["e%d_wngt" % e] = c_(w_in[:, EV_OFF['b_gate']:EV_OFF['b_gate'] + 48])
        shared["e%d_wnbg" % e] = c_(w_in[:, EV_OFF['b_g']:EV_OFF['b_g'] + 2048])
        shared["e%d_gain" % e] = c_(inp['hgrn_norm_w'][e])
        shared["e%d_w1k" % e] = c_(inp['cmp_w1_k'][e])
        shared["e%d_w1v" % e] = c_(inp['cmp_w1_v'][e])
        shared["e%d_w2k" % e] = c_(inp['cmp_w2_k'][e])
        shared["e%d_w2v" % e] = c_(inp['cmp_w2_v'][e])
        shared["e%d_pekT" % e] = c_(inp['cmp_pe_k'][e].T)
        shared["e%d_pevT" % e] = c_(inp['cmp_pe_v'][e].T)
        shared["e%d_wout" % e] = c_(inp['even_w_out'][e])
    for o in range(2):
        w_in = inp['odd_w_in'][o]
        shared["o%d_wx" % o] = c_(w_in[:, 0:D_RNN])
        shared["o%d_wg" % o] = c_(w_in[:, D_RNN:2 * D_RNN])
        shared["o%d_wa" % o] = c_(inp['rg_w_a'][o])
        shared["o%d_wi" % o] = c_(inp['rg_w_i'][o])
        cw = inp['rg_conv_w'][o]
        vec = np.stack([cw[0], cw[1], cw[2], cw[3], inp['rg_conv_b'][o], inp['rg_b_a'][o], inp['rg_b_i'][o],
                        inp['rg_lambda'][o]], axis=-1)
        shared["o%d_vec" % o] = c_(vec.reshape(20, 128, 8).transpose(1, 0, 2).reshape(128, 160).astype(np.float32))
        shared["o%d_wout" % o] = c_(inp['odd_w_out'][o])
    return shared


def kernel(**inputs):
    inp = {k: np.asarray(v) for k, v in inputs.items()}
    shared = pack_inputs(inp)
    x = np.ascontiguousarray(inp['x'], dtype=np.float32)
    nc = _get("fused", build_fused)
    in_maps = []
    for c in range(NCORES):
        m = dict(shared)
        m["x"] = c_(x[c // 2])
        in_maps.append(m)
    res = run_bass_kernel_spmd(nc, in_maps, core_ids=list(range(NCORES)))
    out = np.stack([res.results[2 * b]["out"] for b in range(4)], axis=0)
    return out.astype(np.float32)
```

```python
import numpy as np
from contextlib import ExitStack
import concourse.bass as bass
import concourse.mybir as mybir
from concourse.alu_op_type import AluOpType as ALU
from concourse.bass_utils import run_bass_kernel_spmd

AF = mybir.ActivationFunctionType
F32 = mybir.dt.float32
BF16 = mybir.dt.bfloat16

D = 2048
T = 2048
NT = T // 128
KC = D // 128
D_RNN = 2560
EPS = 1e-6
NCORES = 8


class Buf:
    def __init__(self, t, name=""):
        self.t = t
        self.name = name
        self.w = None
        self.r = []
        self.dsem = None
        self.dcnt = 0
        self.psum = False

    def __getitem__(self, idx):
        return self.t[idx]


ENGS = ('pe', 'act', 'dve', 'pool', 'sp')


class Prog:
    def __init__(self, nc, stack):
        self.nc = nc
        self.stack = stack
        self.ops = {k: [] for k in ENGS}
        self.esem = {k: stack.enter_context(nc.semaphore("es_" + k)) for k in ENGS}
        self.ecnt = {k: 0 for k in ENGS}
        self.waited = {k: {} for k in ENGS}
        self.nsem = 0
        self.nops = 0
        self.dtoks = {}
        self.keep = []
        self.uid = 0
        self.sem_pool = {'sw': [], 'hw': []}
        self.sec_bufs = []

    def sb(self, stack, name, shape, dtype):
        self.uid += 1
        return Buf(stack.enter_context(self.nc.sbuf_tensor("s%d_%s" % (self.uid, name), list(shape), dtype)), name)

    def ps(self, stack, name, shape, dtype=F32):
        self.uid += 1
        b = Buf(stack.enter_context(self.nc.psum_tensor("p%d_%s" % (self.uid, name), list(shape), dtype)), name)
        b.psum = True
        return b

    def dram(self, name, shape, dtype, kind):
        return Buf(self.nc.dram_tensor(name, list(shape), dtype, kind=kind), name)

    def view(self, b, name=""):
        return Buf(b.t, name or b.name)

    def _dsem(self, b, eng):
        kind = 'sw' if eng == 'pool' else 'hw'
        if b.dsem is None:
            b.dsem = {}
            b.dcnt = {}
        if kind not in b.dsem:
            if self.sem_pool[kind]:
                b.dsem[kind], b.dcnt[kind] = self.sem_pool[kind].pop()
            else:
                sem = self.stack.enter_context(self.nc.semaphore("ds%s_%d" % (kind, self.nsem)))
                self.keep.append(sem)
                self.nsem += 1
                b.dsem[kind], b.dcnt[kind] = sem, 0
            self.sec_bufs.append((b, kind))
        return kind

    def end_section(self):
        self.barrier()
        for b, kind in self.sec_bufs:
            self.sem_pool[kind].append((b.dsem.pop(kind), b.dcnt.pop(kind)))
        self.sec_bufs = []

    def _deps(self, eng, reads, writes, waw=True):
        need = {}
        own = self.esem[eng]

        def add(d, raw):
            if d is None:
                return
            s, v = d
            if s is own and eng == 'pe':
                return
            k = id(s)
            if k not in need or need[k][1] < v:
                need[k] = (s, v)

        for b in reads:
            add(b.w, 'raw')
            if b.psum:
                for d in b.r:
                    if d[0] is not own:
                        add(d, 'rar')
        for b in writes:
            if waw:
                add(b.w, 'waw')
            for d in b.r:
                add(d, 'war')
        out = []
        wd = self.waited[eng]
        for k, (s, v) in need.items():
            if wd.get(k, 0) >= v:
                continue
            wd[k] = v
            out.append((s, v))
        return out

    def _commit(self, reads, writes, tok):
        for b in reads:
            b.r.append(tok)
            if len(b.r) > 64:
                b.r = b.r[-64:] if False else b.r
        for b in writes:
            b.w = tok
            b.r = []

    def op(self, eng, fn, reads=(), writes=(), waw=True):
        waits = self._deps(eng, reads, writes, waw)
        self.ecnt[eng] += 1
        tok = (self.esem[eng], self.ecnt[eng])
        self.ops[eng].append((waits, fn, self.esem[eng], 1))
        self._commit(reads, writes, tok)
        self.nops += 1

    def dma(self, eng, out_ap, in_ap, reads=(), writes=(), owner=None):
        waits = self._deps(eng, reads, writes)
        kind = self._dsem(owner, eng)
        sem = owner.dsem[kind]
        owner.dcnt[kind] += 16
        tok = (sem, owner.dcnt[kind])
        self.dtoks[id(sem)] = tok

        def fn(e):
            return e.dma_start(out=out_ap, in_=in_ap)
        self.ops[eng].append((waits, fn, sem, 16))
        self._commit(reads, writes, tok)
        self.nops += 1

    def barrier(self):
        toks = [(self.esem[k], self.ecnt[k]) for k in ENGS if self.ecnt[k] > 0]
        toks += list(self.dtoks.values())
        for k in ENGS:
            waits = []
            wd = self.waited[k]
            for (s, v) in toks:
                if s is self.esem[k]:
                    continue
                if wd.get(id(s), 0) >= v:
                    continue
                wd[id(s)] = v
                waits.append((s, v))
            if waits:
                self.ops[k].append((waits, None, None, 0))

    def wait_all(self, eng, bufs):
        waits = self._deps(eng, bufs, bufs)
        self.ops[eng].append((waits, None, None, 0))

    def mm(self, out, lhsT, rhs, start, stop, reads, writes):
        self.op('pe', lambda e: e.matmul(out, lhsT=lhsT, rhs=rhs, start=start, stop=stop),
                reads, writes)

    def tr(self, out, in_, ident, reads, writes):
        self.op('pe', lambda e: e.transpose(out=out, in_=in_, identity=ident), reads, writes)

    def act(self, out, in_, func, reads, writes, waw=True, **kw):
        self.op('act', lambda e: e.activation(out=out, in_=in_, func=func, **kw), reads, writes, waw)

    def tt(self, eng, out, in0, in1, op, reads, writes, waw=True):
        self.op(eng, lambda e: e.tensor_tensor(out=out, in0=in0, in1=in1, op=op), reads, writes, waw)

    def ts(self, eng, out, in0, s1, s2, op0, op1, reads, writes, waw=True):
        if op1 is None:
            self.op(eng, lambda e: e.tensor_scalar(out=out, in0=in0, scalar1=s1, scalar2=None, op0=op0),
                    reads, writes, waw)
        else:
            self.op(eng, lambda e: e.tensor_scalar(out=out, in0=in0, scalar1=s1, scalar2=s2, op0=op0, op1=op1),
                    reads, writes, waw)

    def stt(self, out, in0, scalar, in1, op0, op1, reads, writes, waw=True):
        self.op('dve', lambda e: e.scalar_tensor_tensor(out=out, in0=in0, scalar=scalar, in1=in1,
                                                         op0=op0, op1=op1), reads, writes, waw)

    def cp(self, eng, out, in_, reads, writes, waw=True):
        if eng == 'act':
            self.op('act', lambda e: e.copy(out=out, in_=in_), reads, writes, waw)
        else:
            self.op(eng, lambda e: e.tensor_copy(out=out, in_=in_), reads, writes, waw)

    def memset(self, eng, ap, val, writes):
        self.op(eng, lambda e: e.memset(ap, val), (), writes)

    def scan(self, out, d0, d1, init, reads, writes):
        self.op('dve', lambda e: e.tensor_tensor_scan(out=out, data0=d0, data1=d1, initial=init,
                                                       op0=ALU.mult, op1=ALU.add), reads, writes)

    def emit(self):
        nc = self.nc
        ops = self.ops

        def replay(k, e):
            for waits, fn, sem, inc in ops[k]:
                for s, v in waits:
                    e.wait_ge(s, v)
                if fn is not None:
                    fn(e).then_inc(sem, inc)

        with nc.Block() as block:
            @block.tensor
            def _(e):
                replay('pe', e)

            @block.scalar
            def _(e):
                replay('act', e)

            @block.vector
            def _(e):
                replay('dve', e)

            @block.gpsimd
            def _(e):
                replay('pool', e)

            @block.sync
            def _(e):
                replay('sp', e)


def make_ident(P, st):
    idf = P.sb(st, "idf", [128, 128], F32)
    ident = P.sb(st, "ident", [128, 128], BF16)
    P.op('pool', lambda e: e.iota(idf[:], pattern=[[1, 128]], base=0, channel_multiplier=-1,
                                  allow_small_or_imprecise_dtypes=True), (), [idf])
    P.ts('dve', ident[:], idf[:], 0.0, None, ALU.is_equal, None, [idf], [ident])
    return ident


def emit_hT(P, st, x_d, nwb, ident, hT, hTv, xt, xn, ptr, small):
    eps, ss = small['eps'], small['ss']
    for tt in range(NT):
        xb = xt[tt % 2]
        xnb = xn[tt % 2]
        P.dma('sp', xb[:, 0:D], x_d.t[tt * 128:(tt + 1) * 128, :], (), [xb], owner=xb)
        s0 = ss[:, 2 * (tt % 2):2 * (tt % 2) + 1]
        s1 = ss[:, 2 * (tt % 2) + 1:2 * (tt % 2) + 2]
        P.act(xnb[:, 0:D], xb[:, 0:D], AF.Square, [xb], [xnb, ss], accum_out=s0)
        P.act(s1, s0, AF.Sqrt, [ss, eps], [ss], scale=1.0 / D, bias=eps[:])
        P.op('dve', lambda e, s1=s1: e.reciprocal(out=s1, in_=s1), [ss], [ss])
        P.stt(xnb[:, 0:D], xb[:, 0:D], s1, nwb[:, 0:D], ALU.mult, ALU.mult, [xb, ss, nwb], [xnb])
        for q in range(4):
            pt = ptr[q % 2]
            for j in range(4):
                kc = q * 4 + j
                P.tr(pt[:, j, :], xnb[:, kc * 128:(kc + 1) * 128], ident[:], [xnb, ident], [pt])
            eng = 'act' if tt % 2 == 0 else 'dve'
            P.cp(eng, hT[:, q * 4:(q + 1) * 4, tt * 128:(tt + 1) * 128], pt[:, :, :], [pt], [hTv[tt]])


_CACHE = {}


def _get(name, fn):
    if name not in _CACHE:
        _CACHE[name] = fn()
    return _CACHE[name]


def c_(a):
    return np.ascontiguousarray(a)


BIG = 30000.0
SCALE = 128 ** -0.5


def _bf16_split(a):
    import ml_dtypes
    a = np.asarray(a, np.float32)
    hi = a.astype(ml_dtypes.bfloat16).astype(np.float32)
    lo = (a - hi).astype(ml_dtypes.bfloat16).astype(np.float32)
    return hi, lo


def even_const_tables():
    tb = {}
    s_l = np.arange(128, dtype=np.float32)
    La = np.zeros((128, 16, 128), np.float32)
    for m in range(16):
        dj = -m
        La[0, m] = s_l
        La[2, m] = s_l
        La[1, m] = 64.0 * (2 * dj - 1)
        La[3, m] = 64.0 * (2 * dj - 1)
    tb['La'] = La.reshape(128, 16 * 128)
    n = np.arange(127, dtype=np.float32)
    Lc = np.zeros((128, 16, 127), np.float32)
    for i in range(16):
        Lc[0, i] = 16.0 * (n - 8 * i)
        Lc[2, i] = 16.0 * (n - 8 * i)
        Lc[1, i] = -33.0
        Lc[3, i] = -33.0
    tb['Lc'] = Lc.reshape(128, 16 * 127)
    slopes = (2.0 ** (-8.0 * np.arange(1, 17) / 16)).astype(np.float32)
    Ra = np.zeros((128, 4, 4, 128), np.float32)
    for gg in range(4):
        g = gg
        for h in range(4):
            hi, lo = _bf16_split(slopes[4 * g + h])
            Ra[0, gg, h] = hi
            Ra[1, gg, h] = hi
            Ra[2, gg, h] = lo
            Ra[3, gg, h] = lo
    tb['Ra'] = Ra.reshape(128, 4 * 512)
    Sel = np.zeros((128, 16, 127), np.float32)
    for i in range(16):
        k = np.clip(np.arange(127) - 8 * i + 64, 0, 127)
        Sel[k, i, np.arange(127)] = 1.0
    tb['Sel'] = Sel.reshape(128, 16 * 127)
    kk = np.arange(128)[:, None]
    tl = np.arange(128)[None, :]
    G = np.where(16 * (kk - 64) + 31 > tl, -BIG, 0.0).astype(np.float32)
    tb['G'] = np.tile(G[:, None, :], (1, 4, 1)).reshape(128, 512)
    tric = np.where(kk > tl, -BIG, 0.0).astype(np.float32)
    trib = np.where(kk <= tl, -BIG, 0.0).astype(np.float32)
    tb['TRIc'] = np.tile(tric[:, None, :], (1, 4, 1)).reshape(128, 512)
    tb['TRIb'] = np.tile(trib[:, None, :], (1, 4, 1)).reshape(128, 512)
    E = np.zeros((128, 16, 128), np.float32)
    for jt in range(16):
        E[2 * jt, jt, 0:64] = 1.0
        E[2 * jt + 1, jt, 64:128] = 1.0
    tb['E'] = E.reshape(128, 16 * 128)
    VAL = np.zeros((128, 16, 32), np.float32)
    ADD = np.zeros((128, 16, 32), np.float32)
    for i in range(16):
        for t_l in range(128):
            cur = (128 * i + t_l) // 64
            for j in range(32):
                if j == 0:
                    ADD[t_l, i, j] = 3e30
                elif j == cur:
                    ADD[t_l, i, j] = 2e30
                elif j == cur - 1:
                    ADD[t_l, i, j] = 1e30
                elif j <= cur:
                    VAL[t_l, i, j] = 1.0
                else:
                    ADD[t_l, i, j] = -1e30
    tb['VAL'] = VAL.reshape(128, 512)
    tb['ADD'] = ADD.reshape(128, 512)
    c_start = np.arange(127) * 16
    s_start = np.arange(32) * 64
    ov = ((c_start[:, None] <= s_start[None, :] + 63) & (c_start[:, None] + 31 >= s_start[None, :]))
    OV = np.zeros((128, 32), np.float32)
    OV[:127] = ov
    tb['OV'] = OV
    rst = np.ones((128, 512), np.float32)
    rst[:, 0::128] = 0.0
    tb['RST'] = rst
    cat = np.concatenate([tb[k] for k in ('La', 'Lc', 'Ra', 'Sel', 'G', 'TRIc', 'TRIb', 'E', 'OV')], axis=1)
    f32 = np.concatenate([tb[k] for k in ('VAL', 'ADD', 'RST')], axis=1)
    return c_(cat.astype(np.float32)), c_(f32)


CB_OFF = {}
_o = 0
for _k, _w in (('La', 2048), ('Lc', 16 * 127), ('Ra', 2048), ('Sel', 16 * 127), ('G', 512), ('TRIc', 512),
               ('TRIb', 512), ('E', 2048), ('OV', 32)):
    CB_OFF[_k] = _o
    _o += _w
CB_W = _o


def emit_hgrn2(P, s1, e_idx, hT, hTv, ident, cm, RST, eps, ptr, whg_d, lbl_d, gain_d, out_d, nh):
    f32b = lambda n: P.sb(s1, n, [128, 512], F32)
    bfb = lambda n: P.sb(s1, n, [128, 512], BF16)
    whg = [P.sb(s1, "whg%d" % i, [128, KC, 512], BF16) for i in range(2)]
    lbl = P.sb(s1, "lbl", [128, 16, 2], F32)
    lbt = P.sb(s1, "lbt", [128, 16, 4], F32)
    lb = P.sb(s1, "lb", [128, 16], F32)
    oml = P.sb(s1, "oml", [128, 16], F32)
    gainb = P.sb(s1, "gainb", [128, 2048], F32)
    qs, t1, t2, gg_, kk, b128, dd, d3 = [f32b("hg_f%d" % i) for i in range(8)]
    EA1, EA2, EB1, EB2, EQB, EKA, E3, E5 = [f32b("hg_e%d" % i) for i in range(8)]
    QA, QB, KA, KB, QOB, KOA, QP, KH = [bfb("hg_o%d" % i) for i in range(8)]
    vtok = [P.sb(s1, "vtok%d" % i, [128, 4, 128], BF16) for i in range(2)]
    ag = [P.sb(s1, "ag%d" % i, [128, 4, 128], F32) for i in range(2)]
    at4 = P.sb(s1, "at4", [128, 4, 128], BF16)
    kht4 = P.sb(s1, "kht4", [128, 4, 128], BF16)
    S5 = P.sb(s1, "S5", [128, 5, 128], F32)
    Sb5 = P.sb(s1, "Sb5", [128, 5, 128], BF16)
    ob4 = P.sb(s1, "ob4", [128, 4, 128], F32)
    junk4 = P.sb(s1, "junk4", [128, 4, 128], F32)
    yb4 = P.sb(s1, "yb4", [128, 4, 128], BF16)
    sst4 = P.sb(s1, "sst4", [128, 8], F32)
    yaT = [P.sb(s1, "yaT%d" % i, [128, T], BF16) for i in range(2)]
    pq = [P.ps(s1, "pq%d" % i, [128, 512], F32) for i in range(2)]
    pv = P.ps(s1, "pv", [128, 2, 512], F32)
    patt = P.ps(s1, "patt", [128, 4, 128], F32)
    po = P.ps(s1, "po", [128, 4, 128], F32)
    pS = P.ps(s1, "pS", [128, 4, 128], F32)

    P.dma('sp', lbl[:], lbl_d.t.ap().rearrange("p (h e) -> p h e", e=2), (), [lbl], owner=lbl)
    P.dma('sp', gainb[:], gain_d.t.ap().partition_broadcast(128), (), [gainb], owner=gainb)
    P.tt('dve', lbt[:, :, 0], lbl[:, :, 0], lbl[:, :, 1], ALU.max, [lbl], [lbt])
    P.tt('dve', lbt[:, :, 1], lbl[:, :, 0], lbt[:, :, 0], ALU.subtract, [lbl, lbt], [lbt])
    P.tt('dve', lbt[:, :, 2], lbl[:, :, 1], lbt[:, :, 0], ALU.subtract, [lbl, lbt], [lbt])
    P.act(lbt[:, :, 1:3], lbt[:, :, 1:3], AF.Exp, [lbt], [lbt])
    P.tt('dve', lbt[:, :, 3], lbt[:, :, 1], lbt[:, :, 2], ALU.add, [lbt], [lbt])
    P.op('dve', lambda e: e.reciprocal(out=lbt[:, :, 3], in_=lbt[:, :, 3]), [lbt], [lbt])
    P.tt('dve', lbt[:, :, 1], lbt[:, :, 1], lbt[:, :, 3], ALU.mult, [lbt], [lbt])
    P.tt('dve', lbt[:, :, 2], lbt[:, :, 2], lbt[:, :, 3], ALU.mult, [lbt], [lbt])
    if e_idx == 0:
        P.tt('dve', lb[:], lbt[:, :, 1], lbt[:, :, 1], ALU.subtract, [lbt], [lb])
    else:
        P.tt('dve', lbt[:, :, 3], lbt[:, :, 1], lbt[:, :, 2], ALU.add, [lbt], [lbt])
        P.tt('dve', lb[:], lbt[:, :, 3], lbt[:, :, 1], ALU.subtract, [lbt], [lb])
    P.ts('dve', oml[:], lb[:], -1.0, 1.0, ALU.mult, ALU.add, [lb], [oml])
    for b_ in (EA1, EA2, EB1, EB2, EQB, EKA):
        P.memset('pool', b_[:], 0.0, [b_])

    def v4(buf):
        return buf[:].rearrange("p (j c) -> p j c", c=128)

    def load_w(hd):
        w = whg[hd % 2]
        P.dma('pool', w[:], whg_d.t[:, hd * 512:(hd + 1) * 512].rearrange("(kc p) c -> p kc c", p=128),
              (), [w], owner=w)

    pt0, pt1 = ptr
    steps = [(hd, tb) for hd in range(nh) for tb in range(4)]

    def Pm(k):
        hd, tb = steps[k]
        w = whg[hd % 2]
        if tb == 0 and hd + 1 < nh:
            load_w(hd + 1)
        hts = hTv[4 * tb:4 * tb + 4]
        tsl = slice(tb * 512, (tb + 1) * 512)
        for kc in range(KC):
            P.mm(pq[0][:], w[:, kc, 0:128], hT[:, kc, tsl], kc == 0, kc == KC - 1, [w] + hts, [pq[0]])
        for kc in range(KC):
            P.mm(pq[1][:], w[:, kc, 128:256], hT[:, kc, tsl], kc == 0, kc == KC - 1, [w] + hts, [pq[1]])
        for j in range(4):
            for kc in range(KC):
                P.mm(pv[:, j // 2, (j % 2) * 256:(j % 2 + 1) * 256],
                     hT[:, kc, tb * 512 + j * 128:tb * 512 + (j + 1) * 128],
                     w[:, kc, 256:512], kc == 0, kc == KC - 1, [w, hts[j]], [pv])

    def Pe(k):
        vt, agk = vtok[k % 2], ag[k % 2]
        P.act(qs[:], pq[0][:], AF.Silu, [pq[0]], [qs])
        P.act(t1[:], pq[1][:], AF.Exp, [pq[1]], [t1], scale=-1.0)
        pv4 = pv[:].rearrange("p b (t c) -> p (b t) c", c=256)
        P.cp('dve', vt[:], pv4[:, :, 0:128], [pv], [vt])
        P.act(agk[:], pv4[:, :, 128:256], AF.Silu, [pv], [agk])

    def E(k):
        hd, tb = steps[k]
        P.act(t1[:], t1[:], AF.Ln, [t1], [t1], bias=1.0)
        P.act(t1[:], t1[:], AF.Exp, [t1], [t1], scale=-1.0)
        P.ts('dve', t2[:], t1[:], oml[:, hd:hd + 1], lb[:, hd:hd + 1], ALU.mult, ALU.add,
             [t1, oml, lb], [t2])
        P.ts('dve', t2[:], t2[:], 1e-30, None, ALU.max, None, [t2], [t2])
        P.act(gg_[:], t2[:], AF.Ln, [t2], [gg_])
        P.ts('dve', kk[:], t2[:], -1.0, 1.0, ALU.mult, ALU.add, [t2], [kk])
        P.scan(b128[:], RST, gg_[:], 0.0, [gg_], [b128])
        bv = v4(b128)
        dv_ = v4(dd)
        P.tt('dve', dv_[:, :, 0:64], bv[:, :, 0:64], bv[:, :, 31:32].to_broadcast([128, 4, 64]),
             ALU.subtract, [b128], [dd])
        P.tt('dve', dv_[:, :, 64:128], bv[:, :, 64:128], bv[:, :, 95:96].to_broadcast([128, 4, 64]),
             ALU.subtract, [b128], [dd])
        P.act(v4(EA1)[:, :, 0:64], dv_[:, :, 0:64], AF.Exp, [dd], [EA1])
        P.act(v4(EA2)[:, :, 0:64], dv_[:, :, 0:64], AF.Exp, [dd], [EA2], scale=-1.0)
        P.act(v4(EB1)[:, :, 64:128], dv_[:, :, 64:128], AF.Exp, [dd], [EB1])
        P.act(v4(EB2)[:, :, 64:128], dv_[:, :, 64:128], AF.Exp, [dd], [EB2], scale=-1.0)
        d3v = v4(d3)
        P.tt('dve', d3v[:, :, :], bv[:, :, :], bv[:, :, 63:64].to_broadcast([128, 4, 128]),
             ALU.subtract, [b128], [d3])
        P.act(v4(EQB)[:, :, 64:128], d3v[:, :, 64:128], AF.Exp, [d3], [EQB])
        P.act(v4(EKA)[:, :, 0:64], d3v[:, :, 0:64], AF.Exp, [d3], [EKA], scale=-1.0)
        P.act(E3[:], b128[:], AF.Exp, [b128], [E3])
        P.tt('dve', d3v[:, :, :], bv[:, :, :], bv[:, :, 127:128].to_broadcast([128, 4, 128]),
             ALU.subtract, [b128], [d3])
        P.act(E5[:], d3[:], AF.Exp, [d3], [E5], scale=-1.0)
        P.tt('dve', QA[:], qs[:], EA1[:], ALU.mult, [qs, EA1], [QA])
        P.tt('dve', QB[:], qs[:], EB1[:], ALU.mult, [qs, EB1], [QB])
        P.tt('dve', KA[:], kk[:], EA2[:], ALU.mult, [kk, EA2], [KA])
        P.tt('dve', KB[:], kk[:], EB2[:], ALU.mult, [kk, EB2], [KB])
        P.tt('pool', QOB[:], qs[:], EQB[:], ALU.mult, [qs, EQB], [QOB])
        P.tt('pool', KOA[:], kk[:], EKA[:], ALU.mult, [kk, EKA], [KOA])
        P.tt('pool', QP[:], qs[:], E3[:], ALU.mult, [qs, E3], [QP])
        P.tt('pool', KH[:], kk[:], E5[:], ALU.mult, [kk, E5], [KH])

    def L(k):
        hd, tb = steps[k]
        vt, agk = vtok[k % 2], ag[k % 2]
        yT = yaT[hd % 2]
        if tb == 0:
            P.memset('dve', S5[:, 0, :], 0.0, [S5])
            P.memset('dve', Sb5[:, 0, :], 0.0, [Sb5])
        cs_ = [slice(j * 128, (j + 1) * 128) for j in range(4)]
        for j in range(4):
            P.mm(patt[:, j, :], KA[:, cs_[j]], QA[:, cs_[j]], True, False, [KA, QA], [patt])
            P.mm(patt[:, j, :], KB[:, cs_[j]], QB[:, cs_[j]], False, False, [KB, QB], [patt])
            P.mm(patt[:, j, :], KOA[:, cs_[j]], QOB[:, cs_[j]], False, True, [KOA, QOB], [patt])
        P.tt('dve', at4[:], patt[:], cm[:].unsqueeze(1).to_broadcast([128, 4, 128]), ALU.mult, [patt, cm], [at4])
        for j in range(4):
            P.tr(pt0[:, j, :], KH[:, cs_[j]], ident[:], [KH, ident], [pt0])
        P.cp('act', kht4[:], pt0[:, 0:4, :], [pt0], [kht4])
        for j in range(4):
            P.mm(pS[:, j, :], kht4[:, j, :], vt[:, j, :], True, True, [kht4, vt], [pS])
        for j in range(4):
            P.stt(S5[:, j + 1, :], S5[:, j, :], E3[:, (j + 1) * 128 - 1:(j + 1) * 128], pS[:, j, :],
                  ALU.mult, ALU.add, [S5, E3, pS], [S5])
        P.cp('act', Sb5[:, 1:5, :], S5[:, 1:5, :], [S5], [Sb5])
        for j in range(4):
            P.mm(po[:, j, :], at4[:, j, :], vt[:, j, :], True, False, [at4, vt], [po])
            P.mm(po[:, j, :], QP[:, cs_[j]], Sb5[:, j, :], False, True, [QP, Sb5], [po])
        P.cp('pool', S5[:, 0, :], S5[:, 4, :], [S5], [S5])
        P.cp('pool', Sb5[:, 0, :], Sb5[:, 4, :], [Sb5], [Sb5])
        P.cp('dve', ob4[:], po[:], [po], [ob4])
        P.act(junk4[:], ob4[:], AF.Square, [ob4], [junk4])
        P.op('dve', lambda e: e.tensor_reduce(out=sst4[:, 0:4], in_=junk4[:], axis=mybir.AxisListType.X,
                                              op=ALU.add), [junk4], [sst4])
        P.act(sst4[:, 4:8], sst4[:, 0:4], AF.Sqrt, [sst4, eps], [sst4], scale=1.0 / 128, bias=eps[:])
        P.op('dve', lambda e: e.reciprocal(out=sst4[:, 4:8], in_=sst4[:, 4:8]), [sst4], [sst4])
        P.tt('dve', ob4[:], ob4[:], sst4[:, 4:8].unsqueeze(2).to_broadcast([128, 4, 128]), ALU.mult,
             [ob4, sst4], [ob4])
        P.tt('dve', ob4[:], ob4[:],
             gainb[:, hd * 128:(hd + 1) * 128].unsqueeze(1).to_broadcast([128, 4, 128]), ALU.mult,
             [ob4, gainb], [ob4])
        P.tt('dve', yb4[:], ob4[:], agk[:], ALU.mult, [ob4, agk], [yb4])
        for j in range(4):
            P.tr(pt1[:, 4 + j, :], yb4[:, j, :], ident[:], [yb4, ident], [pt1])
        P.cp('act', yT[:, tb * 512:(tb + 1) * 512].rearrange("p (j c) -> p j c", c=128), pt1[:, 4:8, :], [pt1], [yT])
        if tb == 3:
            P.dma('sp', out_d.t[hd * 128:(hd + 1) * 128, :], yT[:], [yT], (), owner=yT)

    load_w(0)
    Pm(0)
    Pe(0)
    for k in range(len(steps)):
        if k + 1 < len(steps):
            Pm(k + 1)
        E(k)
        if k + 1 < len(steps):
            Pe(k + 1)
        L(k)


def emit_nsa(P, s2, hT, hTv, ident, ptr, VAL, ADD, wnq_d, wnkv_d, wngt_d, wnbg_d, w1k_d, w1v_d,
             w2k_d, w2v_d, pek_d, pev_d, cb_d, out_d, ng=4, nqt=NT):
    cf_reads = []
    cb = P.sb(s2, "cb", [128, CB_W], BF16)
    P.dma('pool', cb[:], cb_d.t.ap(), (), [cb], owner=cb)

    def CB(name, lo, hi):
        return cb[:, CB_OFF[name] + lo:CB_OFF[name] + hi]

    wbuf = P.sb(s2, "wbuf", [128, KC, 768], BF16)
    wgt = P.sb(s2, "wgt", [128, KC, 12], BF16)
    w1k = P.sb(s2, "w1k", [128, 32, 128], BF16)
    w1v = P.sb(s2, "w1v", [128, 32, 128], BF16)
    w2k = P.sb(s2, "w2k", [128, 128], BF16)
    w2v = P.sb(s2, "w2v", [128, 128], BF16)
    pek = P.sb(s2, "pek", [128, 32], BF16)
    pev = P.sb(s2, "pev", [128, 32], BF16)
    P.dma('pool', w1k[:], w1k_d.t.ap().rearrange("(j d) o -> d j o", d=128), (), [w1k], owner=w1k)
    P.dma('pool', w1v[:], w1v_d.t.ap().rearrange("(j d) o -> d j o", d=128), (), [w1v], owner=w1v)
    P.dma('pool', w2k[:], w2k_d.t.ap(), (), [w2k], owner=w2k)
    P.dma('pool', w2v[:], w2v_d.t.ap(), (), [w2v], owner=w2v)
    P.dma('pool', pek[:], pek_d.t.ap(), (), [pek], owner=pek)
    P.dma('pool', pev[:], pev_d.t.ap(), (), [pev], owner=pev)

    qT = P.sb(s2, "qT", [128, 4, T], BF16)
    kcT = P.sb(s2, "kcT", [128, T], BF16)
    vcT = P.sb(s2, "vcT", [128, T], BF16)
    ksT = P.sb(s2, "ksT", [128, T], BF16)
    kwT = P.sb(s2, "kwT", [128, T], BF16)
    vsA = P.sb(s2, "vsA", [128, NT, 129], BF16)
    vwA = P.sb(s2, "vwA", [128, NT, 129], BF16)
    hidk = P.sb(s2, "hidk", [128, 127], BF16)
    hidv = P.sb(s2, "hidv", [128, 127], BF16)
    cbias = P.sb(s2, "cbias", [128, 2], F32)
    KcT = P.sb(s2, "KcT", [128, 127], BF16)
    VcA = P.sb(s2, "VcA", [128, 161], BF16)
    Rsel = P.sb(s2, "Rsel", [128, 4, 128], BF16)
    Pt = [P.sb(s2, "Pt%d" % i, [128, 4, 128], BF16) for i in range(3)]
    gts = P.sb(s2, "gts", [128, 12], F32)
    sgate = P.sb(s2, "sgate", [128, 512], F32)
    zz = P.sb(s2, "zz", [128, 3, 4], F32)
    cs = P.sb(s2, "cs", [128, 3, 4], F32)
    acc = P.sb(s2, "acc", [128, 4, 128], F32)
    imp = P.sb(s2, "imp", [128, 32], F32)
    score = P.sb(s2, "score", [128, 32], F32)
    work = P.sb(s2, "work", [128, 32], F32)
    m8 = P.sb(s2, "m8", [128, 16], F32)
    nm = P.sb(s2, "nm", [128, 32], F32)
    ybf = P.sb(s2, "ybf", [128, 512], BF16)
    yTt = [P.sb(s2, "yTt%d" % i, [128, 4, 128], BF16) for i in range(2)]
    psA = [P.ps(s2, "psA%d" % i, [128, 4, 128], F32) for i in range(2)]
    ident, identf = ident
    psO4 = P.ps(s2, "psO", [128, 4, 512], F32)
    psO = [psO4] * 4
    ocp = P.sb(s2, "ocp", [128, 4, 161], F32)
    impm = P.sb(s2, "impm", [128, 4, 32], F32)
    tmpo = P.sb(s2, "tmpo", [128, 4, 128], F32)
    psM = P.ps(s2, "psM", [128, 512], F32)
    psG = psM

    P.memset('dve', vsA[:, :, 128:129], 1.0, [vsA])
    P.memset('dve', vwA[:, :, 128:129], 1.0, [vwA])
    P.memset('dve', Rsel[:], 0.0, [Rsel])
    P.memset('dve', VcA[:], 0.0, [VcA])
    P.memset('dve', VcA[:, 128:129], 1.0, [VcA])
    P.cp('dve', VcA[:, 129:161], CB('OV', 0, 32), [cb], [VcA])

    def Oh(h):
        return psO4[:, h, 0:256]

    npa = [0]
    npt = [1]

    def nextA():
        p = psA[npa[0] % 2]
        npa[0] += 1
        return p

    for gg in range(ng):
        Ra = CB('Ra', gg * 512, (gg + 1) * 512)
        P.dma('pool', wbuf[:, :, 0:512], wnq_d.t[:, gg * 512:(gg + 1) * 512].rearrange("(kc p) c -> p kc c", p=128),
              (), [wbuf], owner=wbuf)
        for h in range(4):
            for tb in range(4):
                pp = nextA()
                ppf = pp[:].rearrange("p a b -> p (a b)")
                for kc in range(KC):
                    P.mm(ppf, wbuf[:, kc, h * 128:(h + 1) * 128], hT[:, kc, tb * 512:(tb + 1) * 512],
                         kc == 0, kc == KC - 1, [wbuf] + hTv[4 * tb:4 * tb + 4], [pp])
                P.act(qT[:, h, tb * 512:(tb + 1) * 512], ppf, AF.Copy, [pp], [qT], scale=SCALE)
        P.dma('pool', wbuf[:], wnkv_d.t[:, gg * 768:(gg + 1) * 768].rearrange("(kc p) c -> p kc c", p=128),
              (), [wbuf], owner=wbuf)
        P.dma('pool', wgt[:], wngt_d.t[:, gg * 12:(gg + 1) * 12].rearrange("(kc p) c -> p kc c", p=128),
              (), [wgt], owner=wgt)
        for (dst, col) in ((kcT, 0), (vcT, 128), (ksT, 256), (kwT, 384)):
            for tb in range(4):
                pp = nextA()
                ppf = pp[:].rearrange("p a b -> p (a b)")
                for kc in range(KC):
                    P.mm(ppf, wbuf[:, kc, col:col + 128], hT[:, kc, tb * 512:(tb + 1) * 512],
                         kc == 0, kc == KC - 1, [wbuf] + hTv[4 * tb:4 * tb + 4], [pp])
                P.cp('dve' if tb % 2 else 'act', dst[:, tb * 512:(tb + 1) * 512], ppf, [pp], [dst])
        for tt in range(NT):
            pp = nextA()
            ppf = pp[:].rearrange("p a b -> p (a b)")
            for kc in range(KC):
                P.mm(ppf[:, 0:256], hT[:, kc, tt * 128:(tt + 1) * 128], wbuf[:, kc, 512:768],
                     kc == 0, kc == KC - 1, [wbuf, hTv[tt]], [pp])
            eng = 'dve' if tt % 2 else 'act'
            P.cp(eng, vsA[:, tt, 0:128], ppf[:, 0:128], [pp], [vsA])
            P.cp(eng, vwA[:, tt, 0:128], ppf[:, 128:256], [pp], [vwA])
        P.dma('pool', wbuf[:, :, 0:512], wnbg_d.t[:, gg * 512:(gg + 1) * 512].rearrange("(kc p) c -> p kc c", p=128),
              (), [wbuf], owner=wbuf)
        for (src, w1, w2, pe, hid, col) in ((kcT, w1k, w2k, pek, hidk, 0), (vcT, w1v, w2v, pev, hidv, 1)):
            srcv = src[:].rearrange("p (n r) -> p n r", r=16)
            for j in range(32):
                P.mm(psM[:, 0:127], w1[:, j, :], srcv[:, j // 16:j // 16 + 127, j % 16], j == 0, j == 31,
                     [w1, src], [psM])
            for j in range(32):
                P.mm(psM[:, 128:129], w1[:, j, :], pe[:, j:j + 1], j == 0, j == 31, [w1, pe], [psM])
            P.cp('dve', cbias[:, col:col + 1], psM[:, 128:129], [psM], [cbias])
            P.act(hid[:], psM[:, 0:127], AF.Silu, [psM, cbias], [hid], bias=cbias[:, col:col + 1])
        P.mm(psM[:, 256:383], w2k[:], hidk[:], True, True, [w2k, hidk], [psM])
        P.cp('dve', KcT[:], psM[:, 256:383], [psM], [KcT])
        P.mm(psM[0:127, 384:512], hidv[:], w2v[:], True, True, [hidv, w2v], [psM])
        P.cp('dve', VcA[0:127, 0:128], psM[0:127, 384:512], [psM], [VcA])

        for i in range(nqt):
            isl = slice(i * 128, (i + 1) * 128)
            qrhs = qT[:, :, isl]
            for kc in range(KC):
                P.mm(psG[:], hT[:, kc, isl], wbuf[:, kc, 0:512], kc == 0, kc == KC - 1, [wbuf, hTv[i]], [psG])
            P.act(sgate[:], psG[:], AF.Silu, [psG], [sgate])
            for kc in range(KC):
                P.mm(psM[:, 0:12], hT[:, kc, isl], wgt[:, kc, :], kc == 0, kc == KC - 1, [wgt, hTv[i]], [psM])
            P.act(gts[:], psM[:, 0:12], AF.Sigmoid, [psM], [gts])
            gv = gts[:].rearrange("p (h j) -> p j h", j=3)

            def finish(br, width):
                P.cp('dve', ocp[:, :, 0:width], psO4[:, :, 0:width], [psO4], [ocp])
                P.ts('dve', zz[:, br, :], ocp[:, :, 128], 1e-37, None, ALU.max, None, [ocp], [zz])
                P.op('dve', lambda e, br=br: e.reciprocal(out=zz[:, br, :], in_=zz[:, br, :]), [zz], [zz])
                if br == 0:
                    P.tt('dve', impm[:], ocp[:, :, 129:161], zz[:, 0, :].unsqueeze(2).to_broadcast([128, 4, 32]),
                         ALU.mult, [ocp, zz], [impm])
                    P.op('dve', lambda e: e.tensor_reduce(out=imp[:], in_=impm[:].rearrange("p h j -> p j h"),
                                                          axis=mybir.AxisListType.X, op=ALU.add), [impm], [imp])
                P.tt('dve', cs[:, br, :], zz[:, br, :], gv[:, br, :], ALU.mult, [zz, gts], [cs])
                if br == 0:
                    P.tt('dve', acc[:], ocp[:, :, 0:128], cs[:, br, :].unsqueeze(2).to_broadcast([128, 4, 128]),
                         ALU.mult, [ocp, cs], [acc])
                else:
                    P.tt('dve', tmpo[:], ocp[:, :, 0:128], cs[:, br, :].unsqueeze(2).to_broadcast([128, 4, 128]),
                         ALU.mult, [ocp, cs], [tmpo])
                    P.tt('dve', acc[:], acc[:], tmpo[:], ALU.add, [acc, tmpo], [acc])

            pp = nextA()
            P.mm(pp[0:127, :, :], KcT[:], qrhs, True, False, [KcT, qT], [pp])
            P.mm(pp[0:127, :, :], CB('Lc', i * 127, (i + 1) * 127), Ra.rearrange("p (a b) -> p a b", b=128),
                 False, False, [cb], [pp])
            P.mm(pp[0:127, :, :], CB('Sel', i * 127, (i + 1) * 127),
                 CB('G', 0, 512).rearrange("p (a b) -> p a b", b=128), False, True, [cb], [pp])
            pt_ = Pt[0]
            P.act(pt_[0:127, :, :], pp[0:127, :, :], AF.Exp, [pp], [pt_])
            for h in range(4):
                P.mm(Oh(h)[:, 0:161], pt_[0:127, h, :], VcA[0:127, :], True, True, [pt_, VcA], [psO[h]])
            finish(0, 161)
            need_sel = (2 * i + 2) > 16
            if need_sel:
                P.tt('dve', score[:], imp[:], VAL[:, i * 32:(i + 1) * 32], ALU.mult, [imp], [score])
                P.tt('dve', score[:], score[:], ADD[:, i * 32:(i + 1) * 32], ALU.add, [score], [score])
                P.op('dve', lambda e: e.max(out=m8[:, 0:8], in_=score[:]), [score], [m8])
                P.op('dve', lambda e: e.match_replace(out=work[:], in_to_replace=m8[:, 0:8], in_values=score[:],
                                                      imm_value=-3e38), [score, m8], [work])
                P.op('dve', lambda e: e.max(out=m8[:, 8:16], in_=work[:]), [work], [m8])
                P.ts('dve', nm[:], score[:], m8[:, 15:16], -BIG, ALU.is_lt, ALU.mult, [score, m8], [nm])
                P.tr(psM[0:32, 128:256], nm[:], identf[:], [nm, identf], [psM])
                P.cp('dve', Rsel[0:32, :, :], psM[0:32, 128:256].rearrange("p (a b) -> p a b", a=1).to_broadcast([32, 4, 128]),
                     [psM], [Rsel])

            for br, (kT_, vA_) in ((2, (kwT, vwA)), (1, (ksT, vsA))):
                jlo = 0 if br == 1 else max(0, i - 4)

                def emit_qk(jt, br=br, kT_=kT_, jlo=jlo):
                    dj = jt - i
                    pp = nextA()
                    extra = []
                    if br == 1 and need_sel and jt != i:
                        extra.append((CB('E', jt * 128, (jt + 1) * 128), Rsel[:], [cb, Rsel]))
                    if jt == i:
                        extra.append((ident[:], CB('TRIc', 0, 512).rearrange("p (a b) -> p a b", b=128), [ident, cb]))
                    if br == 2 and jt == i - 4:
                        extra.append((ident[:], CB('TRIb', 0, 512).rearrange("p (a b) -> p a b", b=128), [ident, cb]))
                    P.mm(pp[:], kT_[:, jt * 128:(jt + 1) * 128], qrhs, True, False, [kT_, qT], [pp])
                    P.mm(pp[:], CB('La', (-dj) * 128, (-dj + 1) * 128), Ra.rearrange("p (a b) -> p a b", b=128),
                         False, len(extra) == 0, [cb], [pp])
                    for xi, (l_, r_, rd) in enumerate(extra):
                        P.mm(pp[:], l_, r_, False, xi == len(extra) - 1, rd, [pp])
                    return pp

                def emit_pv(jt, pp, vA_=vA_, jlo=jlo):
                    pt_ = Pt[npt[0] % 3]
                    npt[0] += 1
                    P.act(pt_[:], pp[:], AF.Exp, [pp], [pt_])
                    for h in range(4):
                        P.mm(Oh(h)[:, 0:129], pt_[:, h, :], vA_[:, jt, :], jt == jlo, jt == i,
                             [pt_, vA_], [psO[h]])

                pend = None
                for jt in range(jlo, i + 1):
                    pp = emit_qk(jt)
                    if pend is not None:
                        emit_pv(*pend)
                    pend = (jt, pp)
                emit_pv(*pend)
                finish(br, 129)
            P.tt('dve', ybf[:], acc[:].rearrange("p a b -> p (a b)"), sgate[:], ALU.mult, [acc, sgate], [ybf])
            ptb = ptr
            for h in range(4):
                P.tr(ptb[:, h, :], ybf[:, h * 128:(h + 1) * 128], ident[:], [ybf, ident], [ptb])
            yt = yTt[i % 2]
            P.cp('act', yt[:], ptb[:], [ptb], [yt])
            P.dma('sp', out_d.t[2048 + gg * 512:2048 + (gg + 1) * 512, isl].rearrange("(h c) t -> c h t", c=128),
                  yt[:], [yt], (), owner=yt)


EV_OFF = dict(a_q=0, a_f=2048, a_i=4096, a_g=6144, b_q=8192, b_kc=10240, b_vc=10752, b_ks=11264, b_vs=11776,
              b_kw=12288, b_vw=12800, b_gate=13312, b_g=13360)


def emit_even_A(P, C, e_idx, x_src, nw_row, W, out_d):
    ident, identf, cm, eps, cf = C['ident'], C['identf'], C['cm'], C['eps'], C['cf']
    VAL = cf[:, 0:512]
    ADD = cf[:, 512:1024]
    RST = cf[:, 1024:1536]
    with ExitStack() as sa:
        hT = P.sb(sa, "hT", [128, KC, T], BF16)
        hTv = [P.view(hT, "hT%d" % i) for i in range(NT)]
        ss = P.sb(sa, "ss", [128, 4], F32)
        with ExitStack() as s0:
            xt = [P.sb(s0, "xt%d" % i, [128, D], F32) for i in range(2)]
            xn = [P.sb(s0, "xn%d" % i, [128, D], BF16) for i in range(2)]
            nwb = P.sb(s0, "nwb", [128, D], F32)
            ptr = [P.ps(s0, "ptr%d" % i, [128, 4, 128], BF16) for i in range(2)]
            P.dma('sp', nwb[:], nw_row.partition_broadcast(128), (), [nwb], owner=nwb)
            emit_hT(P, s0, x_src, nwb, ident, hT, hTv, xt, xn, ptr, {'eps': eps, 'ss': ss})
            P.end_section()
        with ExitStack() as s1:
            ptb = P.ps(s1, "ptrh", [128, 8, 128], BF16)
            emit_hgrn2(P, s1, e_idx, hT, hTv, ident, cm, RST, eps, (ptb, P.view(ptb)), W['whg'], W['lbl'], W['gain'],
                       out_d, 16)
            P.end_section()
        with ExitStack() as s2:
            ptr = P.ps(s2, "ptrn", [128, 4, 128], BF16)
            emit_nsa(P, s2, hT, hTv, (ident, identf), ptr, VAL, ADD, W['wnq'], W['wnkv'], W['wngt'], W['wnbg'],
                     W['w1k'], W['w1v'], W['w2k'], W['w2v'], W['pekT'], W['pevT'], C['cb_d'], out_d, 4, NT)
            P.end_section()


def emit_odd_A(P, C, x_src, nw_row, W, out_d):
    ident, eps = C['ident'], C['eps']
    NBLK = 10
    NCT = 20
    wx_d, wg_d, wa_d, wi_d, vec_d = W['wx'], W['wg'], W['wa'], W['wi'], W['vec']
    with ExitStack() as st:
        hT = P.sb(st, "hT", [128, KC, T], BF16)
        hTv = [P.view(hT, "hT%d" % i) for i in range(NT)]
        Tb = [P.sb(st, "T%d" % i, [128, T], F32) for i in range(4)]
        Bb = [P.sb(st, "B%d" % i, [128, T], BF16) for i in range(2)]
        xraw = [P.sb(st, "xraw%d" % i, [128, T + 3], F32) for i in range(2)]
        xc = [P.sb(st, "xc%d" % i, [128, T], F32) for i in range(2)]
        mix = [P.sb(st, "mix%d" % i, [128, T], BF16) for i in range(2)]
        wxs = [P.sb(st, "wxs%d" % i, [128, KC, 256], BF16) for i in range(2)]
        wgs = [P.sb(st, "wgs%d" % i, [128, KC, 256], BF16) for i in range(2)]
        was = [P.sb(st, "was%d" % i, [128, 2, 256], BF16) for i in range(2)]
        wis = [P.sb(st, "wis%d" % i, [128, 2, 256], BF16) for i in range(2)]
        vec = P.sb(st, "vec", [128, NCT, 8], F32)
        cl = P.sb(st, "cl", [128, NCT], F32)
        ss = P.sb(st, "ss", [128, 4], F32)
        ptr = [P.ps(st, "ptr%d" % i, [128, 4, 128], BF16) for i in range(2)]
        pj = [P.ps(st, "pj%d" % i, [128, 512], F32) for i in range(2)]
        pg = [P.ps(st, "pg%d" % i, [128, 512], F32) for i in range(2)]

        nwb = xraw[0]
        P.dma('sp', nwb[:, 0:D], nw_row.partition_broadcast(128), (), [nwb], owner=nwb)
        P.dma('sp', vec[:], vec_d.t.ap().rearrange("p (c k) -> p c k", k=8), (), [vec], owner=vec)
        emit_hT(P, st, x_src, nwb, ident, hT, hTv, Tb[0:2], Bb, ptr, {'eps': eps, 'ss': ss})
        P.act(cl[:], vec[:, :, 7], AF.Exp, [vec], [cl], scale=-1.0)
        P.act(cl[:], cl[:], AF.Ln, [cl], [cl], bias=1.0)
        P.ts('dve', cl[:], cl[:], -8.0, None, ALU.mult, None, [cl], [cl])
        for i in range(2):
            P.memset('dve', xraw[i][:, 0:3], 0.0, [xraw[i]])

        def load_w(n):
            P.dma('pool', wxs[n % 2][:], wx_d.t[:, n * 256:(n + 1) * 256].rearrange("(kc p) c -> p kc c", p=128),
                  (), [wxs[n % 2]], owner=wxs[n % 2])
            P.dma('pool', wgs[n % 2][:], wg_d.t[:, n * 256:(n + 1) * 256].rearrange("(kc p) c -> p kc c", p=128),
                  (), [wgs[n % 2]], owner=wgs[n % 2])
            P.dma('pool', was[n % 2][:], wa_d.t[n].rearrange("(dt p) e -> p dt e", p=128), (), [was[n % 2]],
                  owner=was[n % 2])
            P.dma('pool', wis[n % 2][:], wi_d.t[n].rearrange("(dt p) e -> p dt e", p=128), (), [wis[n % 2]],
                  owner=wis[n % 2])

        npj = 0
        load_w(0)
        for n in range(NBLK):
            w_x, w_g, w_a, w_i = wxs[n % 2], wgs[n % 2], was[n % 2], wis[n % 2]
            if n + 1 < NBLK:
                load_w(n + 1)
            for hf in range(2):
                ct = 2 * n + hf
                xr = xraw[hf]
                for tb in range(4):
                    pp = pj[npj % 2]
                    npj += 1
                    for kc in range(KC):
                        P.mm(pp[:], w_x[:, kc, hf * 128:(hf + 1) * 128], hT[:, kc, tb * 512:(tb + 1) * 512],
                             kc == 0, kc == KC - 1, [w_x] + hTv[4 * tb:4 * tb + 4], [pp])
                    P.cp('act', xr[:, 3 + tb * 512:3 + (tb + 1) * 512], pp[:], [pp], [xr])
                c = xc[hf]
                P.ts('dve', c[:], xr[:, 3:3 + T], vec[:, ct, 3:4], vec[:, ct, 4:5], ALU.mult, ALU.add,
                     [xr, vec], [c])
                for j in range(3):
                    P.stt(c[:], xr[:, j:j + T], vec[:, ct, j:j + 1], c[:], ALU.mult, ALU.add, [xr, vec, c], [c])
                P.cp('pool', Bb[hf][:], c[:], [c], [Bb[hf]])
            for eh in range(2):
                ct = 2 * n + eh
                R, I, S, G = Tb
                for tb in range(4):
                    pp = pj[npj % 2]
                    npj += 1
                    for kc in range(KC):
                        P.mm(pp[:], w_g[:, kc, eh * 128:(eh + 1) * 128], hT[:, kc, tb * 512:(tb + 1) * 512],
                             kc == 0, kc == KC - 1, [w_g] + hTv[4 * tb:4 * tb + 4], [pp])
                    P.act(G[:, tb * 512:(tb + 1) * 512], pp[:], AF.Silu, [pp], [G])
                for (dst, wsb, bcol) in ((R, w_a, 5), (I, w_i, 6)):
                    for tb in range(4):
                        pp = pg[npj % 2]
                        npj += 1
                        for dt_ in range(2):
                            P.mm(pp[:], wsb[:, dt_, eh * 128:(eh + 1) * 128],
                                 Bb[dt_][:, tb * 512:(tb + 1) * 512], dt_ == 0, dt_ == 1, [wsb, Bb[dt_]], [pp])
                        P.act(dst[:, tb * 512:(tb + 1) * 512], pp[:], AF.Sigmoid, [pp, vec], [dst],
                              bias=vec[:, ct, bcol:bcol + 1])
                P.act(R[:], R[:], AF.Exp, [R, cl], [R], scale=cl[:, ct:ct + 1])
                P.tt('pool', S[:], R[:], R[:], ALU.mult, [R], [S])
                P.act(S[:], S[:], AF.Sqrt, [S], [S], scale=-1.0, bias=1.0)
                P.tt('dve', I[:], I[:], xc[eh][:], ALU.mult, [I, xc[eh]], [I])
                P.tt('dve', I[:], I[:], S[:], ALU.mult, [I, S], [I])
                P.scan(S[:], R[:], I[:], 0.0, [R, I], [S])
                m = mix[eh]
                P.tt('dve', m[:], S[:], G[:], ALU.mult, [S, G], [m])
                P.dma('sp', out_d.t[ct * 128:(ct + 1) * 128, :], m[:], [m], (), owner=m)
        P.end_section()


def emit_B(P, C, Cdim, mix_d, w_d, x_src, x_dst, final, fn_row=None, out_d=None):
    KCC = Cdim // 128
    NB = 512
    TB = 1024
    eps = C['eps']
    with ExitStack() as st:
        mix = P.sb(st, "bmix", [128, KCC, TB], BF16)
        ws = [P.sb(st, "bws%d" % i, [128, KCC, NB], BF16) for i in range(2)]
        xb = [P.sb(st, "bx%d" % i, [128, NB], F32) for i in range(4)]
        pp = [P.ps(st, "bpp%d" % i, [128, 512], F32) for i in range(4)]
        k = 0
        nw = 0
        for th in range(2):
            P.dma('sp', mix[:], mix_d.t[:, th * TB:(th + 1) * TB].rearrange("(kc p) t -> p kc t", p=128),
                  (), [mix], owner=mix)
            for nb in range(D // NB):
                w = ws[nw % 2]
                nw += 1
                P.dma('pool', w[:], w_d.t[:, nb * NB:(nb + 1) * NB].rearrange("(kc p) c -> p kc c", p=128),
                      (), [w], owner=w)
                for tt in range(TB // 128):
                    rows = slice(th * TB + tt * 128, th * TB + (tt + 1) * 128)
                    cols = slice(nb * NB, (nb + 1) * NB)
                    p_ = pp[k % 4]
                    xv = xb[k % 4]
                    k += 1
                    P.dma('sp', xv[:], x_src.t[rows, cols], (), [xv], owner=xv)
                    for kc in range(KCC):
                        P.mm(p_[:, 0:NB], mix[:, kc, tt * 128:(tt + 1) * 128], w[:, kc, :], kc == 0, kc == KCC - 1,
                             [mix, w], [p_])
                    P.tt('dve', xv[:], xv[:], p_[:, 0:NB], ALU.add, [xv, p_], [xv])
                    P.dma('act', x_dst.t[rows, cols], xv[:], [xv], (), owner=xv)
        P.end_section()
    if final:
        with ExitStack() as st:
            xt = [P.sb(st, "fx%d" % i, [128, D], F32) for i in range(2)]
            junk = P.sb(st, "fjunk", [128, D], BF16)
            fnb = P.sb(st, "fnb", [128, D], F32)
            ss = P.sb(st, "fss", [128, 4], F32)
            P.dma('sp', fnb[:], fn_row.partition_broadcast(128), (), [fnb], owner=fnb)
            for tt in range(NT):
                xv = xt[tt % 2]
                s0 = ss[:, 2 * (tt % 2):2 * (tt % 2) + 1]
                s1 = ss[:, 2 * (tt % 2) + 1:2 * (tt % 2) + 2]
                P.dma('sp', xv[:], x_dst.t[tt * 128:(tt + 1) * 128, :], (), [xv], owner=xv)
                P.act(junk[:], xv[:], AF.Square, [xv], [junk, ss], accum_out=s0)
                P.act(s1, s0, AF.Sqrt, [ss, eps], [ss], scale=1.0 / D, bias=eps[:])
                P.op('dve', lambda e, s1=s1: e.reciprocal(out=s1, in_=s1), [ss], [ss])
                P.stt(xv[:], xv[:], s1, fnb[:], ALU.mult, ALU.mult, [xv, ss, fnb], [xv])
                P.dma('sp', out_d.t[tt * 128:(tt + 1) * 128, :], xv[:], [xv], (), owner=xv)
            P.end_section()


def build_fused(nlayers=4):
    nc = bass.Bass("TRN2", target_bir_lowering=False)
    with ExitStack() as st:
        P = Prog(nc, st)
        x_d = P.dram("x", [T, D], F32, "ExternalInput")
        nw_d = P.dram("nw", [4, D], F32, "ExternalInput")
        fnw_d = P.dram("fnw", [D], F32, "ExternalInput")
        cb_d = P.dram("cb", [128, CB_W], F32, "ExternalInput")
        cf_d = P.dram("cf", [128, 1536], F32, "ExternalInput")
        lbl_d = P.dram("lbl", [128, 32], F32, "ExternalInput")
        WE, WO = [], []
        for e in range(2):
            W = {'lbl': lbl_d}
            for nm, shp in (('whg', [D, 16 * 512]), ('gain', [2048]), ('wnq', [D, 2048]), ('wnkv', [D, 3072]),
                            ('wngt', [D, 48]), ('wnbg', [D, 2048]), ('w1k', [4096, 128]), ('w1v', [4096, 128]),
                            ('w2k', [128, 128]), ('w2v', [128, 128]), ('pekT', [128, 32]), ('pevT', [128, 32]),
                            ('wout', [4096, D])):
                W[nm] = P.dram("e%d_%s" % (e, nm), shp, F32, "ExternalInput")
            WE.append(W)
        for o in range(2):
            W = {}
            for nm, shp in (('wx', [D, 2560]), ('wg', [D, 2560]), ('wa', [10, 256, 256]), ('wi', [10, 256, 256]),
                            ('vec', [128, 160]), ('wout', [2560, D])):
                W[nm] = P.dram("o%d_%s" % (o, nm), shp, F32, "ExternalInput")
            WO.append(W)
        xs_d = P.dram("xs_scratch", [T, D], F32, "Internal")
        mixE_d = P.dram("mixE_scratch", [4096, T], BF16, "Internal")
        mixO_d = P.dram("mixO_scratch", [2560, T], BF16, "Internal")
        out_d = P.dram("out", [T, D], F32, "ExternalOutput")

        ident = make_ident(P, st)
        idf2 = P.sb(st, "idf2", [128, 128], F32)
        cm = P.sb(st, "cm", [128, 128], F32)
        identf = P.sb(st, "identf", [128, 128], F32)
        P.op('pool', lambda e: e.iota(idf2[:], pattern=[[1, 128]], base=0, channel_multiplier=-1,
                                      allow_small_or_imprecise_dtypes=True), (), [idf2])
        P.ts('dve', cm[:], idf2[:], 0.0, None, ALU.is_ge, None, [idf2], [cm])
        P.ts('dve', identf[:], idf2[:], 0.0, None, ALU.is_equal, None, [idf2], [identf])
        eps = P.sb(st, "eps", [128, 1], F32)
        P.memset('dve', eps[:], EPS, [eps])
        cf = P.sb(st, "cf", [128, 1536], F32)
        P.dma('sp', cf[:], cf_d.t.ap(), (), [cf], owner=cf)
        C = dict(ident=ident, identf=identf, cm=cm, eps=eps, cf=cf, cb_d=cb_d)
        P.end_section()

        for layer in range(nlayers):
            x_src = x_d if layer == 0 else xs_d
            nw_row = nw_d.t[layer]
            last = layer == nlayers - 1
            if layer % 2 == 0:
                W = WE[layer // 2]
                emit_even_A(P, C, layer // 2, x_src, nw_row, W, mixE_d)
                emit_B(P, C, 4096, mixE_d, W['wout'], x_src, xs_d, last, fnw_d.t.ap(), out_d)
            else:
                W = WO[layer // 2]
                emit_odd_A(P, C, x_src, nw_row, W, mixO_d)
                emit_B(P, C, 2560, mixO_d, W['wout'], x_src, xs_d, last, fnw_d.t.ap(), out_d)
        P.barrier()
        P.emit()
    return nc


def pack_inputs(inp):
    cbt, cft = even_const_tables()
    shared = {"nw": c_(inp['norm_w']), "fnw": c_(inp['final_norm_w']), "cb": cbt, "cf": cft}
    lg = inp['hgrn_lb_logits']
    shared["lbl"] = c_(lg.reshape(2, 16, 128).transpose(2, 1, 0).reshape(128, 32))
    for e in range(2):
        w_in = inp['even_w_in'][e]
        cols = []
        for gh in range(16):
            for k in ('a_q', 'a_f', 'a_i', 'a_g'):
                cols.append(w_in[:, EV_OFF[k] + gh * 128:EV_OFF[k] + (gh + 1) * 128])
        shared["e%d_whg" % e] = np.concatenate(cols, axis=1)
        kv = []
        for g in range(4):
            for k in ('b_kc', 'b_vc', 'b_ks', 'b_kw', 'b_vs', 'b_vw'):
                kv.append(w_in[:, EV_OFF[k] + g * 128:EV_OFF[k] + (g + 1) * 128])
        shared["e%d_wnkv" % e] = np.concatenate(kv, axis=1)
        shared["e%d_wnq" % e] = c_(w_in[:, EV_OFF['b_q']:EV_OFF['b_q'] + 2048])
        shared["e%d_wngt" % e] = c_(w_in[:, EV_OFF['b_gate']:EV_OFF['b_gate'] + 48])
        shared["e%d_wnbg" % e] = c_(w_in[:, EV_OFF['b_g']:EV_OFF['b_g'] + 2048])
        shared["e%d_gain" % e] = c_(inp['hgrn_norm_w'][e])
        shared["e%d_w1k" % e] = c_(inp['cmp_w1_k'][e])
        shared["e%d_w1v" % e] = c_(inp['cmp_w1_v'][e])
        shared["e%d_w2k" % e] = c_(inp['cmp_w2_k'][e])
        shared["e%d_w2v" % e] = c_(inp['cmp_w2_v'][e])
        shared["e%d_pekT" % e] = c_(inp['cmp_pe_k'][e].T)
        shared["e%d_pevT" % e] = c_(inp['cmp_pe_v'][e].T)
        shared["e%d_wout" % e] = c_(inp['even_w_out'][e])
    for o in range(2):
        w_in = inp['odd_w_in'][o]
        shared["o%d_wx" % o] = c_(w_in[:, 0:D_RNN])
        shared["o%d_wg" % o] = c_(w_in[:, D_RNN:2 * D_RNN])
        shared["o%d_wa" % o] = c_(inp['rg_w_a'][o])
        shared["o%d_wi" % o] = c_(inp['rg_w_i'][o])
        cw = inp['rg_conv_w'][o]
        vec = np.stack([cw[0], cw[1], cw[2], cw[3], inp['rg_conv_b'][o], inp['rg_b_a'][o], inp['rg_b_i'][o],
                        inp['rg_lambda'][o]], axis=-1)
        shared["o%d_vec" % o] = c_(vec.reshape(20, 128, 8).transpose(1, 0, 2).reshape(128, 160).astype(np.float32))
        shared["o%d_wout" % o] = c_(inp['odd_w_out'][o])
    return shared


def kernel(**inputs):
    inp = {k: np.asarray(v) for k, v in inputs.items()}
    shared = pack_inputs(inp)
    x = np.ascontiguousarray(inp['x'], dtype=np.float32)
    nc = _get("fused", build_fused)
    in_maps = []
    for c in range(NCORES):
        m = dict(shared)
        m["x"] = c_(x[c // 2])
        in_maps.append(m)
    res = run_bass_kernel_spmd(nc, in_maps, core_ids=list(range(NCORES)))
    out = np.stack([res.results[2 * b]["out"] for b in range(4)], axis=0)
    return out.astype(np.float32)
```
